# Optimizing a Trainium2 kernel written in Bass

```python
import math
import jax, jax.numpy as jnp
from jax import lax
import numpy as np

D_MODEL = 1024
BATCH = 8
SEQ = 4096
DEPTH = 4

GRID_W = 64
NA_HEADS = 8
NA_HEAD_DIM = 64
NA_WIDTH = NA_HEADS * NA_HEAD_DIM
NA_WIN_ROWS = 8
NA_WIN_COLS = 16
NA_COL_BLOCK = NA_WIN_COLS
NA_KEY_COL_BLOCK = 2 * NA_WIN_COLS
DIFF_HEADS = 4
DIFF_HEAD_DIM = 64
DIFF_QK_WIDTH = DIFF_HEADS * 2 * DIFF_HEAD_DIM
DIFF_V_WIDTH = DIFF_HEADS * 2 * DIFF_HEAD_DIM
MLA_HEADS = 8
MLA_NOPE_DIM = 64
MLA_ROPE_DIM = 32
MLA_V_DIM = 64
MLA_Q_RANK = 384
MLA_KV_RANK = 256
MLA_WIDTH = MLA_HEADS * MLA_V_DIM
N_BRANCH = 3
BRANCH_WIDTH = 512
D_IN = 3 * NA_WIDTH + 2 * DIFF_QK_WIDTH + DIFF_V_WIDTH + MLA_Q_RANK + MLA_KV_RANK + MLA_ROPE_DIM + N_BRANCH * D_MODEL
D_FF = -(-8 * D_MODEL // (3 * 256)) * 256
ROPE_THETA = 500000.0
DIFF_ROT_DIM = DIFF_HEAD_DIM // 4
Q_BLOCK = 128
DN_ALPHA = (2 * DEPTH) ** 0.25
DN_BETA = (8 * DEPTH) ** -0.25
LN_EPS = 1e-5
RMS_EPS = 1e-6

kernel_name = "hybrid_na_diff_mla_gated_deepnorm_encoder"


def layer_norm(x, g, b):
    xf = x.astype(jnp.float32)
    mu = jnp.mean(xf, -1, keepdims=True)
    var = jnp.mean(jnp.square(xf - mu), -1, keepdims=True)
    return ((xf - mu) * lax.rsqrt(var + LN_EPS) * g.astype(jnp.float32) + b.astype(jnp.float32)).astype(x.dtype)


def rms_norm(x, g):
    xf = x.astype(jnp.float32)
    return (xf * lax.rsqrt(jnp.mean(xf * xf, -1, keepdims=True) + RMS_EPS) * g.astype(jnp.float32)).astype(x.dtype)


def rope(x, rot_dim):
    s = x.shape[1]
    half = rot_dim // 2
    inv_freq = jnp.exp(-math.log(ROPE_THETA) * jnp.arange(half, dtype=jnp.float32) / half)
    ang = jnp.arange(s, dtype=jnp.float32)[:, None] * inv_freq[None, :]
    cos = jnp.cos(ang)[None, :, None, :]
    sin = jnp.sin(ang)[None, :, None, :]
    xr = x[..., :rot_dim].astype(jnp.float32)
    x1, x2 = xr[..., :half], xr[..., half:]
    rot = jnp.concatenate([x1 * cos - x2 * sin, x2 * cos + x1 * sin], -1).astype(x.dtype)
    return jnp.concatenate([rot, x[..., rot_dim:]], -1)


def blocked_attention(q, k, v, scale):
    b, s, h, dq = q.shape
    nb = s // Q_BLOCK
    qb = q.reshape(b, nb, Q_BLOCK, h, dq).transpose(1, 0, 2, 3, 4)

    def one_block(qi):
        sc = jnp.einsum('bqhd,bkhd->bhqk', qi, k, preferred_element_type=jnp.float32) * scale
        p = jax.nn.softmax(sc, axis=-1).astype(v.dtype)
        return jnp.einsum('bhqk,bkhd->bqhd', p, v)

    out = lax.map(one_block, qb)
    return out.transpose(1, 0, 2, 3, 4).reshape(b, s, h, v.shape[-1])


def neighbourhood_attention(q, k, v, rpb):
    b, s, h, d = q.shape
    rows = s // GRID_W
    kr = min(NA_WIN_ROWS, rows)
    kc = NA_WIN_COLS
    ncb = GRID_W // NA_COL_BLOCK
    qg = q.reshape(b, rows, GRID_W, h, d)
    kg = k.reshape(b, rows, GRID_W, h, d)
    vg = v.reshape(b, rows, GRID_W, h, d)
    qcol = jnp.arange(GRID_W).reshape(ncb, NA_COL_BLOCK)
    kstart = jnp.clip(jnp.arange(ncb) * NA_COL_BLOCK - kc // 2, 0, GRID_W - NA_KEY_COL_BLOCK)
    kcol = kstart[:, None] + jnp.arange(NA_KEY_COL_BLOCK)[None, :]
    wstart = jnp.clip(qcol - kc // 2, 0, GRID_W - kc)
    col_valid = (kcol[:, None, :] >= wstart[:, :, None]) & (kcol[:, None, :] < wstart[:, :, None] + kc)
    dcol_idx = jnp.clip(kcol[:, None, :] - qcol[:, :, None], -(kc - 1), kc - 1) + (kc - 1)
    row_center = NA_WIN_ROWS - 1
    scale = d ** -0.5

    def one_row(r):
        rs = jnp.clip(r - kr // 2, 0, rows - kr)
        qr = lax.dynamic_index_in_dim(qg, r, axis=1, keepdims=False)
        k_rows = lax.dynamic_slice_in_dim(kg, rs, kr, axis=1)
        v_rows = lax.dynamic_slice_in_dim(vg, rs, kr, axis=1)
        kb = jnp.take(k_rows, kcol, axis=2)
        vb = jnp.take(v_rows, kcol, axis=2)
        qb = qr.reshape(b, ncb, NA_COL_BLOCK, h, d)
        sc = jnp.einsum('bnqhd,banchd->bhnqac', qb, kb, preferred_element_type=jnp.float32) * scale
        drow = rs + jnp.arange(kr) - r + row_center
        bias = rpb[:, drow][:, :, dcol_idx].transpose(0, 2, 3, 1, 4)
        sc = sc + bias[None].astype(jnp.float32)
        sc = jnp.where(col_valid[None, None, :, :, None, :], sc, -jnp.inf)
        p = jax.nn.softmax(sc, axis=(-2, -1)).astype(v.dtype)
        o = jnp.einsum('bhnqac,banchd->bnqhd', p, vb)
        return o.reshape(b, GRID_W, h, d)

    out = lax.map(one_row, jnp.arange(rows))
    return out.transpose(1, 0, 2, 3, 4).reshape(b, s, h * d)


def diff_attention(q, k, v, lam_vecs, subln_g, lam_init):
    b, s, _ = q.shape
    q = q.reshape(b, s, DIFF_HEADS, 2, DIFF_HEAD_DIM)
    k = k.reshape(b, s, DIFF_HEADS, 2, DIFF_HEAD_DIM)
    v = v.reshape(b, s, DIFF_HEADS, 2 * DIFF_HEAD_DIM)
    q1, q2 = rope(q[:, :, :, 0], DIFF_ROT_DIM), rope(q[:, :, :, 1], DIFF_ROT_DIM)
    k1, k2 = rope(k[:, :, :, 0], DIFF_ROT_DIM), rope(k[:, :, :, 1], DIFF_ROT_DIM)
    lv = lam_vecs.astype(jnp.float32)
    lam = jnp.exp(jnp.sum(lv[0] * lv[1])) - jnp.exp(jnp.sum(lv[2] * lv[3])) + lam_init
    scale = DIFF_HEAD_DIM ** -0.5
    a1 = blocked_attention(q1, k1, v, scale)
    a2 = blocked_attention(q2, k2, v, scale)
    o = a1 - lam.astype(a1.dtype) * a2
    o = rms_norm(o, subln_g) * (1.0 - lam_init)
    return o.reshape(b, s, DIFF_V_WIDTH)


def latent_attention(c_q, c_kv, k_rope, w_qb, w_kvb, q_norm_g, kv_norm_g):
    b, s, _ = c_q.shape
    q = jnp.einsum('bsr,re->bse', rms_norm(c_q, q_norm_g), w_qb).reshape(b, s, MLA_HEADS, MLA_NOPE_DIM + MLA_ROPE_DIM)
    q = jnp.concatenate([q[..., :MLA_NOPE_DIM], rope(q[..., MLA_NOPE_DIM:], MLA_ROPE_DIM)], -1)
    kv = jnp.einsum('bsr,re->bse', rms_norm(c_kv, kv_norm_g), w_kvb).reshape(b, s, MLA_HEADS, MLA_NOPE_DIM + MLA_V_DIM)
    k_nope, v = kv[..., :MLA_NOPE_DIM], kv[..., MLA_NOPE_DIM:]
    k_r = rope(k_rope[:, :, None, :], MLA_ROPE_DIM)
    k = jnp.concatenate([k_nope, jnp.broadcast_to(k_r, (b, s, MLA_HEADS, MLA_ROPE_DIM))], -1)
    o = blocked_attention(q, k, v, (MLA_NOPE_DIM + MLA_ROPE_DIM) ** -0.5)
    return o.reshape(b, s, MLA_WIDTH)


def setup_inputs(seed: int = 0) -> dict:
    key = jax.random.key(seed)
    ks = jax.random.split(key, 20)

    def nrm(k, shape, s):
        return jax.random.normal(k, shape, jnp.float32) * s

    return {
        "x": nrm(ks[0], (BATCH, SEQ, D_MODEL), 1.0),
        "ln_in_g": 1.0 + nrm(ks[1], (D_MODEL,), 0.02),
        "ln_in_b": nrm(ks[2], (D_MODEL,), 0.02),
        "w_in": nrm(ks[3], (DEPTH, D_MODEL, D_IN), D_MODEL ** -0.5),
        "b_gate": nrm(ks[4], (DEPTH, N_BRANCH * D_MODEL), 0.02),
        "na_rpb": nrm(ks[5], (DEPTH, NA_HEADS, 2 * NA_WIN_ROWS - 1, 2 * NA_WIN_COLS - 1), 0.02),
        "diff_lambda": nrm(ks[6], (DEPTH, 4, DIFF_HEAD_DIM), 0.1),
        "diff_subln_g": 1.0 + nrm(ks[7], (DEPTH, 2 * DIFF_HEAD_DIM), 0.02),
        "mla_q_norm_g": 1.0 + nrm(ks[8], (DEPTH, MLA_Q_RANK), 0.02),
        "mla_kv_norm_g": 1.0 + nrm(ks[9], (DEPTH, MLA_KV_RANK), 0.02),
        "w_mla_qb": nrm(ks[10], (DEPTH, MLA_Q_RANK, MLA_HEADS * (MLA_NOPE_DIM + MLA_ROPE_DIM)), MLA_Q_RANK ** -0.5),
        "w_mla_kvb": nrm(ks[11], (DEPTH, MLA_KV_RANK, MLA_HEADS * (MLA_NOPE_DIM + MLA_V_DIM)), MLA_KV_RANK ** -0.5),
        "w_branch": nrm(ks[12], (DEPTH, N_BRANCH, BRANCH_WIDTH, D_MODEL), BRANCH_WIDTH ** -0.5 * DN_BETA),
        "w_out": nrm(ks[13], (DEPTH, D_MODEL, D_MODEL), D_MODEL ** -0.5 * DN_BETA),
        "ln1_g": 1.0 + nrm(ks[14], (DEPTH, D_MODEL), 0.02),
        "ln1_b": nrm(ks[15], (DEPTH, D_MODEL), 0.02),
        "w_ffn_in": nrm(ks[16], (DEPTH, D_MODEL, 2 * D_FF), D_MODEL ** -0.5),
        "w_ffn_out": nrm(ks[17], (DEPTH, D_FF, D_MODEL), D_FF ** -0.5 * DN_BETA),
        "ln2_g": 1.0 + nrm(ks[18], (DEPTH, D_MODEL), 0.02),
        "ln2_b": nrm(ks[19], (DEPTH, D_MODEL), 0.02),
    }


def reference(x, ln_in_g, ln_in_b, w_in, b_gate, na_rpb, diff_lambda, diff_subln_g,
              mla_q_norm_g, mla_kv_norm_g, w_mla_qb, w_mla_kvb, w_branch, w_out,
              ln1_g, ln1_b, w_ffn_in, w_ffn_out, ln2_g, ln2_b):
    b, s, _ = x.shape
    sizes = (NA_WIDTH, NA_WIDTH, NA_WIDTH, DIFF_QK_WIDTH, DIFF_QK_WIDTH, DIFF_V_WIDTH,
             MLA_Q_RANK, MLA_KV_RANK, MLA_ROPE_DIM, N_BRANCH * D_MODEL)
    split_points = np.cumsum(sizes)[:-1].tolist()

    x = layer_norm(x, ln_in_g, ln_in_b)
    for l in range(DEPTH):
        lam_init = 0.8 - 0.6 * math.exp(-0.3 * l)
        proj = jnp.einsum('bsd,de->bse', x, w_in[l])
        (na_q, na_k, na_v, df_q, df_k, df_v, m_cq, m_ckv, m_kr, gate_pre) = jnp.split(proj, split_points, axis=-1)

        na_out = neighbourhood_attention(
            na_q.reshape(b, s, NA_HEADS, NA_HEAD_DIM), na_k.reshape(b, s, NA_HEADS, NA_HEAD_DIM),
            na_v.reshape(b, s, NA_HEADS, NA_HEAD_DIM), na_rpb[l])
        df_out = diff_attention(df_q, df_k, df_v, diff_lambda[l], diff_subln_g[l], lam_init)
        mla_out = latent_attention(m_cq, m_ckv, m_kr, w_mla_qb[l], w_mla_kvb[l],
                                   mla_q_norm_g[l], mla_kv_norm_g[l])

        gates = jax.nn.sigmoid((gate_pre + b_gate[l]).astype(jnp.float32)).astype(x.dtype)
        gates = gates.reshape(b, s, N_BRANCH, D_MODEL)
        merged = (gates[:, :, 0] * jnp.einsum('bse,ed->bsd', na_out, w_branch[l, 0])
                  + gates[:, :, 1] * jnp.einsum('bse,ed->bsd', df_out, w_branch[l, 1])
                  + gates[:, :, 2] * jnp.einsum('bse,ed->bsd', mla_out, w_branch[l, 2]))
        mix = jnp.einsum('bsd,de->bse', merged, w_out[l])
        x = layer_norm(DN_ALPHA * x + mix, ln1_g[l], ln1_b[l])

        gu = jnp.einsum('bsd,df->bsf', x, w_ffn_in[l])
        hidden = jax.nn.silu(gu[..., :D_FF]) * gu[..., D_FF:]
        ffn = jnp.einsum('bsf,fd->bsd', hidden, w_ffn_out[l])
        x = layer_norm(DN_ALPHA * x + ffn, ln2_g[l], ln2_b[l])
    return x
```

```python
import math
import os
import numpy as np
import ml_dtypes
import concourse.bass as bass
import concourse.mybir as mybir
from concourse.bass_utils import run_bass_kernel_spmd

F32 = mybir.dt.float32
BF16 = mybir.dt.bfloat16
AF = mybir.ActivationFunctionType
ALU = mybir.AluOpType
AX = mybir.AxisListType

T = 4096
D = 1024
KC = 8
NL = 4
DIN = 6816
DFF = 2816
FC = 22
ALPHA = (2 * NL) ** 0.25
LN_EPS = 1e-5
RMS_EPS = 1e-6
ROPE_THETA = 500000.0
NEG = -30000.0
DBGB = os.environ.get('DBG_B', 'qkvr')


class Buf:
    __slots__ = ("name", "w", "r", "lsem", "ssem", "last_s")

    def __init__(self, name):
        self.name = name
        self.w = None
        self.r = {}
        self.lsem = None
        self.ssem = None
        self.last_s = None


class Prog:
    CE = ("pe", "act", "dve", "pool")

    def __init__(self, nc, sems):
        self.nc = nc
        self.sems = sems
        self.q = {e: [] for e in ("pe", "act", "dve", "pool", "sp")}
        self.esi = {e: i for i, e in enumerate(self.CE)}
        self.bar_si = 4
        self.free = list(range(5, len(sems)))
        self.ecnt = {e: 0 for e in self.CE}
        self.barcnt = 0
        self.semval = {i: 0 for i in range(len(sems))}
        self.waited = {e: {} for e in self.q}
        self.phase_dma = {}
        self.sem_bufs = []
        self.ninst = 0

    def wait(self, eng, ev):
        if ev is None:
            return
        si, val = ev
        if self.waited[eng].get(si, 0) >= val:
            return
        self.waited[eng][si] = val
        sem = self.sems[si]
        self.q[eng].append(("wait_ge", (sem, val), {}))

    def _deps(self, eng, reads, writes):
        deps = {}

        def add(ev):
            if ev is None:
                return
            si, val = ev
            if deps.get(si, 0) < val:
                deps[si] = val

        for b in reads:
            add(b.w)
        for b in writes:
            add(b.w)
            for si, val in b.r.items():
                add((si, val))
        for si, val in deps.items():
            if eng == "pe" and si == self.esi["pe"]:
                continue
            self.wait(eng, (si, val))

    def op(self, eng, fns, reads=(), writes=()):
        self._deps(eng, reads, writes)
        si = self.esi[eng]
        self.ecnt[eng] += 1
        val = self.ecnt[eng]
        ev = (si, val)
        if isinstance(fns, tuple):
            fns = [fns]
        for f in fns[:-1]:
            self.q[eng].append(f)
        last = fns[-1]
        sem = self.sems[si]
        self.q[eng].append((last[0], last[1], last[2], sem, 1))
        self.ninst += len(fns)
        for b in reads:
            if b.r.get(si, 0) < val:
                b.r[si] = val
        for b in writes:
            b.w = ev
            b.r = {}
        return ev

    def _bufsem(self, b, kind):
        cur = b.lsem if kind == "l" else b.ssem
        if cur is None:
            cur = self.free.pop()
            if kind == "l":
                b.lsem = cur
            else:
                b.ssem = cur
            self.sem_bufs.append(b)
        return cur

    def dma(self, pairs, reads=(), writes=(), owner=None, kind="l", eng="sp"):
        self._deps(eng, reads, writes)
        if kind == "s" and owner.last_s is not None:
            self.wait(eng, owner.last_s)
        si = self._bufsem(owner, kind)
        self.semval[si] += 16 * len(pairs)
        val = self.semval[si]
        assert val < 60000, "semaphore value too large"
        ev = (si, val)
        sem = self.sems[si]
        for (o, i) in pairs:
            self.q[eng].append(("dma_start", (), dict(out=o, in_=i), sem, 16))
        self.ninst += len(pairs)
        for b in reads:
            if b.r.get(si, 0) < val:
                b.r[si] = val
        for b in writes:
            b.w = ev
            b.r = {}
        if kind == "s":
            owner.last_s = ev
        self.phase_dma[si] = val
        return ev

    def barrier(self):
        for e in self.CE:
            if self.ecnt[e] > 0:
                self.wait("sp", (self.esi[e], self.ecnt[e]))
        for si, val in self.phase_dma.items():
            self.wait("sp", (si, val))
        self.barcnt += 1
        n = self.barcnt
        sem = self.sems[self.bar_si]
        self.q["sp"].append(("sem_inc", (sem, 1), {}))
        for e in self.CE:
            self.wait(e, (self.bar_si, n))
        for e in self.q:
            for x in self.CE:
                self.waited[e][self.esi[x]] = self.ecnt[x]
            for si, val in self.semval.items():
                if si > self.bar_si:
                    self.waited[e][si] = val
        self.phase_dma = {}
        for b in self.sem_bufs:
            if b.lsem is not None:
                self.free.append(b.lsem)
                b.lsem = None
            if b.ssem is not None:
                self.free.append(b.ssem)
                b.ssem = None
        self.sem_bufs = []

    def emit(self, block):
        q = self.q

        def run(e, lst):
            for it in lst:
                ins = getattr(e, it[0])(*it[1], **it[2])
                if len(it) == 5:
                    ins.then_inc(it[3], it[4])

        @block.tensor
        def _(e):
            run(e, q["pe"])

        @block.scalar
        def _(e):
            run(e, q["act"])

        @block.vector
        def _(e):
            run(e, q["dve"])

        @block.gpsimd
        def _(e):
            run(e, q["pool"])

        @block.sync
        def _(e):
            run(e, q["sp"])


def I(name, *a, **k):
    return (name, a, k)


class Carver:
    def __init__(self, t, n):
        self.t = t
        self.n = n
        self.off = 0

    def reset(self):
        self.off = 0

    def take(self, n):
        assert self.off + n <= self.n, ("sbuf pool overflow", self.off, n, self.n)
        a = self.t[:, self.off:self.off + n]
        self.off += n
        return a


def build(n_layers=NL, debug=(), stop_after=None):
    nc = bass.Bass("TRN2", target_bir_lowering=False)

    def din(name, shape, dt=F32):
        return nc.dram_tensor(name, list(shape), dt, kind="ExternalInput").ap()

    def scr(name, shape, dt):
        kind = "ExternalOutput" if name in debug else "Internal"
        return nc.dram_tensor(name, list(shape), dt, kind=kind).ap()

    x_in = din("x", [T, D])
    ln_in_g = din("ln_in_g", [1, D])
    ln_in_b = din("ln_in_b", [1, D])
    w_in = din("w_in", [NL, D, DIN])
    bgate = din("bgate", [NL, 128, 24])
    nabias = din("nabias", [NL, 8, 960, 64])
    namask = din("namask", [128, 14 * 64])
    dlam = din("dlam", [NL, 1, 256])
    subg = din("subg", [NL, 128, 1])
    gq_in = din("gq", [NL, 128, 3])
    gkv_in = din("gkv", [NL, 128, 2])
    w_qb = din("w_qb", [NL, 384, 768])
    w_kvb = din("w_kvb", [NL, 256, 1024])
    w_br = din("w_br", [NL, 3, 512, D])
    w_out = din("w_out", [NL, D, D])
    ln1_g = din("ln1_g", [NL, D])
    ln1_b = din("ln1_b", [NL, D])
    w_fi = din("w_fi", [NL, D, 2 * DFF])
    w_fo = din("w_fo", [NL, DFF, D])
    ln2_g = din("ln2_g", [NL, D])
    ln2_b = din("ln2_b", [NL, D])
    c_ident = din("c_ident", [128, 128], BF16)
    c_onesb = din("c_onesb", [128, 128], BF16)
    c_onesf = din("c_onesf", [128, 128])
    c_rd = din("c_rd", [128, 128], BF16)
    c_rm = din("c_rm", [128, 128], BF16)
    c_rk = din("c_rk", [32, 32], BF16)
    c_cosd = din("c_cosd", [128, T])
    c_sind = din("c_sind", [128, T])
    c_cosm = din("c_cosm", [128, T])
    c_sinm = din("c_sinm", [128, T])
    c_cosk = din("c_cosk", [32, T])
    c_sink = din("c_sink", [32, T])
    out = nc.dram_tensor("out", [T, D], F32, kind="ExternalOutput").ap()

    XTOK = scr("XTOK", [T, D], F32)
    NQ = scr("NQ", [4, 128, T], BF16)
    NK = scr("NK", [4, 128, T], BF16)
    NV = scr("NV", [T, 512], BF16)
    DQ = scr("DQ", [4, 128, T], BF16)
    DK = scr("DK", [4, 128, T], BF16)
    DV = scr("DV", [T, 512], BF16)
    MC = scr("MC", [6, 128, T], F32)
    GT = scr("GT", [24, 128, T], BF16)
    MQ = scr("MQ", [8, 128, T], BF16)
    MK = scr("MK", [8, 128, T], BF16)
    MV = scr("MV", [T, 512], BF16)
    AO = scr("AO", [3, 4, 128, T], BF16)
    HT = scr("HT", [FC, 128, T], BF16)

    NPB = 36864
    NPF = 10240
    xT = nc.alloc_sbuf_tensor("xT", [128, KC, T], BF16)
    PBt = nc.alloc_sbuf_tensor("PB", [128, NPB], BF16)
    PFt = nc.alloc_sbuf_tensor("PF", [128, NPF], F32)
    ident = nc.alloc_sbuf_tensor("ident", [128, 128], BF16)
    onesb = nc.alloc_sbuf_tensor("onesb", [128, 128], BF16)
    onesf = nc.alloc_sbuf_tensor("onesf", [128, 128], F32)
    rd = nc.alloc_sbuf_tensor("rd", [128, 128], BF16)
    rm = nc.alloc_sbuf_tensor("rm", [128, 128], BF16)
    rk = nc.alloc_sbuf_tensor("rk", [32, 32], BF16)
    bg_t = nc.alloc_sbuf_tensor("bg_t", [128, NL * 24], F32)
    gq_t = nc.alloc_sbuf_tensor("gq_t", [128, NL * 3], F32)
    gkv_t = nc.alloc_sbuf_tensor("gkv_t", [128, NL * 2], F32)
    gsc_t = nc.alloc_sbuf_tensor("gsc_t", [128, NL], F32)
    nlam_t = nc.alloc_sbuf_tensor("nlam_t", [128, NL], F32)
    lamw = nc.alloc_sbuf_tensor("lamw", [128, NL * 256], F32)
    lamp = nc.alloc_sbuf_tensor("lamp", [128, NL * 128], F32)
    lams = nc.alloc_sbuf_tensor("lams", [128, NL * 2], F32)
    lame = nc.alloc_sbuf_tensor("lame", [128, NL * 2], F32)
    lnst = nc.alloc_sbuf_tensor("lnst", [128, 4 * 16], F32)
    epsln = nc.alloc_sbuf_tensor("epsln", [128, 1], F32)
    epsrms = nc.alloc_sbuf_tensor("epsrms", [128, 1], F32)
    PSt = [nc.alloc_psum_tensor("ps%d" % i, [128, 1024], F32) for i in range(4)]

    def bank(i):
        return PSt[i // 2][:, (i % 2) * 512:(i % 2) * 512 + 512]

    PB = Carver(PBt, NPB)
    PF = Carver(PFt, NPF)

    import contextlib
    with contextlib.ExitStack() as es:
        sems = [es.enter_context(nc.semaphore("s%d" % i)) for i in range(100)]
        block = es.enter_context(nc.Block())
        P = Prog(nc, sems)

        def wload(src, dst_ap, dst_buf, stage):
            a, n = src.shape[1], src.shape[2]
            per = max(1, 2048 // n)
            i = 0
            k = wload.k
            while i < a:
                j = min(a, i + per)
                sap, sbuf = stage[k % len(stage)]
                k += 1
                sv = sap[:, 0:(j - i) * n].rearrange("p (a n) -> p a n", n=n)
                P.dma([(sv, src[:, i:j, :])], writes=[sbuf], owner=sbuf)
                P.op("pool", I("tensor_copy", dst_ap[:, i:j, :], sv), reads=[sbuf], writes=[dst_buf])
                i = j
            wload.k = k

        wload.k = 0

        def rope_epi(ps_ap, ps_buf, rows, cos_ap, sin_ap, cs_buf, rmat, ps2_ap, ps2_buf, xb, t1, t2, stg_ap, stg_buf):
            xb_ap, xb_buf = xb
            t1_ap, t1_buf = t1
            t2_ap, t2_buf = t2
            nst = int(os.environ.get('DBG_RSTEPS', '5'))
            P.op("act", I("copy", xb_ap[0:rows, :], ps_ap[0:rows, :]), reads=[ps_buf], writes=[xb_buf])
            if nst < 2: return
            P.op("pe", I("matmul", ps2_ap[0:rows, :], lhsT=rmat[0:rows, 0:rows], rhs=xb_ap[0:rows, :], start=True, stop=True),
                 reads=[xb_buf], writes=[ps2_buf])
            if nst < 3: return
            v = os.environ.get('DBG_T1', '')
            if v == 'xb':
                P.op("dve", I("tensor_tensor", t1_ap[0:rows, :], xb_ap[0:rows, :], cos_ap[0:rows, :], ALU.mult),
                     reads=[xb_buf, cs_buf], writes=[t1_buf])
            elif v == 'sin':
                P.op("dve", I("tensor_tensor", t1_ap[0:rows, :], ps_ap[0:rows, :], sin_ap[0:rows, :], ALU.mult),
                     reads=[ps_buf, cs_buf], writes=[t1_buf])
            elif v == 't2':
                P.op("dve", I("tensor_tensor", t2_ap[0:rows, :], ps_ap[0:rows, :], cos_ap[0:rows, :], ALU.mult),
                     reads=[ps_buf, cs_buf], writes=[t2_buf])
            else:
                P.op("dve", I("tensor_tensor", t1_ap[0:rows, :], ps_ap[0:rows, :], cos_ap[0:rows, :], ALU.mult),
                     reads=[ps_buf, cs_buf, xb_buf], writes=[t1_buf])
            if nst < 4: return
            P.op("dve", I("tensor_tensor", t2_ap[0:rows, :], ps2_ap[0:rows, :], sin_ap[0:rows, :], ALU.mult),
                 reads=[ps2_buf, cs_buf], writes=[t2_buf])
            if nst < 5: return
            P.op("pool", I("tensor_tensor", stg_ap[0:rows, :], t1_ap[0:rows, :], t2_ap[0:rows, :], ALU.add),
                 reads=[t1_buf, t2_buf], writes=[stg_buf])

        class LNCtx:
            pass

        def ln_setup(g_src, b_src, need_y=True):
            c = LNCtx()
            c.g = PF.take(1024)
            c.b = PF.take(1024)
            c.gb = Buf("lngb")
            P.dma([(c.g, g_src.partition_broadcast(128)[:, 0, :]), (c.b, b_src.partition_broadcast(128)[:, 0, :])],
                  writes=[c.gb], owner=c.gb)
            c.y = [(PF.take(1024), Buf("lny%d" % i)) for i in range(2)] if need_y else None
            c.xo = [(PF.take(1024), Buf("lnxo%d" % i)) for i in range(2)]
            c.xb = [(PB.take(1024), Buf("lnxb%d" % i)) for i in range(2)]
            c.st = [(lnst[:, i * 16:(i + 1) * 16], Buf("lnst%d" % i)) for i in range(2)]
            c.k = 0
            return c

        def ln_tile(c, y_ap, y_buf, t, psT_ap, psT_buf, final=False):
            k = c.k
            c.k += 1
            st_ap, st_buf = c.st[k % 2]
            xo_ap, xo_buf = c.xo[k % 2]
            xb_ap, xb_buf = c.xb[k % 2]
            P.op("dve", [I("bn_stats", st_ap[:, 0:6], y_ap[:, 0:512]),
                         I("bn_stats", st_ap[:, 6:12], y_ap[:, 512:1024])], reads=[y_buf], writes=[st_buf])
            P.op("dve", I("bn_aggr", st_ap[:, 12:14], st_ap[:, 0:12]), reads=[st_buf], writes=[st_buf])
            P.op("act", I("activation", st_ap[:, 14:15], st_ap[:, 13:14], AF.Sqrt, bias=epsln[:, 0:1], scale=1.0), reads=[st_buf, cbuf], writes=[st_buf])
            P.op("dve", I("reciprocal", st_ap[:, 14:15], st_ap[:, 14:15]), reads=[st_buf], writes=[st_buf])
            P.op("dve", I("tensor_scalar", y_ap, y_ap, st_ap[:, 12:13], st_ap[:, 14:15], op0=ALU.subtract, op1=ALU.mult),
                 reads=[st_buf, y_buf], writes=[y_buf])
            P.op("pool", I("tensor_tensor", xo_ap, y_ap, c.g, ALU.mult), reads=[y_buf, c.gb], writes=[xo_buf])
            P.op("pool", I("tensor_tensor", xo_ap, xo_ap, c.b, ALU.add), reads=[xo_buf, c.gb], writes=[xo_buf])
            pairs = [(XTOK[t * 128:(t + 1) * 128, :], xo_ap)]
            if final:
                pairs = [(out[t * 128:(t + 1) * 128, :], xo_ap)]
            P.dma(pairs, reads=[xo_buf], owner=xo_buf, kind="s")
            if final:
                return
            P.op("act", I("copy", xb_ap, xo_ap), reads=[xo_buf], writes=[xb_buf])
            psb = psT_ap.bitcast(BF16)
            P.op("pe", [I("transpose", psb[:, kc * 128:(kc + 1) * 128], xb_ap[:, kc * 128:(kc + 1) * 128], ident[:, :])
                        for kc in range(KC)], reads=[xb_buf, cbuf], writes=[psT_buf])
            P.op("act", I("copy", xT[:, :, t * 128:(t + 1) * 128], psb.rearrange("p (a n) -> p a n", n=128)),
                 reads=[psT_buf], writes=[])

        cbuf = Buf("consts")
        P.dma([(ident[:, :], c_ident), (onesb[:, :], c_onesb), (onesf[:, :], c_onesf), (rd[:, :], c_rd), (rm[:, :], c_rm),
               (rk[:, :], c_rk)], writes=[cbuf], owner=cbuf)
        P.op("pool", [I("memset", epsln[:, :], LN_EPS), I("memset", epsrms[:, :], RMS_EPS)], writes=[cbuf])
        sbuf_ = Buf("smalls")
        pairs = []
        for l in range(NL):
            pairs += [(bg_t[:, l * 24:(l + 1) * 24], bgate[l]), (gq_t[:, l * 3:(l + 1) * 3], gq_in[l]),
                      (gkv_t[:, l * 2:(l + 1) * 2], gkv_in[l]), (gsc_t[:, l:l + 1], subg[l]),
                      (lamw[:, l * 256:(l + 1) * 256], dlam[l].partition_broadcast(128)[:, 0, :])]
        P.dma(pairs, writes=[sbuf_], owner=sbuf_)
        lb = Buf("lam")
        for l in range(NL):
            lam_init = 0.8 - 0.6 * math.exp(-0.3 * l)
            lw = lamw[:, l * 256:(l + 1) * 256].rearrange("p (a b) -> p a b", b=64)
            lp = lamp[:, l * 128:(l + 1) * 128].rearrange("p (a b) -> p a b", b=64)
            P.op("dve", I("tensor_tensor", lp, lw[:, 0:4:2, :], lw[:, 1:4:2, :], ALU.mult), reads=[sbuf_], writes=[lb])
            P.op("dve", I("reduce_sum", lams[:, l * 2:(l + 1) * 2], lp, AX.X), reads=[lb], writes=[lb])
            P.op("act", I("activation", lame[:, l * 2:(l + 1) * 2], lams[:, l * 2:(l + 1) * 2], AF.Exp), reads=[lb], writes=[lb])
            P.op("dve", I("tensor_tensor", nlam_t[:, l:l + 1], lame[:, l * 2 + 1:l * 2 + 2], lame[:, l * 2:l * 2 + 1], ALU.subtract),
                 reads=[lb], writes=[lb])
            P.op("dve", I("tensor_scalar", nlam_t[:, l:l + 1], nlam_t[:, l:l + 1], -lam_init, None, op0=ALU.add),
                 reads=[lb], writes=[lb])
            P.op("dve", I("tensor_scalar", gsc_t[:, l:l + 1], gsc_t[:, l:l + 1], 1.0 - lam_init, None, op0=ALU.mult),
                 reads=[lb, sbuf_], writes=[lb])

        PB.reset(); PF.reset()
        c = ln_setup(ln_in_g[0:1, :], ln_in_b[0:1, :])
        psT = [(bank(i), Buf("psT%d" % i)) for i in range(2)]
        for t in range(T // 128):
            y_ap, y_buf = c.y[t % 2]
            P.dma([(y_ap, x_in[t * 128:(t + 1) * 128, :])], writes=[y_buf], owner=y_buf)
            ln_tile(c, y_ap, y_buf, t, psT[t % 2][0], psT[t % 2][1])
        P.barrier()

        for l in range(n_layers):
            if stop_after == "0":
                break
            last_layer = (l == NL - 1)
            PB.reset(); PF.reset()
            wstage = [(PF.take(2048), Buf("wstg%d" % i)) for i in range(2)]
            wst = [(PB.take(2048).rearrange("p (a n) -> p a n", n=256), Buf("wst%d" % i)) for i in range(3)]
            stg = [(PB.take(512), Buf("stg%d" % i)) for i in range(4)]
            stgf = [(PF.take(512), Buf("stgf%d" % i)) for i in range(2)]
            xbr = [(PB.take(512), Buf("xbr%d" % i)) for i in range(2)]
            t1s = [(PF.take(512), Buf("t1s%d" % i)) for i in range(2)]
            t2s = [(PF.take(512), Buf("t2s%d" % i)) for i in range(2)]
            css = [(PF.take(1024), Buf("css%d" % i)) for i in range(2)]
            psA = [(bank(i), Buf("psA%d" % i)) for i in range(6)]
            ps2 = [(bank(6 + i), Buf("ps2%d" % i)) for i in range(2)]
            cnt = dict(item=0, stg=0, stgf=0, rope=0)

            strips = []
            for e0 in range(0, 1024, 256):
                strips.append(("fm", e0))
            strips.append(("v", 1024)); strips.append(("v", 1280))
            for e0 in range(1536, 2560, 256):
                strips.append(("fm", e0))
            strips.append(("v", 2560)); strips.append(("v", 2816))
            for e0 in range(3072, 3584, 256):
                strips.append(("fm", e0))
            strips.append(("fm", 3584))
            for e0 in range(3744, DIN, 256):
                strips.append(("gate", e0))

            def chunk_dst(e0, wd):
                if e0 < 512:
                    return "plain", NQ[e0 // 128]
                if e0 < 1024:
                    return "plain", NK[(e0 - 512) // 128]
                if 1536 <= e0 < 2048:
                    return "rope", DQ[(e0 - 1536) // 128]
                if 2048 <= e0 < 2560:
                    return "rope", DK[(e0 - 2048) // 128]
                if 3072 <= e0 < 3712:
                    return "f32", MC[(e0 - 3072) // 128]
                if e0 == 3712:
                    return "f32", MC[5]
                if e0 >= 3744:
                    return "gate", GT[(e0 - 3744) // 128]
                raise AssertionError(e0)

            def load_strip(si_):
                kind, e0 = strips[si_]
                ncol = 256
                if kind == "fm" and e0 == 3584:
                    ncol = 160
                w_ap, w_buf = wst[si_ % 3]
                src = w_in[l][:, e0:e0 + ncol].rearrange("(kc p) e -> p kc e", p=128)
                wload(src, w_ap[:, :, 0:ncol], w_buf, wstage)

            load_strip(0)
            load_strip(1)
            for si_, (kind, e0) in enumerate(strips):
                if si_ + 2 < len(strips):
                    load_strip(si_ + 2)
                w_ap, w_buf = wst[si_ % 3]
                if kind == "v":
                    vdst = NV if e0 < 1536 else DV
                    c0 = (e0 - 1024) if e0 < 1536 else (e0 - 2560)
                    for t in range(T // 128):
                        ps_ap, ps_buf = psA[cnt["item"] % 6]
                        cnt["item"] += 1
                        P.op("pe", [I("matmul",
                            ps_ap[:, 0:256], lhsT=xT[:, kc, t * 128:(t + 1) * 128], rhs=w_ap[:, kc, :], start=(kc == 0), stop=(kc == KC - 1))
                            for kc in range(KC)], reads=[w_buf], writes=[ps_buf])
                        s_ap, s_buf = stg[cnt["stg"] % 4]
                        cnt["stg"] += 1
                        P.op("act", I("copy", s_ap[:, 0:256], ps_ap[:, 0:256]), reads=[ps_buf], writes=[s_buf])
                        P.dma([(vdst[t * 128:(t + 1) * 128, c0:c0 + 256], s_ap[:, 0:256])], reads=[s_buf], owner=s_buf, kind="s")
                    continue
                if kind == "fm" and e0 == 3584:
                    chunks = [(3584, 128, 0), (3712, 32, 128)]
                else:
                    chunks = [(e0, 128, 0), (e0 + 128, 128, 128)]
                for (ce0, wd, coff) in chunks:
                    ckind, dst = chunk_dst(ce0, wd)
                    for tb in range(8):
                        ps_ap, ps_buf = psA[cnt["item"] % 6]
                        cnt["item"] += 1
                        P.op("pe", [I("matmul",
                            ps_ap[0:wd, :], lhsT=w_ap[:, kc, coff:coff + wd], rhs=xT[:, kc, tb * 512:(tb + 1) * 512],
                            start=(kc == 0), stop=(kc == KC - 1)) for kc in range(KC)], reads=[w_buf], writes=[ps_buf])
                        dsl = dst[0:wd, tb * 512:(tb + 1) * 512]
                        if ckind == "plain":
                            s_ap, s_buf = stg[cnt["stg"] % 4]; cnt["stg"] += 1
                            P.op("act", I("copy", s_ap, ps_ap), reads=[ps_buf], writes=[s_buf])
                            P.dma([(dsl, s_ap)], reads=[s_buf], owner=s_buf, kind="s")
                        elif ckind == "gate":
                            gi = (ce0 - 3744) // 128
                            s_ap, s_buf = stg[cnt["stg"] % 4]; cnt["stg"] += 1
                            P.op("act", I("activation",
                                s_ap, ps_ap, AF.Sigmoid, bias=bg_t[:, l * 24 + gi:l * 24 + gi + 1]), reads=[ps_buf], writes=[s_buf])
                            P.dma([(dsl, s_ap)], reads=[s_buf], owner=s_buf, kind="s")
                        elif ckind == "f32":
                            s_ap, s_buf = stgf[cnt["stgf"] % 2]; cnt["stgf"] += 1
                            P.op("act", I("copy", s_ap[0:wd, :], ps_ap[0:wd, :]), reads=[ps_buf], writes=[s_buf])
                            P.dma([(dsl, s_ap[0:wd, :])], reads=[s_buf], owner=s_buf, kind="s")
                        else:
                            k = cnt["rope"]; cnt["rope"] += 1
                            cs_ap, cs_buf = css[k % 2]
                            P.dma([(cs_ap[:, 0:512], c_cosd[:, tb * 512:(tb + 1) * 512]), (cs_ap[:, 512:1024], c_sind[:, tb * 512:(tb + 1) * 512])],
                                  writes=[cs_buf], owner=cs_buf)
                            s_ap, s_buf = stg[cnt["stg"] % 4]; cnt["stg"] += 1
                            rope_epi(ps_ap, ps_buf, 128, cs_ap[:, 0:512], cs_ap[:, 512:1024], cs_buf, rd, ps2[k % 2][0], ps2[k % 2][1],
                                     xbr[k % 2], t1s[k % 2], t2s[k % 2], s_ap, s_buf)
                            P.dma([(dsl, s_ap)], reads=[s_buf], owner=s_buf, kind="s")
            P.barrier()
            if stop_after == "A":
                break

            PB.reset(); PF.reset()
            wstage = [(PF.take(2048), Buf("wstg%d" % i)) for i in range(1)]
            wqb = PB.take(3 * 800).rearrange("p (a n) -> p a n", n=800); wqb_buf = Buf("wqb")
            zt_ap = PB.take(512); zt_buf = Buf("zt")
            P.op("pool", [I("memset", wqb[:, :, 768:800], 0.0), I("memset", zt_ap, 0.0)], writes=[wqb_buf, zt_buf])
            wkvb = PB.take(2 * 1024).rearrange("p (a n) -> p a n", n=1024); wkvb_buf = Buf("wkvb")
            wload(w_qb[l].rearrange("(kc p) e -> p kc e", p=128), wqb[:, :, 0:768], wqb_buf, wstage)
            wload(w_kvb[l].rearrange("(kc p) e -> p kc e", p=128), wkvb, wkvb_buf, wstage)
            wkvb_v = wkvb.rearrange("p a (h two d) -> p a h two d", two=2, d=64)
            cq = [(PF.take(5 * 512).rearrange("p (a n) -> p a n", n=512), Buf("cq%d" % i)) for i in range(1)]
            krs = [(PF.take(512), Buf("krs%d" % i)) for i in range(1)]
            sq = (PF.take(512), Buf("sq"))
            rstd = [(PF.take(512), Buf("rstd%d" % i)) for i in range(2)]
            cn = (PB.take(5 * 512).rearrange("p (a n) -> p a n", n=512), Buf("cn"))
            hlb = (PB.take(1024).rearrange("p (a n) -> p a n", n=512), Buf("hlb"))
            css = [(PF.take(1024), Buf("cssm%d" % i)) for i in range(1)]
            csk = [(PF.take(1024), Buf("csk%d" % i)) for i in range(1)]
            xbr = [(PB.take(512), Buf("xbr%d" % i)) for i in range(2)]
            t1s = [(PF.take(512), Buf("t1s%d" % i)) for i in range(1)]
            t2s = [(PF.take(512), Buf("t2s%d" % i)) for i in range(1)]
            stg = [(PB.take(512), Buf("stg%d" % i)) for i in range(4)]
            psA = [(bank(i), Buf("psA%d" % i)) for i in range(4)]
            ps2 = [(bank(4 + i), Buf("ps2%d" % i)) for i in range(2)]
            psS = [(bank(6 + i), Buf("psS%d" % i)) for i in range(2)]
            cnt = dict(item=0, stg=0, rope=0)
            for tb in range(int(os.environ.get('DBG_BTB', '8'))):
                tsl = slice(tb * 512, (tb + 1) * 512)
                cq_ap, cq_buf = cq[0]
                kr_ap, kr_buf = krs[0]
                P.dma([(cq_ap, MC[0:5, :, tsl].rearrange("c p t -> p c t"))], writes=[cq_buf], owner=cq_buf)
                P.dma([(kr_ap[0:32, :], MC[5, 0:32, tsl])], writes=[kr_buf], owner=kr_buf)
                cs_ap, cs_buf = css[0]
                P.dma([(cs_ap[:, 0:512], c_cosm[:, tsl]), (cs_ap[:, 512:1024], c_sinm[:, tsl])], writes=[cs_buf], owner=cs_buf)
                ck_ap, ck_buf = csk[0]
                P.dma([(ck_ap[0:32, 0:512], c_cosk[:, tsl]), (ck_ap[0:32, 512:1024], c_sink[:, tsl])], writes=[ck_buf], owner=ck_buf)
                cn_ap, cn_buf = cn
                hl_ap, hl_buf = hlb
                for (c0, ncn, gt, goff, nfeat) in ((0, 3, gq_t, l * 3, 384.0), (3, 2, gkv_t, l * 2, 256.0)):
                    pS_ap, pS_buf = psS[0 if c0 == 0 else 1]
                    sq_ap, sq_buf = sq
                    for ci in range(ncn):
                        P.op("act", I("activation", sq_ap, cq_ap[:, c0 + ci, :], AF.Square), reads=[cq_buf], writes=[sq_buf])
                        P.op("pool", I("tensor_copy", hl_ap[:, 0, :], sq_ap), reads=[sq_buf], writes=[hl_buf])
                        P.op("pool", I("tensor_tensor", hl_ap[:, 1, :], sq_ap, hl_ap[:, 0, :], ALU.subtract), reads=[sq_buf, hl_buf], writes=[hl_buf])
                        P.op("pe", [I("matmul", pS_ap, lhsT=onesb[:, :], rhs=hl_ap[:, 0, :], start=(ci == 0), stop=False),
                                    I("matmul", pS_ap, lhsT=onesb[:, :], rhs=hl_ap[:, 1, :], start=False, stop=(ci == ncn - 1))],
                             reads=[hl_buf], writes=[pS_buf])
                    r_ap, r_buf = rstd[0 if c0 == 0 else 1]
                    P.op("act", I("activation", r_ap, pS_ap, AF.Sqrt, bias=epsrms[:, 0:1], scale=1.0 / nfeat), reads=[pS_buf], writes=[r_buf])
                    P.op("dve", I("reciprocal", r_ap, r_ap), reads=[r_buf], writes=[r_buf])
                    for ci in range(ncn):
                        P.op("dve", I("scalar_tensor_tensor",
                            cn_ap[:, c0 + ci, :], cq_ap[:, c0 + ci, :], gt[:, goff + ci:goff + ci + 1], r_ap, op0=ALU.mult, op1=ALU.mult),
                            reads=[cq_buf, r_buf], writes=[cn_buf])
                for h in range(int(os.environ.get('DBG_QH', '8')) if 'q' in DBGB else 0):
                    ps_ap, ps_buf = psA[cnt["item"] % 4]; cnt["item"] += 1
                    P.op("pe", [I("matmul", ps_ap, lhsT=wqb[:, ci, h * 96:h * 96 + 128], rhs=cn_ap[:, ci, :],
                                                                        start=(ci == 0), stop=(ci == 2)) for ci in range(3)],
                         reads=[cn_buf, wqb_buf], writes=[ps_buf])
                    k = cnt["rope"]; cnt["rope"] += 1
                    s_ap, s_buf = stg[cnt["stg"] % 4]; cnt["stg"] += 1
                    if os.environ.get('DBG_NOROPE'):
                        P.op("act", I("copy", s_ap, ps_ap), reads=[ps_buf], writes=[s_buf])
                    else:
                        rope_epi(ps_ap, ps_buf, int(os.environ.get('DBG_ROWS', '128')), cs_ap[:, 0:512], cs_ap[:, 512:1024], cs_buf, (rd if os.environ.get('DBG_RD') else rm), ps2[k % 2][0], ps2[k % 2][1],
                                 xbr[k % 2], t1s[0], t2s[0], s_ap, s_buf)
                    P.dma([(MQ[h, :, tsl], s_ap)], reads=[s_buf], owner=s_buf, kind="s")
                for h in range(8 if 'k' in DBGB else 0):
                    ps_ap, ps_buf = psA[cnt["item"] % 4]; cnt["item"] += 1
                    P.op("pe", [I("matmul", ps_ap[0:64, :], lhsT=wkvb[:, ci, h * 128:h * 128 + 64], rhs=cn_ap[:, 3 + ci, :],
                                                                        start=(ci == 0), stop=(ci == 1)) for ci in range(2)],
                         reads=[cn_buf, wkvb_buf], writes=[ps_buf])
                    s_ap, s_buf = stg[cnt["stg"] % 4]; cnt["stg"] += 1
                    P.op("act", I("copy", s_ap[0:64, :], ps_ap[0:64, :]), reads=[ps_buf], writes=[s_buf])
                    P.dma([(MK[h, 0:64, tsl], s_ap[0:64, :])], reads=[s_buf], owner=s_buf, kind="s")
                for s in range(4 if 'v' in DBGB else 0):
                    ps_ap, ps_buf = psA[cnt["item"] % 4]; cnt["item"] += 1
                    P.op("pe", [I("matmul", ps_ap.rearrange("p (h d) -> p h d", d=64), lhsT=cn_ap[:, 3 + ci, s * 128:(s + 1) * 128],
                                                                        rhs=wkvb_v[:, ci, :, 1, :], start=(ci == 0), stop=(ci == 1)) for ci in range(2)],
                         reads=[cn_buf, wkvb_buf], writes=[ps_buf])
                    s_ap, s_buf = stg[cnt["stg"] % 4]; cnt["stg"] += 1
                    P.op("act", I("copy", s_ap, ps_ap), reads=[ps_buf], writes=[s_buf])
                    tt = tb * 4 + s
                    P.dma([(MV[tt * 128:(tt + 1) * 128, :], s_ap)], reads=[s_buf], owner=s_buf, kind="s")
                if 'r' not in DBGB:
                    continue
                k = cnt["rope"]; cnt["rope"] += 1
                s_ap, s_buf = stg[cnt["stg"] % 4]; cnt["stg"] += 1
                rope_epi(kr_ap, kr_buf, 32, ck_ap[:, 0:512], ck_ap[:, 512:1024], ck_buf, rk, ps2[k % 2][0], ps2[k % 2][1],
                         xbr[k % 2], t1s[0], t2s[0], s_ap, s_buf)
                P.dma([(MK[h, 64:96, tsl], s_ap[0:32, :]) for h in range(8)], reads=[s_buf], owner=s_buf, kind="s")
                P.dma([(MK[h, 96:128, tsl], zt_ap[0:32, :]) for h in range(8)], reads=[zt_buf], owner=zt_buf, kind="s")
            P.barrier()
            if stop_after == "B":
                break

            PB.reset(); PF.reset()
            hq = [(PB.take(T), Buf("hq%d" % i)) for i in range(2)]
            hk = [(PB.take(T), Buf("hk%d" % i)) for i in range(2)]
            hv = [(PB.take(32 * 128).rearrange("p (a n) -> p a n", n=128), Buf("hv%d" % i)) for i in range(2)]
            pts = [(PB.take(1024).rearrange("p (a n) -> p a n", n=512), Buf("pt%d" % i)) for i in range(3)]
            stg = [(PB.take(512), Buf("stg%d" % i)) for i in range(2)]
            pss = [(PSt[i][:, :].rearrange("p (a n) -> p a n", n=512), Buf("pss%d" % i)) for i in range(2)]
            accO = [(bank(4 + 2 * i), Buf("accO%d" % i)) for i in range(2)]
            accS = [(bank(5 + 2 * i), Buf("accS%d" % i)) for i in range(2)]
            rr = [(PF.take(512), Buf("rr%d" % i)) for i in range(2)]
            a1 = (PF.take(512), Buf("a1"))
            a2 = (PF.take(512), Buf("a2"))
            oo = (PF.take(512), Buf("oo"))
            sqq = (PF.take(512), Buf("sqq"))
            hld = (PB.take(1024).rearrange("p (a n) -> p a n", n=512), Buf("hld"))
            cnt = dict(g=0, acc=0, pt=0, stg=0, rr=0)

            def attn_pass(q_ap, k_ap, q_buf, k_buf, v_ap, v_buf, kd, dv, scale, qb):
                ai = cnt["acc"] % 2; cnt["acc"] += 1
                O_ap, O_buf = accO[ai]
                S_ap, S_buf = accS[ai]
                NG = 16
                gl = []

                def emit_S(g):
                    ps_ap, ps_buf = pss[(cnt["g"] + g) % 2]
                    P.op("pe", [I("matmul", ps_ap[:, j, :], lhsT=k_ap[0:kd, (2 * g + j) * 128:(2 * g + j + 1) * 128],
                                                                       rhs=q_ap[0:kd, qb * 512:(qb + 1) * 512], start=True, stop=True) for j in range(2)],
                         reads=[q_buf, k_buf], writes=[ps_buf])

                emit_S(0)
                emit_S(1)
                for g in range(NG):
                    ps_ap, ps_buf = pss[(cnt["g"] + g) % 2]
                    pt_ap, pt_buf = pts[cnt["pt"] % 3]; cnt["pt"] += 1
                    P.op("act", I("activation", pt_ap, ps_ap, AF.Exp, scale=scale), reads=[ps_buf], writes=[pt_buf])
                    fns = []
                    for j in range(2):
                        first = (g == 0 and j == 0)
                        lastf = (g == NG - 1 and j == 1)
                        fns.append(I("matmul",
                            O_ap[0:dv, :], lhsT=v_ap[:, 2 * g + j, 0:dv], rhs=pt_ap[:, j, :], start=first, stop=lastf))
                        fns.append(I("matmul",
                            S_ap[0:dv, :], lhsT=onesb[:, 0:dv], rhs=pt_ap[:, j, :], start=first, stop=lastf))
                    P.op("pe", fns, reads=[pt_buf, v_buf], writes=[O_buf, S_buf])
                    if g + 2 < NG:
                        emit_S(g + 2)
                cnt["g"] += NG
                return ai

            heads = [("d", h) for h in range(int(os.environ.get('DBG_ND', '4')))] + [("m", h) for h in range(int(os.environ.get('DBG_NM', '8')))]

            def load_head(i):
                kind, h = heads[i]
                q_ap, q_buf = hq[i % 2]; k_ap, k_buf = hk[i % 2]; v_ap, v_buf = hv[i % 2]
                if kind == "d":
                    P.dma([(q_ap, DQ[h])], writes=[q_buf], owner=q_buf)
                    P.dma([(k_ap, DK[h])], writes=[k_buf], owner=k_buf)
                    P.dma([(v_ap, DV[:, h * 128:(h + 1) * 128].rearrange("(t p) d -> p t d", p=128))], writes=[v_buf], owner=v_buf)
                else:
                    P.dma([(q_ap, MQ[h])], writes=[q_buf], owner=q_buf)
                    P.dma([(k_ap, MK[h])], writes=[k_buf], owner=k_buf)
                    P.dma([(v_ap[:, :, 0:64], MV[:, h * 64:(h + 1) * 64].rearrange("(t p) d -> p t d", p=128))], writes=[v_buf], owner=v_buf)

            load_head(0)
            for i, (kind, h) in enumerate(heads):
                if i + 1 < len(heads):
                    load_head(i + 1)
                q_ap, q_buf = hq[i % 2]; k_ap, k_buf = hk[i % 2]; v_ap, v_buf = hv[i % 2]
                for qb in range(int(os.environ.get('DBG_NQB', '8'))):
                    tsl = slice(qb * 512, (qb + 1) * 512)
                    if kind == "d":
                        res = []
                        for comp in range(2):
                            ai = attn_pass(q_ap[comp * 64:(comp + 1) * 64, :], k_ap[comp * 64:(comp + 1) * 64, :], q_buf, k_buf,
                                           v_ap, v_buf, 64, 128, 0.125, qb)
                            O_ap, O_buf = accO[ai]; S_ap, S_buf = accS[ai]
                            r_ap, r_buf = rr[cnt["rr"] % 2]; cnt["rr"] += 1
                            a_ap, a_buf = (a1, a2)[comp]
                            P.op("dve", I("reciprocal", r_ap, S_ap), reads=[S_buf], writes=[r_buf])
                            P.op("dve", I("tensor_tensor", a_ap, O_ap, r_ap, ALU.mult),
                                 reads=[O_buf, r_buf], writes=[a_buf])
                            res.append(ai)
                        ai = res[1]
                        S_ap, S_buf = accS[ai]
                        o_ap, o_buf = oo
                        s_ap2, s_buf2 = sqq
                        P.op("dve", I("scalar_tensor_tensor", o_ap, a2[0], nlam_t[:, l:l + 1], a1[0], op0=ALU.mult, op1=ALU.add),
                             reads=[a1[1], a2[1]], writes=[o_buf])
                        P.op("pool", I("tensor_tensor", s_ap2, o_ap, o_ap, ALU.mult), reads=[o_buf], writes=[s_buf2])
                        P.op("pool", I("tensor_copy", hld[0][:, 0, :], s_ap2), reads=[s_buf2], writes=[hld[1]])
                        P.op("pool", I("tensor_tensor", hld[0][:, 1, :], s_ap2, hld[0][:, 0, :], ALU.subtract), reads=[s_buf2, hld[1]], writes=[hld[1]])
                        P.op("pe", [I("matmul", S_ap, lhsT=onesb[:, :], rhs=hld[0][:, 0, :], start=True, stop=False),
                                    I("matmul", S_ap, lhsT=onesb[:, :], rhs=hld[0][:, 1, :], start=False, stop=True)], reads=[hld[1]], writes=[S_buf])
                        r_ap, r_buf = rr[cnt["rr"] % 2]; cnt["rr"] += 1
                        P.op("act", I("activation", r_ap, S_ap, AF.Sqrt, bias=epsrms[:, 0:1], scale=1.0 / 128.0), reads=[S_buf], writes=[r_buf])
                        P.op("dve", I("reciprocal", r_ap, r_ap), reads=[r_buf], writes=[r_buf])
                        st_ap, st_buf = stg[cnt["stg"] % 2]; cnt["stg"] += 1
                        P.op("dve", I("scalar_tensor_tensor", st_ap, o_ap, gsc_t[:, l:l + 1], r_ap, op0=ALU.mult, op1=ALU.mult),
                             reads=[o_buf, r_buf], writes=[st_buf])
                        P.dma([(AO[1, h, :, tsl], st_ap)], reads=[st_buf], owner=st_buf, kind="s")
                    else:
                        ai = attn_pass(q_ap, k_ap, q_buf, k_buf, v_ap, v_buf, 128, 64, 96.0 ** -0.5, qb)
                        O_ap, O_buf = accO[ai]; S_ap, S_buf = accS[ai]
                        r_ap, r_buf = rr[cnt["rr"] % 2]; cnt["rr"] += 1
                        P.op("dve", I("reciprocal", r_ap[0:64, :], S_ap[0:64, :]), reads=[S_buf], writes=[r_buf])
                        st_ap, st_buf = stg[cnt["stg"] % 2]; cnt["stg"] += 1
                        P.op("dve", I("tensor_tensor", st_ap[0:64, :], O_ap[0:64, :], r_ap[0:64, :], ALU.mult),
                             reads=[O_buf, r_buf], writes=[st_buf])
                        P.dma([(AO[2, h // 2, (h % 2) * 64:(h % 2) * 64 + 64, tsl], st_ap[0:64, :])], reads=[st_buf], owner=st_buf, kind="s")
            P.barrier()
            if stop_after == "CD":
                break

            PB.reset(); PF.reset()
            nq = [(PB.take(T), Buf("nq%d" % i)) for i in range(2)]
            nk = [(PB.take(T), Buf("nk%d" % i)) for i in range(2)]
            nve = [(PB.take(32 * 64).rearrange("p (a n) -> p a n", n=64), Buf("nve%d" % i)) for i in range(2)]
            nvo = [(PB.take(32 * 64).rearrange("p (a n) -> p a n", n=64), Buf("nvo%d" % i)) for i in range(2)]
            npt = [(PB.take(1024).rearrange("p (r j n) -> p r j n", r=4, j=4), Buf("npt%d" % i)) for i in range(2)]
            nout = [(PB.take(T), Buf("nout%d" % i)) for i in range(2)]
            tab = [(PF.take(14 * 64).rearrange("p (s n) -> p s n", n=64), Buf("tab%d" % i)) for i in range(2)]
            msk = (PF.take(14 * 64).rearrange("p (s n) -> p s n", n=64), Buf("msk"))
            tmp = [(PF.take(1024).rearrange("p (r j n) -> p r j n", r=4, j=4), Buf("tmp%d" % i)) for i in range(2)]
            nrr = [(PF.take(256), Buf("nrr%d" % i)) for i in range(2)]
            pssn = [(PSt[i][:, :].rearrange("p (r j n) -> p r j n", r=4, j=4), Buf("pssn%d" % i)) for i in range(2)]
            naO = [(bank(4 + 2 * i), Buf("naO%d" % i)) for i in range(2)]
            naS = [(bank(5 + 2 * i), Buf("naS%d" % i)) for i in range(2)]
            P.dma([(msk[0].rearrange("p s n -> p (s n)"), namask)], writes=[msk[1]], owner=msk[1])
            it = 0
            for cch in range(4):
                q_ap, q_buf = nq[cch % 2]; k_ap, k_buf = nk[cch % 2]
                P.dma([(q_ap, NQ[cch])], writes=[q_buf], owner=q_buf)
                P.dma([(k_ap, NK[cch])], writes=[k_buf], owner=k_buf)
                for hh in range(2):
                    h = cch * 2 + hh
                    hi = h % 2
                    ve_ap, ve_buf = nve[hi]; vo_ap, vo_buf = nvo[hi]
                    P.dma([(ve_ap, NV[:, h * 64:(h + 1) * 64].rearrange("(t p) d -> p t d", p=128))], writes=[ve_buf], owner=ve_buf)
                    P.dma([(vo_ap[:, 0:31, :], NV[64:T - 64, h * 64:(h + 1) * 64].rearrange("(t p) d -> p t d", p=128))], writes=[vo_buf], owner=vo_buf)
                    tb_ap, tb_buf = tab[hi]
                    src = bass.AP(tensor=nabias.tensor, offset=nabias[l, h].offset, ap=[[64, 128], [64 * 64, 14], [1, 64]])
                    P.dma([(tb_ap, src)], writes=[tb_buf], owner=tb_buf)
                    P.op("pool", I("tensor_tensor", tb_ap, tb_ap, msk[0], ALU.add), reads=[tb_buf, msk[1]], writes=[tb_buf])
                    o_ap, o_buf = nout[hi]
                    qh = q_ap[hh * 64:(hh + 1) * 64, :]
                    kh = k_ap[hh * 64:(hh + 1) * 64, :]
                    for rg in range(16):
                        ps_ap, ps_buf = pssn[it % 2]
                        tm_ap, tm_buf = tmp[it % 2]
                        pt_ap, pt_buf = npt[it % 2]
                        O_ap, O_buf = naO[it % 2]; S_ap, S_buf = naS[it % 2]
                        r_ap, r_buf = nrr[it % 2]
                        it += 1
                        fns = []
                        rows = []
                        for ri in range(4):
                            r = rg * 4 + ri
                            rs = min(max(r - 4, 0), 56)
                            rows.append((r, rs))
                            for j in range(4):
                                k0 = (rs + 2 * j) * 64
                                fns.append(I("matmul",
                                    ps_ap[:, ri, j, :], lhsT=kh[:, k0:k0 + 128], rhs=qh[:, r * 64:(r + 1) * 64], start=True, stop=True))
                        P.op("pe", fns, reads=[q_buf, k_buf], writes=[ps_buf])
                        for ri, (r, rs) in enumerate(rows):
                            off = rs - r + 7
                            P.op("dve", I("scalar_tensor_tensor",
                                tm_ap[:, ri, :, :], ps_ap[:, ri, :, :], 0.125, tb_ap[:, off:off + 7:2, :], op0=ALU.mult, op1=ALU.add),
                                reads=[ps_buf, tb_buf], writes=[tm_buf])
                        P.op("act", I("activation", pt_ap, tm_ap, AF.Exp), reads=[tm_buf], writes=[pt_buf])
                        fns = []
                        for ri, (r, rs) in enumerate(rows):
                            for j in range(4):
                                a = rs + 2 * j
                                if a % 2 == 0:
                                    vt = ve_ap[:, a // 2, :]
                                else:
                                    vt = vo_ap[:, (a - 1) // 2, :]
                                fns.append(I("matmul",
                                    O_ap[0:64, ri * 64:(ri + 1) * 64], lhsT=vt, rhs=pt_ap[:, ri, j, :], start=(j == 0), stop=(j == 3)))
                                fns.append(I("matmul",
                                    S_ap[0:64, ri * 64:(ri + 1) * 64], lhsT=onesb[:, 0:64], rhs=pt_ap[:, ri, j, :], start=(j == 0), stop=(j == 3)))
                        P.op("pe", fns, reads=[pt_buf, ve_buf, vo_buf], writes=[O_buf, S_buf])
                        P.op("dve", I("reciprocal", r_ap[0:64, :], S_ap[0:64, 0:256]), reads=[S_buf], writes=[r_buf])
                        P.op("dve", I("tensor_tensor",
                            o_ap[0:64, rg * 256:(rg + 1) * 256], O_ap[0:64, 0:256], r_ap[0:64, :], ALU.mult), reads=[O_buf, r_buf], writes=[o_buf])
                    P.dma([(AO[0, cch, hh * 64:(hh + 1) * 64, :], o_ap[0:64, :])], reads=[o_buf], owner=o_buf, kind="s")
            P.barrier()
            if stop_after == "E":
                break

            PB.reset(); PF.reset()
            wstage = [(PF.take(2048), Buf("wstg%d" % i)) for i in range(1)]
            wbr = PB.take(12 * 1024).rearrange("p (a n) -> p a n", n=1024); wbr_buf = Buf("wbr")
            wo = PB.take(8 * 1024).rearrange("p (a n) -> p a n", n=1024); wo_buf = Buf("wo")
            for i in range(3):
                wload(w_br[l, i].rearrange("(c p) d -> p c d", p=128), wbr[:, 4 * i:4 * i + 4, :], wbr_buf, wstage)
            wload(w_out[l].rearrange("(c p) d -> p c d", p=128), wo, wo_buf, wstage)
            c = ln_setup(ln1_g[l:l + 1, :], ln1_b[l:l + 1, :], need_y=False)
            aob = (PB.take(12 * 512).rearrange("p (a n) -> p a n", n=512), Buf("aob"))
            gts = [(PB.take(3 * 512).rearrange("p (a n) -> p a n", n=512), Buf("gts%d" % i)) for i in range(2)]
            mg = (PB.take(8 * 512).rearrange("p (a n) -> p a n", n=512), [Buf("mg%d" % i) for i in range(8)])
            mm = [(PF.take(512), Buf("mm%d" % i)) for i in range(3)]
            xold = [(PF.take(1024), Buf("xold%d" % i)) for i in range(2)]
            psb = [(bank(i), Buf("psb%d" % i)) for i in range(5)]
            psy = (PSt[3][:, :], Buf("psy"))
            psT = [(bank(5), Buf("psT"))]
            GTv = GT.rearrange("(i c) p t -> c p i t", i=3)
            cnt = dict(b=0, g=0, x=0)
            for tb in range(8):
                tsl = slice(tb * 512, (tb + 1) * 512)
                ao_ap, ao_buf = aob
                P.dma([(ao_ap[:, 4 * i:4 * i + 4, :], AO[i, :, :, tsl].rearrange("c p t -> p c t")) for i in range(3)], writes=[ao_buf], owner=ao_buf)
                for dm in range(8):
                    g_ap, g_buf = gts[cnt["g"] % 2]; cnt["g"] += 1
                    P.dma([(g_ap, GTv[dm, :, :, tsl])], writes=[g_buf], owner=g_buf)
                    pbs = []
                    for i in range(3):
                        pb_ap, pb_buf = psb[cnt["b"] % 5]; cnt["b"] += 1
                        pbs.append((pb_ap, pb_buf))
                        P.op("pe", [I("matmul", pb_ap, lhsT=wbr[:, 4 * i + cc, dm * 128:(dm + 1) * 128],
                                                                                  rhs=ao_ap[:, 4 * i + cc, :], start=(cc == 0), stop=(cc == 3)) for cc in range(4)],
                             reads=[ao_buf, wbr_buf], writes=[pb_buf])
                    for i in range(3):
                        P.op("dve", I("tensor_tensor", mm[i][0], pbs[i][0], g_ap[:, i, :], ALU.mult),
                             reads=[pbs[i][1], g_buf], writes=[mm[i][1]])
                    P.op("pool", I("tensor_tensor", mm[0][0], mm[0][0], mm[1][0], ALU.add), reads=[mm[0][1], mm[1][1]], writes=[mm[0][1]])
                    P.op("pool", I("tensor_tensor", mg[0][:, dm, :], mm[0][0], mm[2][0], ALU.add), reads=[mm[0][1], mm[2][1]], writes=[mg[1][dm]])
                for s in range(4):
                    t = tb * 4 + s
                    xo_ap, xo_buf = xold[cnt["x"] % 2]
                    y_ap, y_buf = xo_ap, xo_buf
                    cnt["x"] += 1
                    P.dma([(xo_ap, XTOK[t * 128:(t + 1) * 128, :])], writes=[xo_buf], owner=xo_buf)
                    py_ap, py_buf = psy
                    P.op("pe", [I("matmul", py_ap[:, half * 512:(half + 1) * 512], lhsT=mg[0][:, dm, s * 128:(s + 1) * 128],
                                                                      rhs=wo[:, dm, half * 512:(half + 1) * 512], start=(dm == 0), stop=(dm == 7))
                                for half in range(2) for dm in range(8)], reads=mg[1] + [wo_buf], writes=[py_buf])
                    P.op("dve", I("scalar_tensor_tensor", y_ap, xo_ap, ALPHA, py_ap, op0=ALU.mult, op1=ALU.add),
                         reads=[xo_buf, py_buf], writes=[y_buf])
                    ln_tile(c, y_ap, y_buf, t, psT[0][0], psT[0][1])
            P.barrier()
            if stop_after == "F":
                break

            PB.reset(); PF.reset()
            wstage = [(PF.take(2048), Buf("wstg%d" % i)) for i in range(2)]
            wg = [(PB.take(1024).rearrange("p (a n) -> p a n", n=128), Buf("wg%d" % i)) for i in range(3)]
            wu = [(PB.take(1024).rearrange("p (a n) -> p a n", n=128), Buf("wu%d" % i)) for i in range(3)]
            stg = [(PB.take(512), Buf("stg%d" % i)) for i in range(3)]
            sg = [(PF.take(512), Buf("sg%d" % i)) for i in range(2)]
            psG = [(bank(2 * i), Buf("psG%d" % i)) for i in range(4)]
            psU = [(bank(2 * i + 1), Buf("psU%d" % i)) for i in range(4)]

            def load_f(fc):
                wload(w_fi[l][:, fc * 128:(fc + 1) * 128].rearrange("(kc p) e -> p kc e", p=128), wg[fc % 3][0], wg[fc % 3][1], wstage)
                wload(w_fi[l][:, DFF + fc * 128:DFF + (fc + 1) * 128].rearrange("(kc p) e -> p kc e", p=128), wu[fc % 3][0], wu[fc % 3][1], wstage)

            load_f(0); load_f(1)
            it = 0
            for fc in range(FC):
                if fc + 2 < FC:
                    load_f(fc + 2)
                wg_ap, wg_buf = wg[fc % 3]; wu_ap, wu_buf = wu[fc % 3]
                for tb in range(8):
                    pg_ap, pg_buf = psG[it % 4]; pu_ap, pu_buf = psU[it % 4]
                    P.op("pe", [I("matmul", pg_ap, lhsT=wg_ap[:, kc, :], rhs=xT[:, kc, tb * 512:(tb + 1) * 512],
                                                                                    start=(kc == 0), stop=(kc == KC - 1)) for kc in range(KC)],
                         reads=[wg_buf], writes=[pg_buf])
                    P.op("pe", [I("matmul", pu_ap, lhsT=wu_ap[:, kc, :], rhs=xT[:, kc, tb * 512:(tb + 1) * 512],
                                                                                    start=(kc == 0), stop=(kc == KC - 1)) for kc in range(KC)],
                         reads=[wu_buf], writes=[pu_buf])
                    sg_ap, sg_buf = sg[it % 2]
                    st_ap, st_buf = stg[it % 3]
                    it += 1
                    P.op("act", I("activation", sg_ap, pg_ap, AF.Silu), reads=[pg_buf], writes=[sg_buf])
                    P.op("dve", I("tensor_tensor", st_ap, pu_ap, sg_ap, ALU.mult),
                         reads=[pu_buf, sg_buf], writes=[st_buf])
                    P.dma([(HT[fc, :, tb * 512:(tb + 1) * 512], st_ap)], reads=[st_buf], owner=st_buf, kind="s")
            P.barrier()
            if stop_after == "G":
                break

            PB.reset(); PF.reset()
            wstage = [(PF.take(2048), Buf("wstg%d" % i)) for i in range(1)]
            wfo = PB.take(FC * 1024).rearrange("p (a n) -> p a n", n=1024); wfo_buf = Buf("wfo")
            wload(w_fo[l].rearrange("(c p) d -> p c d", p=128), wfo, wfo_buf, wstage)
            c = ln_setup(ln2_g[l:l + 1, :], ln2_b[l:l + 1, :], need_y=False)
            hb = [(PB.take(FC * 256).rearrange("p (a n) -> p a n", n=256), Buf("hb%d" % i)) for i in range(2)]
            xold = [(PF.take(1024), Buf("xold%d" % i)) for i in range(2)]
            psy = [(PSt[i][:, :], Buf("psy%d" % i)) for i in range(2)]
            psT = [(bank(4 + i), Buf("psT%d" % i)) for i in range(2)]
            k = 0
            for hbi in range(16):
                h_ap, h_buf = hb[hbi % 2]
                P.dma([(h_ap, HT[:, :, hbi * 256:(hbi + 1) * 256].rearrange("c p t -> p c t"))], writes=[h_buf], owner=h_buf)
                for s in range(2):
                    t = hbi * 2 + s
                    xo_ap, xo_buf = xold[k % 2]
                    y_ap, y_buf = xo_ap, xo_buf
                    py_ap, py_buf = psy[k % 2]
                    pT_ap, pT_buf = psT[k % 2]
                    k += 1
                    P.dma([(xo_ap, XTOK[t * 128:(t + 1) * 128, :])], writes=[xo_buf], owner=xo_buf)
                    P.op("pe", [I("matmul",
                        py_ap[:, half * 512:(half + 1) * 512], lhsT=h_ap[:, fc, s * 128:(s + 1) * 128], rhs=wfo[:, fc, half * 512:(half + 1) * 512],
                        start=(fc == 0), stop=(fc == FC - 1)) for half in range(2) for fc in range(FC)], reads=[h_buf, wfo_buf], writes=[py_buf])
                    P.op("dve", I("scalar_tensor_tensor", y_ap, xo_ap, ALPHA, py_ap, op0=ALU.mult, op1=ALU.add),
                         reads=[xo_buf, py_buf], writes=[y_buf])
                    ln_tile(c, y_ap, y_buf, t, pT_ap, pT_buf, final=(l == n_layers - 1))
            P.barrier()

        P.emit(block)
        build.stats = dict(ninst=P.ninst, ecnt=dict(P.ecnt), maxsem=max(P.semval.values()))
    return nc


def _consts():
    bf = ml_dtypes.bfloat16
    cst = {}
    cst["c_ident"] = np.eye(128, dtype=np.float32).astype(bf)
    cst["c_onesb"] = np.ones((128, 128), dtype=np.float32).astype(bf)
    cst["c_onesf"] = np.ones((128, 128), dtype=np.float32)
    pos = np.arange(T, dtype=np.float32)

    def tables(half):
        inv = np.exp(-math.log(ROPE_THETA) * np.arange(half, dtype=np.float32) / half).astype(np.float32)
        ang = (pos[None, :] * inv[:, None]).astype(np.float32)
        return np.cos(ang.astype(np.float64)).astype(np.float32), np.sin(ang.astype(np.float64)).astype(np.float32)

    cos8, sin8 = tables(8)
    cosd = np.ones((128, T), np.float32); sind = np.zeros((128, T), np.float32)
    rdm = np.zeros((128, 128), np.float32)
    for gb in (0, 64):
        for i in range(8):
            cosd[gb + i] = cos8[i]; cosd[gb + 8 + i] = cos8[i]
            sind[gb + i] = -sin8[i]; sind[gb + 8 + i] = sin8[i]
            rdm[gb + 8 + i, gb + i] = 1.0
            rdm[gb + i, gb + 8 + i] = 1.0
    cst["c_cosd"] = cosd; cst["c_sind"] = sind; cst["c_rd"] = rdm.astype(bf)
    cos16, sin16 = tables(16)
    cosm = np.ones((128, T), np.float32); sinm = np.zeros((128, T), np.float32)
    rmm = np.zeros((128, 128), np.float32)
    for i in range(16):
        cosm[64 + i] = cos16[i]; cosm[80 + i] = cos16[i]
        sinm[64 + i] = -sin16[i]; sinm[80 + i] = sin16[i]
        rmm[80 + i, 64 + i] = 1.0
        rmm[64 + i, 80 + i] = 1.0
    cst["c_cosm"] = cosm; cst["c_sinm"] = sinm; cst["c_rm"] = rmm.astype(bf)
    cst["c_cosk"] = np.ascontiguousarray(cosm[64:96]); cst["c_sink"] = np.ascontiguousarray(sinm[64:96])
    cst["c_rk"] = np.ascontiguousarray(rmm[64:96, 64:96]).astype(bf)
    qc = np.arange(64)
    ws = np.clip(qc - 8, 0, 48)
    kc_ = np.arange(64)
    valid = (kc_[:, None] >= ws[None, :]) & (kc_[:, None] < ws[None, :] + 16)
    m = np.where(valid, 0.0, NEG).astype(np.float32)
    m2 = np.concatenate([m, m], axis=0)
    cst["namask"] = np.ascontiguousarray(np.broadcast_to(m2[:, None, :], (128, 14, 64))).reshape(128, 14 * 64)
    return cst


_CACHE = {}


def _prep_shared(inp):
    f = lambda a: np.ascontiguousarray(np.asarray(a, dtype=np.float32))
    sh = {}
    sh["ln_in_g"] = f(inp["ln_in_g"]).reshape(1, D)
    sh["ln_in_b"] = f(inp["ln_in_b"]).reshape(1, D)
    sh["w_in"] = f(inp["w_in"])
    sh["bgate"] = np.ascontiguousarray(f(inp["b_gate"]).reshape(NL, 24, 128).transpose(0, 2, 1))
    rpb = f(inp["na_rpb"])
    kcol = np.arange(64)[:, None]; qcol = np.arange(64)[None, :]
    idx = np.clip(kcol - qcol, -15, 15) + 15
    sh["nabias"] = np.ascontiguousarray(rpb[:, :, :, idx]).reshape(NL, 8, 960, 64)
    sh["dlam"] = f(inp["diff_lambda"]).reshape(NL, 1, 256)
    sh["subg"] = f(inp["diff_subln_g"]).reshape(NL, 128, 1)
    sh["gq"] = np.ascontiguousarray(f(inp["mla_q_norm_g"]).reshape(NL, 3, 128).transpose(0, 2, 1))
    sh["gkv"] = np.ascontiguousarray(f(inp["mla_kv_norm_g"]).reshape(NL, 2, 128).transpose(0, 2, 1))
    sh["w_qb"] = f(inp["w_mla_qb"])
    sh["w_kvb"] = f(inp["w_mla_kvb"])
    sh["w_br"] = f(inp["w_branch"])
    sh["w_out"] = f(inp["w_out"])
    sh["ln1_g"] = f(inp["ln1_g"]); sh["ln1_b"] = f(inp["ln1_b"])
    sh["w_fi"] = f(inp["w_ffn_in"]); sh["w_fo"] = f(inp["w_ffn_out"])
    sh["ln2_g"] = f(inp["ln2_g"]); sh["ln2_b"] = f(inp["ln2_b"])
    sh.update(_consts())
    return sh


def kernel(**inputs):
    x = np.ascontiguousarray(np.asarray(inputs["x"], dtype=np.float32))
    nb = x.shape[0]
    sh = _prep_shared(inputs)
    if "nc" not in _CACHE:
        _CACHE["nc"] = build()
    nc = _CACHE["nc"]
    in_maps = []
    for b in range(nb):
        m = dict(sh)
        m["x"] = x[b]
        in_maps.append(m)
    res = run_bass_kernel_spmd(nc, in_maps, core_ids=list(range(nb)))
    return np.stack([np.asarray(r["out"], dtype=np.float32) for r in res.results], axis=0)
```

```python
import math
import os
import numpy as np
import ml_dtypes
import concourse.bass as bass
import concourse.mybir as mybir
from concourse.bass_utils import run_bass_kernel_spmd

F32 = mybir.dt.float32
BF16 = mybir.dt.bfloat16
AF = mybir.ActivationFunctionType
ALU = mybir.AluOpType
AX = mybir.AxisListType

T = 4096
D = 1024
KC = 8
NL = 4
DIN = 6816
DFF = 2816
FC = 22
ALPHA = (2 * NL) ** 0.25
LN_EPS = 1e-5
RMS_EPS = 1e-6
ROPE_THETA = 500000.0
NEG = -30000.0
DBGB = os.environ.get('DBG_B', 'qkvr')


class Buf:
    __slots__ = ("name", "w", "r", "lsem", "ssem", "last_s")

    def __init__(self, name):
        self.name = name
        self.w = None
        self.r = {}
        self.lsem = None
        self.ssem = None
        self.last_s = None


class Prog:
    CE = ("pe", "act", "dve", "pool")

    def __init__(self, nc, sems):
        self.nc = nc
        self.sems = sems
        self.q = {e: [] for e in ("pe", "act", "dve", "pool", "sp")}
        self.esi = {e: i for i, e in enumerate(self.CE)}
        self.bar_si = 4
        self.free = list(range(5, len(sems)))
        self.ecnt = {e: 0 for e in self.CE}
        self.barcnt = 0
        self.semval = {i: 0 for i in range(len(sems))}
        self.waited = {e: {} for e in self.q}
        self.phase_dma = {}
        self.sem_bufs = []
        self.ninst = 0

    def wait(self, eng, ev):
        if ev is None:
            return
        si, val = ev
        if self.waited[eng].get(si, 0) >= val:
            return
        self.waited[eng][si] = val
        sem = self.sems[si]
        self.q[eng].append(("wait_ge", (sem, val), {}))

    def _deps(self, eng, reads, writes):
        deps = {}

        def add(ev):
            if ev is None:
                return
            si, val = ev
            if deps.get(si, 0) < val:
                deps[si] = val

        for b in reads:
            add(b.w)
        for b in writes:
            add(b.w)
            for si, val in b.r.items():
                add((si, val))
        for si, val in deps.items():
            if eng == "pe" and si == self.esi["pe"]:
                continue
            self.wait(eng, (si, val))

    def op(self, eng, fns, reads=(), writes=()):
        self._deps(eng, reads, writes)
        si = self.esi[eng]
        self.ecnt[eng] += 1
        val = self.ecnt[eng]
        ev = (si, val)
        if isinstance(fns, tuple):
            fns = [fns]
        for f in fns[:-1]:
            self.q[eng].append(f)
        last = fns[-1]
        sem = self.sems[si]
        self.q[eng].append((last[0], last[1], last[2], sem, 1))
        self.ninst += len(fns)
        for b in reads:
            if b.r.get(si, 0) < val:
                b.r[si] = val
        for b in writes:
            b.w = ev
            b.r = {}
        return ev

    def _bufsem(self, b, kind):
        cur = b.lsem if kind == "l" else b.ssem
        if cur is None:
            cur = self.free.pop()
            if kind == "l":
                b.lsem = cur
            else:
                b.ssem = cur
            self.sem_bufs.append(b)
        return cur

    def dma(self, pairs, reads=(), writes=(), owner=None, kind="l", eng="sp"):
        self._deps(eng, reads, writes)
        if kind == "s" and owner.last_s is not None:
            self.wait(eng, owner.last_s)
        si = self._bufsem(owner, kind)
        self.semval[si] += 16 * len(pairs)
        val = self.semval[si]
        assert val < 60000, "semaphore value too large"
        ev = (si, val)
        sem = self.sems[si]
        for (o, i) in pairs:
            self.q[eng].append(("dma_start", (), dict(out=o, in_=i), sem, 16))
        self.ninst += len(pairs)
        for b in reads:
            if b.r.get(si, 0) < val:
                b.r[si] = val
        for b in writes:
            b.w = ev
            b.r = {}
        if kind == "s":
            owner.last_s = ev
        self.phase_dma[si] = val
        return ev

    def barrier(self):
        for e in self.CE:
            if self.ecnt[e] > 0:
                self.wait("sp", (self.esi[e], self.ecnt[e]))
        for si, val in self.phase_dma.items():
            self.wait("sp", (si, val))
        self.barcnt += 1
        n = self.barcnt
        sem = self.sems[self.bar_si]
        self.q["sp"].append(("sem_inc", (sem, 1), {}))
        for e in self.CE:
            self.wait(e, (self.bar_si, n))
        for e in self.q:
            for x in self.CE:
                self.waited[e][self.esi[x]] = self.ecnt[x]
            for si, val in self.semval.items():
                if si > self.bar_si:
                    self.waited[e][si] = val
        self.phase_dma = {}
        for b in self.sem_bufs:
            if b.lsem is not None:
                self.free.append(b.lsem)
                b.lsem = None
            if b.ssem is not None:
                self.free.append(b.ssem)
                b.ssem = None
        self.sem_bufs = []

    def emit(self, block):
        q = self.q

        def run(e, lst):
            for it in lst:
                ins = getattr(e, it[0])(*it[1], **it[2])
                if len(it) == 5:
                    ins.then_inc(it[3], it[4])

        @block.tensor
        def _(e):
            run(e, q["pe"])

        @block.scalar
        def _(e):
            run(e, q["act"])

        @block.vector
        def _(e):
            run(e, q["dve"])

        @block.gpsimd
        def _(e):
            run(e, q["pool"])

        @block.sync
        def _(e):
            run(e, q["sp"])


def I(name, *a, **k):
    return (name, a, k)


class Carver:
    def __init__(self, t, n):
        self.t = t
        self.n = n
        self.off = 0

    def reset(self):
        self.off = 0

    def take(self, n):
        assert self.off + n <= self.n, ("sbuf pool overflow", self.off, n, self.n)
        a = self.t[:, self.off:self.off + n]
        self.off += n
        return a


def build(n_layers=NL, debug=(), stop_after=None):
    nc = bass.Bass("TRN2", target_bir_lowering=False)

    def din(name, shape, dt=F32):
        return nc.dram_tensor(name, list(shape), dt, kind="ExternalInput").ap()

    def scr(name, shape, dt):
        kind = "ExternalOutput" if name in debug else "Internal"
        return nc.dram_tensor(name, list(shape), dt, kind=kind).ap()

    x_in = din("x", [T, D])
    ln_in_g = din("ln_in_g", [1, D])
    ln_in_b = din("ln_in_b", [1, D])
    w_in = din("w_in", [NL, D, DIN])
    bgate = din("bgate", [NL, 128, 24])
    nabias = din("nabias", [NL, 8, 960, 64])
    namask = din("namask", [128, 14 * 64])
    dlam = din("dlam", [NL, 1, 256])
    subg = din("subg", [NL, 128, 1])
    gq_in = din("gq", [NL, 128, 3])
    gkv_in = din("gkv", [NL, 128, 2])
    w_qb = din("w_qb", [NL, 384, 768])
    w_kvb = din("w_kvb", [NL, 256, 1024])
    w_br = din("w_br", [NL, 3, 512, D])
    w_out = din("w_out", [NL, D, D])
    ln1_g = din("ln1_g", [NL, D])
    ln1_b = din("ln1_b", [NL, D])
    w_fi = din("w_fi", [NL, D, 2 * DFF])
    w_fo = din("w_fo", [NL, DFF, D])
    ln2_g = din("ln2_g", [NL, D])
    ln2_b = din("ln2_b", [NL, D])
    c_ident = din("c_ident", [128, 128], BF16)
    c_onesb = din("c_onesb", [128, 128], BF16)
    c_onesf = din("c_onesf", [128, 128])
    c_rd = din("c_rd", [128, 128], BF16)
    c_rm = din("c_rm", [128, 128], BF16)
    c_rk = din("c_rk", [32, 32], BF16)
    c_cosd = din("c_cosd", [128, T])
    c_sind = din("c_sind", [128, T])
    c_cosm = din("c_cosm", [128, T])
    c_sinm = din("c_sinm", [128, T])
    c_cosk = din("c_cosk", [32, T])
    c_sink = din("c_sink", [32, T])
    out = nc.dram_tensor("out", [T, D], F32, kind="ExternalOutput").ap()

    XTOK = scr("XTOK", [T, D], F32)
    NQ = scr("NQ", [4, 128, T], BF16)
    NK = scr("NK", [4, 128, T], BF16)
    NV = scr("NV", [T, 512], BF16)
    DQ = scr("DQ", [4, 128, T], BF16)
    DK = scr("DK", [4, 128, T], BF16)
    DV = scr("DV", [T, 512], BF16)
    MC = scr("MC", [6, 128, T], F32)
    GT = scr("GT", [24, 128, T], BF16)
    MQ = scr("MQ", [8, 128, T], BF16)
    MK = scr("MK", [8, 128, T], BF16)
    MV = scr("MV", [T, 512], BF16)
    AO = scr("AO", [3, 4, 128, T], BF16)
    HT = scr("HT", [FC, 128, T], BF16)

    NPB = 36864
    NPF = 10240
    xT = nc.alloc_sbuf_tensor("xT", [128, KC, T], BF16)
    PBt = nc.alloc_sbuf_tensor("PB", [128, NPB], BF16)
    PFt = nc.alloc_sbuf_tensor("PF", [128, NPF], F32)
    ident = nc.alloc_sbuf_tensor("ident", [128, 128], BF16)
    onesb = nc.alloc_sbuf_tensor("onesb", [128, 128], BF16)
    onesf = nc.alloc_sbuf_tensor("onesf", [128, 128], F32)
    rd = nc.alloc_sbuf_tensor("rd", [128, 128], BF16)
    rm = nc.alloc_sbuf_tensor("rm", [128, 128], BF16)
    rk = nc.alloc_sbuf_tensor("rk", [32, 32], BF16)
    bg_t = nc.alloc_sbuf_tensor("bg_t", [128, NL * 24], F32)
    gq_t = nc.alloc_sbuf_tensor("gq_t", [128, NL * 3], F32)
    gkv_t = nc.alloc_sbuf_tensor("gkv_t", [128, NL * 2], F32)
    gsc_t = nc.alloc_sbuf_tensor("gsc_t", [128, NL], F32)
    nlam_t = nc.alloc_sbuf_tensor("nlam_t", [128, NL], F32)
    lamw = nc.alloc_sbuf_tensor("lamw", [128, NL * 256], F32)
    lamp = nc.alloc_sbuf_tensor("lamp", [128, NL * 128], F32)
    lams = nc.alloc_sbuf_tensor("lams", [128, NL * 2], F32)
    lame = nc.alloc_sbuf_tensor("lame", [128, NL * 2], F32)
    lnst = nc.alloc_sbuf_tensor("lnst", [128, 4 * 16], F32)
    epsln = nc.alloc_sbuf_tensor("epsln", [128, 1], F32)
    epsrms = nc.alloc_sbuf_tensor("epsrms", [128, 1], F32)
    PSt = [nc.alloc_psum_tensor("ps%d" % i, [128, 1024], F32) for i in range(4)]

    def bank(i):
        return PSt[i // 2][:, (i % 2) * 512:(i % 2) * 512 + 512]

    PB = Carver(PBt, NPB)
    PF = Carver(PFt, NPF)

    import contextlib
    with contextlib.ExitStack() as es:
        sems = [es.enter_context(nc.semaphore("s%d" % i)) for i in range(100)]
        block = es.enter_context(nc.Block())
        P = Prog(nc, sems)

        def wload(src, dst_ap, dst_buf, stage):
            a, n = src.shape[1], src.shape[2]
            per = max(1, 2048 // n)
            i = 0
            k = wload.k
            while i < a:
                j = min(a, i + per)
                sap, sbuf = stage[k % len(stage)]
                k += 1
                sv = sap[:, 0:(j - i) * n].rearrange("p (a n) -> p a n", n=n)
                P.dma([(sv, src[:, i:j, :])], writes=[sbuf], owner=sbuf)
                P.op("pool", I("tensor_copy", dst_ap[:, i:j, :], sv), reads=[sbuf], writes=[dst_buf])
                i = j
            wload.k = k

        wload.k = 0

        def rope_epi(ps_ap, ps_buf, rows, cos_ap, sin_ap, cs_buf, rmat, ps2_ap, ps2_buf, xb, t1, t2, stg_ap, stg_buf):
            xb_ap, xb_buf = xb
            t1_ap, t1_buf = t1
            t2_ap, t2_buf = t2
            nst = int(os.environ.get('DBG_RSTEPS', '5'))
            P.op("act", I("copy", xb_ap[0:rows, :], ps_ap[0:rows, :]), reads=[ps_buf], writes=[xb_buf])
            if nst < 2: return
            P.op("pe", I("matmul", ps2_ap[0:rows, :], lhsT=rmat[0:rows, 0:rows], rhs=xb_ap[0:rows, :], start=True, stop=True),
                 reads=[xb_buf], writes=[ps2_buf])
            if nst < 3: return
            v = os.environ.get('DBG_T1', '')
            if v == 'xb':
                P.op("dve", I("tensor_tensor", t1_ap[0:rows, :], xb_ap[0:rows, :], cos_ap[0:rows, :], ALU.mult),
                     reads=[xb_buf, cs_buf], writes=[t1_buf])
            elif v == 'sin':
                P.op("dve", I("tensor_tensor", t1_ap[0:rows, :], ps_ap[0:rows, :], sin_ap[0:rows, :], ALU.mult),
                     reads=[ps_buf, cs_buf], writes=[t1_buf])
            elif v == 't2':
                P.op("dve", I("tensor_tensor", t2_ap[0:rows, :], ps_ap[0:rows, :], cos_ap[0:rows, :], ALU.mult),
                     reads=[ps_buf, cs_buf], writes=[t2_buf])
            else:
                P.op("dve", I("tensor_tensor", t1_ap[0:rows, :], ps_ap[0:rows, :], cos_ap[0:rows, :], ALU.mult),
                     reads=[ps_buf, cs_buf, xb_buf], writes=[t1_buf])
            if nst < 4: return
            P.op("dve", I("tensor_tensor", t2_ap[0:rows, :], ps2_ap[0:rows, :], sin_ap[0:rows, :], ALU.mult),
                 reads=[ps2_buf, cs_buf], writes=[t2_buf])
            if nst < 5: return
            P.op("pool", I("tensor_tensor", stg_ap[0:rows, :], t1_ap[0:rows, :], t2_ap[0:rows, :], ALU.add),
                 reads=[t1_buf, t2_buf], writes=[stg_buf])

        class LNCtx:
            pass

        def ln_setup(g_src, b_src, need_y=True):
            c = LNCtx()
            c.g = PF.take(1024)
            c.b = PF.take(1024)
            c.gb = Buf("lngb")
            P.dma([(c.g, g_src.partition_broadcast(128)[:, 0, :]), (c.b, b_src.partition_broadcast(128)[:, 0, :])],
                  writes=[c.gb], owner=c.gb)
            c.y = [(PF.take(1024), Buf("lny%d" % i)) for i in range(2)] if need_y else None
            c.xo = [(PF.take(1024), Buf("lnxo%d" % i)) for i in range(2)]
            c.xb = [(PB.take(1024), Buf("lnxb%d" % i)) for i in range(2)]
            c.st = [(lnst[:, i * 16:(i + 1) * 16], Buf("lnst%d" % i)) for i in range(2)]
            c.k = 0
            return c

        def ln_tile(c, y_ap, y_buf, t, psT_ap, psT_buf, final=False):
            k = c.k
            c.k += 1
            st_ap, st_buf = c.st[k % 2]
            xo_ap, xo_buf = c.xo[k % 2]
            xb_ap, xb_buf = c.xb[k % 2]
            P.op("dve", [I("bn_stats", st_ap[:, 0:6], y_ap[:, 0:512]),
                         I("bn_stats", st_ap[:, 6:12], y_ap[:, 512:1024])], reads=[y_buf], writes=[st_buf])
            P.op("dve", I("bn_aggr", st_ap[:, 12:14], st_ap[:, 0:12]), reads=[st_buf], writes=[st_buf])
            P.op("act", I("activation", st_ap[:, 14:15], st_ap[:, 13:14], AF.Sqrt, bias=epsln[:, 0:1], scale=1.0), reads=[st_buf, cbuf], writes=[st_buf])
            P.op("dve", I("reciprocal", st_ap[:, 14:15], st_ap[:, 14:15]), reads=[st_buf], writes=[st_buf])
            P.op("dve", I("tensor_scalar", y_ap, y_ap, st_ap[:, 12:13], st_ap[:, 14:15], op0=ALU.subtract, op1=ALU.mult),
                 reads=[st_buf, y_buf], writes=[y_buf])
            P.op("pool", I("tensor_tensor", xo_ap, y_ap, c.g, ALU.mult), reads=[y_buf, c.gb], writes=[xo_buf])
            P.op("pool", I("tensor_tensor", xo_ap, xo_ap, c.b, ALU.add), reads=[xo_buf, c.gb], writes=[xo_buf])
            pairs = [(XTOK[t * 128:(t + 1) * 128, :], xo_ap)]
            if final:
                pairs = [(out[t * 128:(t + 1) * 128, :], xo_ap)]
            P.dma(pairs, reads=[xo_buf], owner=xo_buf, kind="s")
            if final:
                return
            P.op("act", I("copy", xb_ap, xo_ap), reads=[xo_buf], writes=[xb_buf])
            psb = psT_ap.bitcast(BF16)
            P.op("pe", [I("transpose", psb[:, kc * 128:(kc + 1) * 128], xb_ap[:, kc * 128:(kc + 1) * 128], ident[:, :])
                        for kc in range(KC)], reads=[xb_buf, cbuf], writes=[psT_buf])
            P.op("act", I("copy", xT[:, :, t * 128:(t + 1) * 128], psb.rearrange("p (a n) -> p a n", n=128)),
                 reads=[psT_buf], writes=[])

        cbuf = Buf("consts")
        P.dma([(ident[:, :], c_ident), (onesb[:, :], c_onesb), (onesf[:, :], c_onesf), (rd[:, :], c_rd), (rm[:, :], c_rm),
               (rk[:, :], c_rk)], writes=[cbuf], owner=cbuf)
        P.op("pool", [I("memset", epsln[:, :], LN_EPS), I("memset", epsrms[:, :], RMS_EPS)], writes=[cbuf])
        sbuf_ = Buf("smalls")
        pairs = []
        for l in range(NL):
            pairs += [(bg_t[:, l * 24:(l + 1) * 24], bgate[l]), (gq_t[:, l * 3:(l + 1) * 3], gq_in[l]),
                      (gkv_t[:, l * 2:(l + 1) * 2], gkv_in[l]), (gsc_t[:, l:l + 1], subg[l]),
                      (lamw[:, l * 256:(l + 1) * 256], dlam[l].partition_broadcast(128)[:, 0, :])]
        P.dma(pairs, writes=[sbuf_], owner=sbuf_)
        lb = Buf("lam")
        for l in range(NL):
            lam_init = 0.8 - 0.6 * math.exp(-0.3 * l)
            lw = lamw[:, l * 256:(l + 1) * 256].rearrange("p (a b) -> p a b", b=64)
            lp = lamp[:, l * 128:(l + 1) * 128].rearrange("p (a b) -> p a b", b=64)
            P.op("dve", I("tensor_tensor", lp, lw[:, 0:4:2, :], lw[:, 1:4:2, :], ALU.mult), reads=[sbuf_], writes=[lb])
            P.op("dve", I("reduce_sum", lams[:, l * 2:(l + 1) * 2], lp, AX.X), reads=[lb], writes=[lb])
            P.op("act", I("activation", lame[:, l * 2:(l + 1) * 2], lams[:, l * 2:(l + 1) * 2], AF.Exp), reads=[lb], writes=[lb])
            P.op("dve", I("tensor_tensor", nlam_t[:, l:l + 1], lame[:, l * 2 + 1:l * 2 + 2], lame[:, l * 2:l * 2 + 1], ALU.subtract),
                 reads=[lb], writes=[lb])
            P.op("dve", I("tensor_scalar", nlam_t[:, l:l + 1], nlam_t[:, l:l + 1], -lam_init, None, op0=ALU.add),
                 reads=[lb], writes=[lb])
            P.op("dve", I("tensor_scalar", gsc_t[:, l:l + 1], gsc_t[:, l:l + 1], 1.0 - lam_init, None, op0=ALU.mult),
                 reads=[lb, sbuf_], writes=[lb])

        PB.reset(); PF.reset()
        c = ln_setup(ln_in_g[0:1, :], ln_in_b[0:1, :])
        psT = [(bank(i), Buf("psT%d" % i)) for i in range(2)]
        for t in range(T // 128):
            y_ap, y_buf = c.y[t % 2]
            P.dma([(y_ap, x_in[t * 128:(t + 1) * 128, :])], writes=[y_buf], owner=y_buf)
            ln_tile(c, y_ap, y_buf, t, psT[t % 2][0], psT[t % 2][1])
        P.barrier()

        for l in range(n_layers):
            if stop_after == "0":
                break
            last_layer = (l == NL - 1)
            PB.reset(); PF.reset()
            wstage = [(PF.take(2048), Buf("wstg%d" % i)) for i in range(2)]
            wst = [(PB.take(2048).rearrange("p (a n) -> p a n", n=256), Buf("wst%d" % i)) for i in range(3)]
            stg = [(PB.take(512), Buf("stg%d" % i)) for i in range(4)]
            stgf = [(PF.take(512), Buf("stgf%d" % i)) for i in range(2)]
            xbr = [(PB.take(512), Buf("xbr%d" % i)) for i in range(2)]
            t1s = [(PF.take(512), Buf("t1s%d" % i)) for i in range(2)]
            t2s = [(PF.take(512), Buf("t2s%d" % i)) for i in range(2)]
            css = [(PF.take(1024), Buf("css%d" % i)) for i in range(2)]
            psA = [(bank(i), Buf("psA%d" % i)) for i in range(6)]
            ps2 = [(bank(6 + i), Buf("ps2%d" % i)) for i in range(2)]
            cnt = dict(item=0, stg=0, stgf=0, rope=0)

            strips = []
            for e0 in range(0, 1024, 256):
                strips.append(("fm", e0))
            strips.append(("v", 1024)); strips.append(("v", 1280))
            for e0 in range(1536, 2560, 256):
                strips.append(("fm", e0))
            strips.append(("v", 2560)); strips.append(("v", 2816))
            for e0 in range(3072, 3584, 256):
                strips.append(("fm", e0))
            strips.append(("fm", 3584))
            for e0 in range(3744, DIN, 256):
                strips.append(("gate", e0))

            def chunk_dst(e0, wd):
                if e0 < 512:
                    return "plain", NQ[e0 // 128]
                if e0 < 1024:
                    return "plain", NK[(e0 - 512) // 128]
                if 1536 <= e0 < 2048:
                    return "rope", DQ[(e0 - 1536) // 128]
                if 2048 <= e0 < 2560:
                    return "rope", DK[(e0 - 2048) // 128]
                if 3072 <= e0 < 3712:
                    return "f32", MC[(e0 - 3072) // 128]
                if e0 == 3712:
                    return "f32", MC[5]
                if e0 >= 3744:
                    return "gate", GT[(e0 - 3744) // 128]
                raise AssertionError(e0)

            def load_strip(si_):
                kind, e0 = strips[si_]
                ncol = 256
                if kind == "fm" and e0 == 3584:
                    ncol = 160
                w_ap, w_buf = wst[si_ % 3]
                src = w_in[l][:, e0:e0 + ncol].rearrange("(kc p) e -> p kc e", p=128)
                wload(src, w_ap[:, :, 0:ncol], w_buf, wstage)

            load_strip(0)
            load_strip(1)
            for si_, (kind, e0) in enumerate(strips):
                if si_ + 2 < len(strips):
                    load_strip(si_ + 2)
                w_ap, w_buf = wst[si_ % 3]
                if kind == "v":
                    vdst = NV if e0 < 1536 else DV
                    c0 = (e0 - 1024) if e0 < 1536 else (e0 - 2560)
                    for t in range(T // 128):
                        ps_ap, ps_buf = psA[cnt["item"] % 6]
                        cnt["item"] += 1
                        P.op("pe", [I("matmul",
                            ps_ap[:, 0:256], lhsT=xT[:, kc, t * 128:(t + 1) * 128], rhs=w_ap[:, kc, :], start=(kc == 0), stop=(kc == KC - 1))
                            for kc in range(KC)], reads=[w_buf], writes=[ps_buf])
                        s_ap, s_buf = stg[cnt["stg"] % 4]
                        cnt["stg"] += 1
                        P.op("act", I("copy", s_ap[:, 0:256], ps_ap[:, 0:256]), reads=[ps_buf], writes=[s_buf])
                        P.dma([(vdst[t * 128:(t + 1) * 128, c0:c0 + 256], s_ap[:, 0:256])], reads=[s_buf], owner=s_buf, kind="s")
                    continue
                if kind == "fm" and e0 == 3584:
                    chunks = [(3584, 128, 0), (3712, 32, 128)]
                else:
                    chunks = [(e0, 128, 0), (e0 + 128, 128, 128)]
                for (ce0, wd, coff) in chunks:
                    ckind, dst = chunk_dst(ce0, wd)
                    for tb in range(8):
                        ps_ap, ps_buf = psA[cnt["item"] % 6]
                        cnt["item"] += 1
                        P.op("pe", [I("matmul",
                            ps_ap[0:wd, :], lhsT=w_ap[:, kc, coff:coff + wd], rhs=xT[:, kc, tb * 512:(tb + 1) * 512],
                            start=(kc == 0), stop=(kc == KC - 1)) for kc in range(KC)], reads=[w_buf], writes=[ps_buf])
                        dsl = dst[0:wd, tb * 512:(tb + 1) * 512]
                        if ckind == "plain":
                            s_ap, s_buf = stg[cnt["stg"] % 4]; cnt["stg"] += 1
                            P.op("act", I("copy", s_ap, ps_ap), reads=[ps_buf], writes=[s_buf])
                            P.dma([(dsl, s_ap)], reads=[s_buf], owner=s_buf, kind="s")
                        elif ckind == "gate":
                            gi = (ce0 - 3744) // 128
                            s_ap, s_buf = stg[cnt["stg"] % 4]; cnt["stg"] += 1
                            P.op("act", I("activation",
                                s_ap, ps_ap, AF.Sigmoid, bias=bg_t[:, l * 24 + gi:l * 24 + gi + 1]), reads=[ps_buf], writes=[s_buf])
                            P.dma([(dsl, s_ap)], reads=[s_buf], owner=s_buf, kind="s")
                        elif ckind == "f32":
                            s_ap, s_buf = stgf[cnt["stgf"] % 2]; cnt["stgf"] += 1
                            P.op("act", I("copy", s_ap[0:wd, :], ps_ap[0:wd, :]), reads=[ps_buf], writes=[s_buf])
                            P.dma([(dsl, s_ap[0:wd, :])], reads=[s_buf], owner=s_buf, kind="s")
                        else:
                            k = cnt["rope"]; cnt["rope"] += 1
                            cs_ap, cs_buf = css[k % 2]
                            P.dma([(cs_ap[:, 0:512], c_cosd[:, tb * 512:(tb + 1) * 512]), (cs_ap[:, 512:1024], c_sind[:, tb * 512:(tb + 1) * 512])],
                                  writes=[cs_buf], owner=cs_buf)
                            s_ap, s_buf = stg[cnt["stg"] % 4]; cnt["stg"] += 1
                            rope_epi(ps_ap, ps_buf, 128, cs_ap[:, 0:512], cs_ap[:, 512:1024], cs_buf, rd, ps2[k % 2][0], ps2[k % 2][1],
                                     xbr[k % 2], t1s[k % 2], t2s[k % 2], s_ap, s_buf)
                            P.dma([(dsl, s_ap)], reads=[s_buf], owner=s_buf, kind="s")
            P.barrier()
            if stop_after == "A":
                break

            PB.reset(); PF.reset()
            wstage = [(PF.take(2048), Buf("wstg%d" % i)) for i in range(1)]
            wqb = PB.take(3 * 800).rearrange("p (a n) -> p a n", n=800); wqb_buf = Buf("wqb")
            zt_ap = PB.take(512); zt_buf = Buf("zt")
            P.op("pool", [I("memset", wqb[:, :, 768:800], 0.0), I("memset", zt_ap, 0.0)], writes=[wqb_buf, zt_buf])
            wkvb = PB.take(2 * 1024).rearrange("p (a n) -> p a n", n=1024); wkvb_buf = Buf("wkvb")
            wload(w_qb[l].rearrange("(kc p) e -> p kc e", p=128), wqb[:, :, 0:768], wqb_buf, wstage)
            wload(w_kvb[l].rearrange("(kc p) e -> p kc e", p=128), wkvb, wkvb_buf, wstage)
            wkvb_v = wkvb.rearrange("p a (h two d) -> p a h two d", two=2, d=64)
            cq = [(PF.take(5 * 512).rearrange("p (a n) -> p a n", n=512), Buf("cq%d" % i)) for i in range(1)]
            krs = [(PF.take(512), Buf("krs%d" % i)) for i in range(1)]
            sq = (PF.take(512), Buf("sq"))
            rstd = [(PF.take(512), Buf("rstd%d" % i)) for i in range(2)]
            cn = (PB.take(5 * 512).rearrange("p (a n) -> p a n", n=512), Buf("cn"))
            hlb = (PB.take(1024).rearrange("p (a n) -> p a n", n=512), Buf("hlb"))
            css = [(PF.take(1024), Buf("cssm%d" % i)) for i in range(1)]
            csk = [(PF.take(1024), Buf("csk%d" % i)) for i in range(1)]
            xbr = [(PB.take(512), Buf("xbr%d" % i)) for i in range(2)]
            t1s = [(PF.take(512), Buf("t1s%d" % i)) for i in range(1)]
            t2s = [(PF.take(512), Buf("t2s%d" % i)) for i in range(1)]
            stg = [(PB.take(512), Buf("stg%d" % i)) for i in range(4)]
            psA = [(bank(i), Buf("psA%d" % i)) for i in range(4)]
            ps2 = [(bank(4 + i), Buf("ps2%d" % i)) for i in range(2)]
            psS = [(bank(6 + i), Buf("psS%d" % i)) for i in range(2)]
            cnt = dict(item=0, stg=0, rope=0)
            for tb in range(int(os.environ.get('DBG_BTB', '8'))):
                tsl = slice(tb * 512, (tb + 1) * 512)
                cq_ap, cq_buf = cq[0]
                kr_ap, kr_buf = krs[0]
                P.dma([(cq_ap, MC[0:5, :, tsl].rearrange("c p t -> p c t"))], writes=[cq_buf], owner=cq_buf)
                P.dma([(kr_ap[0:32, :], MC[5, 0:32, tsl])], writes=[kr_buf], owner=kr_buf)
                cs_ap, cs_buf = css[0]
                P.dma([(cs_ap[:, 0:512], c_cosm[:, tsl]), (cs_ap[:, 512:1024], c_sinm[:, tsl])], writes=[cs_buf], owner=cs_buf)
                ck_ap, ck_buf = csk[0]
                P.dma([(ck_ap[0:32, 0:512], c_cosk[:, tsl]), (ck_ap[0:32, 512:1024], c_sink[:, tsl])], writes=[ck_buf], owner=ck_buf)
                cn_ap, cn_buf = cn
                hl_ap, hl_buf = hlb
                for (c0, ncn, gt, goff, nfeat) in ((0, 3, gq_t, l * 3, 384.0), (3, 2, gkv_t, l * 2, 256.0)):
                    pS_ap, pS_buf = psS[0 if c0 == 0 else 1]
                    sq_ap, sq_buf = sq
                    for ci in range(ncn):
                        P.op("act", I("activation", sq_ap, cq_ap[:, c0 + ci, :], AF.Square), reads=[cq_buf], writes=[sq_buf])
                        P.op("pool", I("tensor_copy", hl_ap[:, 0, :], sq_ap), reads=[sq_buf], writes=[hl_buf])
                        P.op("pool", I("tensor_tensor", hl_ap[:, 1, :], sq_ap, hl_ap[:, 0, :], ALU.subtract), reads=[sq_buf, hl_buf], writes=[hl_buf])
                        P.op("pe", [I("matmul", pS_ap, lhsT=onesb[:, :], rhs=hl_ap[:, 0, :], start=(ci == 0), stop=False),
                                    I("matmul", pS_ap, lhsT=onesb[:, :], rhs=hl_ap[:, 1, :], start=False, stop=(ci == ncn - 1))],
                             reads=[hl_buf], writes=[pS_buf])
                    r_ap, r_buf = rstd[0 if c0 == 0 else 1]
                    P.op("act", I("activation", r_ap, pS_ap, AF.Sqrt, bias=epsrms[:, 0:1], scale=1.0 / nfeat), reads=[pS_buf], writes=[r_buf])
                    P.op("dve", I("reciprocal", r_ap, r_ap), reads=[r_buf], writes=[r_buf])
                    for ci in range(ncn):
                        P.op("dve", I("scalar_tensor_tensor",
                            cn_ap[:, c0 + ci, :], cq_ap[:, c0 + ci, :], gt[:, goff + ci:goff + ci + 1], r_ap, op0=ALU.mult, op1=ALU.mult),
                            reads=[cq_buf, r_buf], writes=[cn_buf])
                for h in range(int(os.environ.get('DBG_QH', '8')) if 'q' in DBGB else 0):
                    ps_ap, ps_buf = psA[cnt["item"] % 4]; cnt["item"] += 1
                    P.op("pe", [I("matmul", ps_ap, lhsT=wqb[:, ci, h * 96:h * 96 + 128], rhs=cn_ap[:, ci, :],
                                                                        start=(ci == 0), stop=(ci == 2)) for ci in range(3)],
                         reads=[cn_buf, wqb_buf], writes=[ps_buf])
                    k = cnt["rope"]; cnt["rope"] += 1
                    s_ap, s_buf = stg[cnt["stg"] % 4]; cnt["stg"] += 1
                    if os.environ.get('DBG_NOROPE'):
                        P.op("act", I("copy", s_ap, ps_ap), reads=[ps_buf], writes=[s_buf])
                    else:
                        rope_epi(ps_ap, ps_buf, int(os.environ.get('DBG_ROWS', '128')), cs_ap[:, 0:512], cs_ap[:, 512:1024], cs_buf, (rd if os.environ.get('DBG_RD') else rm), ps2[k % 2][0], ps2[k % 2][1],
                                 xbr[k % 2], t1s[0], t2s[0], s_ap, s_buf)
                    P.dma([(MQ[h, :, tsl], s_ap)], reads=[s_buf], owner=s_buf, kind="s")
                for h in range(8 if 'k' in DBGB else 0):
                    ps_ap, ps_buf = psA[cnt["item"] % 4]; cnt["item"] += 1
                    P.op("pe", [I("matmul", ps_ap[0:64, :], lhsT=wkvb[:, ci, h * 128:h * 128 + 64], rhs=cn_ap[:, 3 + ci, :],
                                                                        start=(ci == 0), stop=(ci == 1)) for ci in range(2)],
                         reads=[cn_buf, wkvb_buf], writes=[ps_buf])
                    s_ap, s_buf = stg[cnt["stg"] % 4]; cnt["stg"] += 1
                    P.op("act", I("copy", s_ap[0:64, :], ps_ap[0:64, :]), reads=[ps_buf], writes=[s_buf])
                    P.dma([(MK[h, 0:64, tsl], s_ap[0:64, :])], reads=[s_buf], owner=s_buf, kind="s")
                for s in range(4 if 'v' in DBGB else 0):
                    ps_ap, ps_buf = psA[cnt["item"] % 4]; cnt["item"] += 1
                    P.op("pe", [I("matmul", ps_ap.rearrange("p (h d) -> p h d", d=64), lhsT=cn_ap[:, 3 + ci, s * 128:(s + 1) * 128],
                                                                        rhs=wkvb_v[:, ci, :, 1, :], start=(ci == 0), stop=(ci == 1)) for ci in range(2)],
                         reads=[cn_buf, wkvb_buf], writes=[ps_buf])
                    s_ap, s_buf = stg[cnt["stg"] % 4]; cnt["stg"] += 1
                    P.op("act", I("copy", s_ap, ps_ap), reads=[ps_buf], writes=[s_buf])
                    tt = tb * 4 + s
                    P.dma([(MV[tt * 128:(tt + 1) * 128, :], s_ap)], reads=[s_buf], owner=s_buf, kind="s")
                if 'r' not in DBGB:
                    continue
                k = cnt["rope"]; cnt["rope"] += 1
                s_ap, s_buf = stg[cnt["stg"] % 4]; cnt["stg"] += 1
                rope_epi(kr_ap, kr_buf, 32, ck_ap[:, 0:512], ck_ap[:, 512:1024], ck_buf, rk, ps2[k % 2][0], ps2[k % 2][1],
                         xbr[k % 2], t1s[0], t2s[0], s_ap, s_buf)
                P.dma([(MK[h, 64:96, tsl], s_ap[0:32, :]) for h in range(8)], reads=[s_buf], owner=s_buf, kind="s")
                P.dma([(MK[h, 96:128, tsl], zt_ap[0:32, :]) for h in range(8)], reads=[zt_buf], owner=zt_buf, kind="s")
            P.barrier()
            if stop_after == "B":
                break

            PB.reset(); PF.reset()
            hq = [(PB.take(T), Buf("hq%d" % i)) for i in range(2)]
            hk = [(PB.take(T), Buf("hk%d" % i)) for i in range(2)]
            hv = [(PB.take(32 * 128).rearrange("p (a n) -> p a n", n=128), Buf("hv%d" % i)) for i in range(2)]
            pts = [(PB.take(1024).rearrange("p (a n) -> p a n", n=512), Buf("pt%d" % i)) for i in range(4)]
            stg = [(PB.take(512), Buf("stg%d" % i)) for i in range(2)]
            pss = [(PSt[i][:, :].rearrange("p (a n) -> p a n", n=512), Buf("pss%d" % i)) for i in range(3)]
            accO = [(bank(6), Buf("accO0"))]
            accS = [(bank(7), Buf("accS0"))]
            pend = []
            rr = [(PF.take(512), Buf("rr%d" % i)) for i in range(2)]
            a1 = (PF.take(512), Buf("a1"))
            a2 = (PF.take(512), Buf("a2"))
            oo = (PF.take(512), Buf("oo"))
            sqq = (PF.take(512), Buf("sqq"))
            hld = (PB.take(1024).rearrange("p (a n) -> p a n", n=512), Buf("hld"))
            cnt = dict(g=0, acc=0, pt=0, stg=0, rr=0)

            def attn_pass(q_ap, k_ap, q_buf, k_buf, v_ap, v_buf, kd, dv, scale, qb):
                ai = 0
                O_ap, O_buf = accO[ai]
                S_ap, S_buf = accS[ai]
                NG = 16
                gl = []

                def emit_S(g):
                    ps_ap, ps_buf = pss[(cnt["g"] + g) % 3]
                    P.op("pe", [I("matmul", ps_ap[:, j, :], lhsT=k_ap[0:kd, (2 * g + j) * 128:(2 * g + j + 1) * 128],
                                                                       rhs=q_ap[0:kd, qb * 512:(qb + 1) * 512], start=True, stop=True) for j in range(2)],
                         reads=[q_buf, k_buf], writes=[ps_buf])

                emit_S(0)
                emit_S(1)
                emit_S(2)
                for f in pend:
                    f()
                del pend[:]
                for g in range(NG):
                    ps_ap, ps_buf = pss[(cnt["g"] + g) % 3]
                    pt_ap, pt_buf = pts[cnt["pt"] % 4]; cnt["pt"] += 1
                    P.op("act", I("activation", pt_ap, ps_ap, AF.Exp, scale=scale), reads=[ps_buf], writes=[pt_buf])
                    fns = []
                    for j in range(2):
                        first = (g == 0 and j == 0)
                        lastf = (g == NG - 1 and j == 1)
                        fns.append(I("matmul",
                            O_ap[0:dv, :], lhsT=v_ap[:, 2 * g + j, 0:dv], rhs=pt_ap[:, j, :], start=first, stop=lastf))
                        fns.append(I("matmul",
                            S_ap[0:dv, :], lhsT=onesb[:, 0:dv], rhs=pt_ap[:, j, :], start=first, stop=lastf))
                    P.op("pe", fns, reads=[pt_buf, v_buf], writes=[O_buf, S_buf])
                    if g + 3 < NG:
                        emit_S(g + 3)
                cnt["g"] += NG
                return ai

            heads = [("d", h) for h in range(int(os.environ.get('DBG_ND', '4')))] + [("m", h) for h in range(int(os.environ.get('DBG_NM', '8')))]

            def load_head(i):
                kind, h = heads[i]
                q_ap, q_buf = hq[i % 2]; k_ap, k_buf = hk[i % 2]; v_ap, v_buf = hv[i % 2]
                if kind == "d":
                    P.dma([(q_ap, DQ[h])], writes=[q_buf], owner=q_buf)
                    P.dma([(k_ap, DK[h])], writes=[k_buf], owner=k_buf)
                    P.dma([(v_ap, DV[:, h * 128:(h + 1) * 128].rearrange("(t p) d -> p t d", p=128))], writes=[v_buf], owner=v_buf)
                else:
                    P.dma([(q_ap, MQ[h])], writes=[q_buf], owner=q_buf)
                    P.dma([(k_ap, MK[h])], writes=[k_buf], owner=k_buf)
                    P.dma([(v_ap[:, :, 0:64], MV[:, h * 64:(h + 1) * 64].rearrange("(t p) d -> p t d", p=128))], writes=[v_buf], owner=v_buf)

            load_head(0)
            for i, (kind, h) in enumerate(heads):
                if i + 1 < len(heads):
                    load_head(i + 1)
                q_ap, q_buf = hq[i % 2]; k_ap, k_buf = hk[i % 2]; v_ap, v_buf = hv[i % 2]
                for qb in range(int(os.environ.get('DBG_NQB', '8'))):
                    tsl = slice(qb * 512, (qb + 1) * 512)
                    if kind == "d":
                        res = []
                        for comp in range(2):
                            ai = attn_pass(q_ap[comp * 64:(comp + 1) * 64, :], k_ap[comp * 64:(comp + 1) * 64, :], q_buf, k_buf,
                                           v_ap, v_buf, 64, 128, 0.125, qb)
                            O_ap, O_buf = accO[ai]; S_ap, S_buf = accS[ai]
                            r_ap, r_buf = rr[cnt["rr"] % 2]; cnt["rr"] += 1
                            a_ap, a_buf = (a1, a2)[comp]
                            P.op("dve", I("reciprocal", r_ap, S_ap), reads=[S_buf], writes=[r_buf])
                            P.op("dve", I("tensor_tensor", a_ap, O_ap, r_ap, ALU.mult),
                                 reads=[O_buf, r_buf], writes=[a_buf])
                            res.append(ai)
                        ai = res[1]
                        S_ap, S_buf = accS[ai]
                        o_ap, o_buf = oo
                        s_ap2, s_buf2 = sqq
                        P.op("dve", I("scalar_tensor_tensor", o_ap, a2[0], nlam_t[:, l:l + 1], a1[0], op0=ALU.mult, op1=ALU.add),
                             reads=[a1[1], a2[1]], writes=[o_buf])
                        P.op("pool", I("tensor_tensor", s_ap2, o_ap, o_ap, ALU.mult), reads=[o_buf], writes=[s_buf2])
                        P.op("pool", I("tensor_copy", hld[0][:, 0, :], s_ap2), reads=[s_buf2], writes=[hld[1]])
                        P.op("pool", I("tensor_tensor", hld[0][:, 1, :], s_ap2, hld[0][:, 0, :], ALU.subtract), reads=[s_buf2, hld[1]], writes=[hld[1]])

                        def tail(S_ap=S_ap, S_buf=S_buf, o_ap=o_ap, o_buf=o_buf, h=h, tsl=tsl, l=l):
                            P.op("pe", [I("matmul", S_ap, lhsT=onesb[:, :], rhs=hld[0][:, 0, :], start=True, stop=False),
                                        I("matmul", S_ap, lhsT=onesb[:, :], rhs=hld[0][:, 1, :], start=False, stop=True)], reads=[hld[1]], writes=[S_buf])
                            r_ap, r_buf = rr[cnt["rr"] % 2]; cnt["rr"] += 1
                            P.op("act", I("activation", r_ap, S_ap, AF.Sqrt, bias=epsrms[:, 0:1], scale=1.0 / 128.0), reads=[S_buf], writes=[r_buf])
                            P.op("dve", I("reciprocal", r_ap, r_ap), reads=[r_buf], writes=[r_buf])
                            st_ap, st_buf = stg[cnt["stg"] % 2]; cnt["stg"] += 1
                            P.op("dve", I("scalar_tensor_tensor", st_ap, o_ap, gsc_t[:, l:l + 1], r_ap, op0=ALU.mult, op1=ALU.mult),
                                 reads=[o_buf, r_buf], writes=[st_buf])
                            P.dma([(AO[1, h, :, tsl], st_ap)], reads=[st_buf], owner=st_buf, kind="s")
                        pend.append(tail)
                    else:
                        ai = attn_pass(q_ap, k_ap, q_buf, k_buf, v_ap, v_buf, 128, 64, 96.0 ** -0.5, qb)
                        O_ap, O_buf = accO[ai]; S_ap, S_buf = accS[ai]
                        r_ap, r_buf = rr[cnt["rr"] % 2]; cnt["rr"] += 1
                        P.op("dve", I("reciprocal", r_ap[0:64, :], S_ap[0:64, :]), reads=[S_buf], writes=[r_buf])
                        st_ap, st_buf = stg[cnt["stg"] % 2]; cnt["stg"] += 1
                        P.op("dve", I("tensor_tensor", st_ap[0:64, :], O_ap[0:64, :], r_ap[0:64, :], ALU.mult),
                             reads=[O_buf, r_buf], writes=[st_buf])
                        P.dma([(AO[2, h // 2, (h % 2) * 64:(h % 2) * 64 + 64, tsl], st_ap[0:64, :])], reads=[st_buf], owner=st_buf, kind="s")
            for f in pend:
                f()
            del pend[:]
            P.barrier()
            if stop_after == "CD":
                break

            PB.reset(); PF.reset()
            nq = [(PB.take(T), Buf("nq%d" % i)) for i in range(2)]
            nk = [(PB.take(T), Buf("nk%d" % i)) for i in range(2)]
            nve = [(PB.take(32 * 64).rearrange("p (a n) -> p a n", n=64), Buf("nve%d" % i)) for i in range(2)]
            nvo = [(PB.take(32 * 64).rearrange("p (a n) -> p a n", n=64), Buf("nvo%d" % i)) for i in range(2)]
            npt = [(PB.take(1024).rearrange("p (r j n) -> p r j n", r=4, j=4), Buf("npt%d" % i)) for i in range(3)]
            nout = [(PB.take(T), Buf("nout%d" % i)) for i in range(2)]
            tab = [(PF.take(14 * 64).rearrange("p (s n) -> p s n", n=64), Buf("tab%d" % i)) for i in range(2)]
            msk = (PF.take(14 * 64).rearrange("p (s n) -> p s n", n=64), Buf("msk"))
            tmp = [(PF.take(1024).rearrange("p (r j n) -> p r j n", r=4, j=4), Buf("tmp%d" % i)) for i in range(3)]
            nrr = [(PF.take(256), Buf("nrr%d" % i)) for i in range(2)]
            tabi = [PF.take(256) for i in range(2)]
            pssn = [(PSt[i][:, :].rearrange("p (r j n) -> p r j n", r=4, j=4), Buf("pssn%d" % i)) for i in range(3)]
            naOS = [(bank(6 + i), Buf("naOS%d" % i)) for i in range(2)]
            P.dma([(msk[0].rearrange("p s n -> p (s n)"), namask)], writes=[msk[1]], owner=msk[1])
            it = 0
            for cch in range(4):
                q_ap, q_buf = nq[cch % 2]; k_ap, k_buf = nk[cch % 2]
                P.dma([(q_ap, NQ[cch])], writes=[q_buf], owner=q_buf)
                P.dma([(k_ap, NK[cch])], writes=[k_buf], owner=k_buf)
                for hh in range(2):
                    h = cch * 2 + hh
                    hi = h % 2
                    ve_ap, ve_buf = nve[hi]; vo_ap, vo_buf = nvo[hi]
                    P.dma([(ve_ap, NV[:, h * 64:(h + 1) * 64].rearrange("(t p) d -> p t d", p=128))], writes=[ve_buf], owner=ve_buf)
                    P.dma([(vo_ap[:, 0:31, :], NV[64:T - 64, h * 64:(h + 1) * 64].rearrange("(t p) d -> p t d", p=128))], writes=[vo_buf], owner=vo_buf)
                    tb_ap, tb_buf = tab[hi]
                    src = bass.AP(tensor=nabias.tensor, offset=nabias[l, h].offset, ap=[[64, 128], [64 * 64, 14], [1, 64]])
                    P.dma([(tb_ap, src)], writes=[tb_buf], owner=tb_buf)
                    P.op("pool", I("tensor_tensor", tb_ap, tb_ap, msk[0], ALU.add), reads=[tb_buf, msk[1]], writes=[tb_buf])
                    ti_ap = tabi[hi]
                    P.op("pool", I("tensor_copy", ti_ap.rearrange("p (j n) -> p j n", n=64), tb_ap[:, 3:10:2, :]), reads=[tb_buf], writes=[tb_buf])
                    o_ap, o_buf = nout[hi]
                    qh = q_ap[hh * 64:(hh + 1) * 64, :]
                    kh = k_ap[hh * 64:(hh + 1) * 64, :]
                    def rows_of(rg):
                        return [(rg * 4 + ri, min(max(rg * 4 + ri - 4, 0), 56)) for ri in range(4)]

                    def na_S(rg, k):
                        ps_ap, ps_buf = pssn[k % 3]
                        fns = []
                        for ri, (r, rs) in enumerate(rows_of(rg)):
                            for j in range(4):
                                k0 = (rs + 2 * j) * 64
                                fns.append(I("matmul", ps_ap[:, ri, j, :], lhsT=kh[:, k0:k0 + 128], rhs=qh[:, r * 64:(r + 1) * 64], start=True, stop=True))
                        P.op("pe", fns, reads=[q_buf, k_buf], writes=[ps_buf])

                    for g0 in range(3):
                        na_S(g0, it + g0)
                    for rg in range(16):
                        k = it + rg
                        ps_ap, ps_buf = pssn[k % 3]
                        tm_ap, tm_buf = tmp[k % 3]
                        pt_ap, pt_buf = npt[k % 3]
                        OS_ap, OS_buf = naOS[k % 2]
                        O_ap = OS_ap[:, 0:256]; S_ap = OS_ap[:, 256:512]
                        r_ap, r_buf = nrr[k % 2]
                        rows = rows_of(rg)
                        offs = [rs - r + 7 for (r, rs) in rows]
                        if all(o == 3 for o in offs):
                            bc = bass.AP(tensor=ti_ap.tensor, offset=ti_ap.offset, ap=[list(ti_ap.ap[0]), [0, 4], [1, 256]])
                            P.op("dve", I("scalar_tensor_tensor", tm_ap.rearrange("p r j n -> p r (j n)"), ps_ap.rearrange("p r j n -> p r (j n)"), 0.125, bc,
                                          op0=ALU.mult, op1=ALU.add), reads=[ps_buf, tb_buf], writes=[tm_buf])
                        else:
                            for ri, off in enumerate(offs):
                                P.op("dve", I("scalar_tensor_tensor", tm_ap[:, ri, :, :], ps_ap[:, ri, :, :], 0.125, tb_ap[:, off:off + 7:2, :],
                                              op0=ALU.mult, op1=ALU.add), reads=[ps_buf, tb_buf], writes=[tm_buf])
                        P.op("act", I("activation", pt_ap, tm_ap, AF.Exp), reads=[tm_buf], writes=[pt_buf])
                        fns = []
                        for ri, (r, rs) in enumerate(rows):
                            for j in range(4):
                                a = rs + 2 * j
                                if a % 2 == 0:
                                    vt = ve_ap[:, a // 2, :]
                                else:
                                    vt = vo_ap[:, (a - 1) // 2, :]
                                fns.append(I("matmul", O_ap[0:64, ri * 64:(ri + 1) * 64], lhsT=vt, rhs=pt_ap[:, ri, j, :], start=(j == 0), stop=(j == 3)))
                            for j in range(4):
                                fns.append(I("matmul", S_ap[0:64, ri * 64:(ri + 1) * 64], lhsT=onesb[:, 0:64], rhs=pt_ap[:, ri, j, :], start=(j == 0), stop=(j == 3)))
                        P.op("pe", fns, reads=[pt_buf, ve_buf, vo_buf], writes=[OS_buf])
                        if rg + 3 < 16:
                            na_S(rg + 3, it + rg + 3)
                        P.op("dve", I("reciprocal", r_ap[0:64, :], S_ap[0:64, :]), reads=[OS_buf], writes=[r_buf])
                        P.op("dve", I("tensor_tensor", o_ap[0:64, rg * 256:(rg + 1) * 256], O_ap[0:64, :], r_ap[0:64, :], ALU.mult),
                             reads=[OS_buf, r_buf], writes=[o_buf])
                    it += 16
                    P.dma([(AO[0, cch, hh * 64:(hh + 1) * 64, :], o_ap[0:64, :])], reads=[o_buf], owner=o_buf, kind="s")
            P.barrier()
            if stop_after == "E":
                break

            PB.reset(); PF.reset()
            wstage = [(PF.take(2048), Buf("wstg%d" % i)) for i in range(1)]
            wbr = PB.take(12 * 1024).rearrange("p (a n) -> p a n", n=1024); wbr_buf = Buf("wbr")
            wo = PB.take(8 * 1024).rearrange("p (a n) -> p a n", n=1024); wo_buf = Buf("wo")
            for i in range(3):
                wload(w_br[l, i].rearrange("(c p) d -> p c d", p=128), wbr[:, 4 * i:4 * i + 4, :], wbr_buf, wstage)
            wload(w_out[l].rearrange("(c p) d -> p c d", p=128), wo, wo_buf, wstage)
            c = ln_setup(ln1_g[l:l + 1, :], ln1_b[l:l + 1, :], need_y=False)
            aob = (PB.take(12 * 512).rearrange("p (a n) -> p a n", n=512), Buf("aob"))
            gts = [(PB.take(3 * 512).rearrange("p (a n) -> p a n", n=512), Buf("gts%d" % i)) for i in range(2)]
            mg = (PB.take(8 * 512).rearrange("p (a n) -> p a n", n=512), [Buf("mg%d" % i) for i in range(8)])
            mm = [(PF.take(512), Buf("mm%d" % i)) for i in range(3)]
            xold = [(PF.take(1024), Buf("xold%d" % i)) for i in range(2)]
            psb = [(bank(i), Buf("psb%d" % i)) for i in range(5)]
            psy = (PSt[3][:, :], Buf("psy"))
            psT = [(bank(5), Buf("psT"))]
            GTv = GT.rearrange("(i c) p t -> c p i t", i=3)
            cnt = dict(b=0, g=0, x=0)
            for tb in range(8):
                tsl = slice(tb * 512, (tb + 1) * 512)
                ao_ap, ao_buf = aob
                P.dma([(ao_ap[:, 4 * i:4 * i + 4, :], AO[i, :, :, tsl].rearrange("c p t -> p c t")) for i in range(3)], writes=[ao_buf], owner=ao_buf)
                for dm in range(8):
                    g_ap, g_buf = gts[cnt["g"] % 2]; cnt["g"] += 1
                    P.dma([(g_ap, GTv[dm, :, :, tsl])], writes=[g_buf], owner=g_buf)
                    pbs = []
                    for i in range(3):
                        pb_ap, pb_buf = psb[cnt["b"] % 5]; cnt["b"] += 1
                        pbs.append((pb_ap, pb_buf))
                        P.op("pe", [I("matmul", pb_ap, lhsT=wbr[:, 4 * i + cc, dm * 128:(dm + 1) * 128],
                                                                                  rhs=ao_ap[:, 4 * i + cc, :], start=(cc == 0), stop=(cc == 3)) for cc in range(4)],
                             reads=[ao_buf, wbr_buf], writes=[pb_buf])
                    for i in range(3):
                        P.op("dve", I("tensor_tensor", mm[i][0], pbs[i][0], g_ap[:, i, :], ALU.mult),
                             reads=[pbs[i][1], g_buf], writes=[mm[i][1]])
                    P.op("pool", I("tensor_tensor", mm[0][0], mm[0][0], mm[1][0], ALU.add), reads=[mm[0][1], mm[1][1]], writes=[mm[0][1]])
                    P.op("pool", I("tensor_tensor", mg[0][:, dm, :], mm[0][0], mm[2][0], ALU.add), reads=[mm[0][1], mm[2][1]], writes=[mg[1][dm]])
                for s in range(4):
                    t = tb * 4 + s
                    xo_ap, xo_buf = xold[cnt["x"] % 2]
                    y_ap, y_buf = xo_ap, xo_buf
                    cnt["x"] += 1
                    P.dma([(xo_ap, XTOK[t * 128:(t + 1) * 128, :])], writes=[xo_buf], owner=xo_buf)
                    py_ap, py_buf = psy
                    P.op("pe", [I("matmul", py_ap[:, half * 512:(half + 1) * 512], lhsT=mg[0][:, dm, s * 128:(s + 1) * 128],
                                                                      rhs=wo[:, dm, half * 512:(half + 1) * 512], start=(dm == 0), stop=(dm == 7))
                                for half in range(2) for dm in range(8)], reads=mg[1] + [wo_buf], writes=[py_buf])
                    P.op("dve", I("scalar_tensor_tensor", y_ap, xo_ap, ALPHA, py_ap, op0=ALU.mult, op1=ALU.add),
                         reads=[xo_buf, py_buf], writes=[y_buf])
                    ln_tile(c, y_ap, y_buf, t, psT[0][0], psT[0][1])
            P.barrier()
            if stop_after == "F":
                break

            PB.reset(); PF.reset()
            wstage = [(PF.take(2048), Buf("wstg%d" % i)) for i in range(2)]
            wg = [(PB.take(1024).rearrange("p (a n) -> p a n", n=128), Buf("wg%d" % i)) for i in range(3)]
            wu = [(PB.take(1024).rearrange("p (a n) -> p a n", n=128), Buf("wu%d" % i)) for i in range(3)]
            stg = [(PB.take(512), Buf("stg%d" % i)) for i in range(3)]
            sg = [(PF.take(512), Buf("sg%d" % i)) for i in range(2)]
            psG = [(bank(2 * i), Buf("psG%d" % i)) for i in range(4)]
            psU = [(bank(2 * i + 1), Buf("psU%d" % i)) for i in range(4)]

            def load_f(fc):
                wload(w_fi[l][:, fc * 128:(fc + 1) * 128].rearrange("(kc p) e -> p kc e", p=128), wg[fc % 3][0], wg[fc % 3][1], wstage)
                wload(w_fi[l][:, DFF + fc * 128:DFF + (fc + 1) * 128].rearrange("(kc p) e -> p kc e", p=128), wu[fc % 3][0], wu[fc % 3][1], wstage)

            load_f(0); load_f(1)
            it = 0
            for fc in range(FC):
                if fc + 2 < FC:
                    load_f(fc + 2)
                wg_ap, wg_buf = wg[fc % 3]; wu_ap, wu_buf = wu[fc % 3]
                for tb in range(8):
                    pg_ap, pg_buf = psG[it % 4]; pu_ap, pu_buf = psU[it % 4]
                    P.op("pe", [I("matmul", pg_ap, lhsT=wg_ap[:, kc, :], rhs=xT[:, kc, tb * 512:(tb + 1) * 512],
                                                                                    start=(kc == 0), stop=(kc == KC - 1)) for kc in range(KC)],
                         reads=[wg_buf], writes=[pg_buf])
                    P.op("pe", [I("matmul", pu_ap, lhsT=wu_ap[:, kc, :], rhs=xT[:, kc, tb * 512:(tb + 1) * 512],
                                                                                    start=(kc == 0), stop=(kc == KC - 1)) for kc in range(KC)],
                         reads=[wu_buf], writes=[pu_buf])
                    sg_ap, sg_buf = sg[it % 2]
                    st_ap, st_buf = stg[it % 3]
                    it += 1
                    P.op("act", I("activation", sg_ap, pg_ap, AF.Silu), reads=[pg_buf], writes=[sg_buf])
                    P.op("dve", I("tensor_tensor", st_ap, pu_ap, sg_ap, ALU.mult),
                         reads=[pu_buf, sg_buf], writes=[st_buf])
                    P.dma([(HT[fc, :, tb * 512:(tb + 1) * 512], st_ap)], reads=[st_buf], owner=st_buf, kind="s")
            P.barrier()
            if stop_after == "G":
                break

            PB.reset(); PF.reset()
            wstage = [(PF.take(2048), Buf("wstg%d" % i)) for i in range(1)]
            wfo = PB.take(FC * 1024).rearrange("p (a n) -> p a n", n=1024); wfo_buf = Buf("wfo")
            wload(w_fo[l].rearrange("(c p) d -> p c d", p=128), wfo, wfo_buf, wstage)
            c = ln_setup(ln2_g[l:l + 1, :], ln2_b[l:l + 1, :], need_y=False)
            hb = [(PB.take(FC * 256).rearrange("p (a n) -> p a n", n=256), Buf("hb%d" % i)) for i in range(2)]
            xold = [(PF.take(1024), Buf("xold%d" % i)) for i in range(2)]
            psy = [(PSt[i][:, :], Buf("psy%d" % i)) for i in range(2)]
            psT = [(bank(4 + i), Buf("psT%d" % i)) for i in range(2)]
            k = 0
            for hbi in range(16):
                h_ap, h_buf = hb[hbi % 2]
                P.dma([(h_ap, HT[:, :, hbi * 256:(hbi + 1) * 256].rearrange("c p t -> p c t"))], writes=[h_buf], owner=h_buf)
                for s in range(2):
                    t = hbi * 2 + s
                    xo_ap, xo_buf = xold[k % 2]
                    y_ap, y_buf = xo_ap, xo_buf
                    py_ap, py_buf = psy[k % 2]
                    pT_ap, pT_buf = psT[k % 2]
                    k += 1
                    P.dma([(xo_ap, XTOK[t * 128:(t + 1) * 128, :])], writes=[xo_buf], owner=xo_buf)
                    P.op("pe", [I("matmul",
                        py_ap[:, half * 512:(half + 1) * 512], lhsT=h_ap[:, fc, s * 128:(s + 1) * 128], rhs=wfo[:, fc, half * 512:(half + 1) * 512],
                        start=(fc == 0), stop=(fc == FC - 1)) for half in range(2) for fc in range(FC)], reads=[h_buf, wfo_buf], writes=[py_buf])
                    P.op("dve", I("scalar_tensor_tensor", y_ap, xo_ap, ALPHA, py_ap, op0=ALU.mult, op1=ALU.add),
                         reads=[xo_buf, py_buf], writes=[y_buf])
                    ln_tile(c, y_ap, y_buf, t, pT_ap, pT_buf, final=(l == n_layers - 1))
            P.barrier()

        P.emit(block)
        build.stats = dict(ninst=P.ninst, ecnt=dict(P.ecnt), maxsem=max(P.semval.values()))
    return nc


def _consts():
    bf = ml_dtypes.bfloat16
    cst = {}
    cst["c_ident"] = np.eye(128, dtype=np.float32).astype(bf)
    cst["c_onesb"] = np.ones((128, 128), dtype=np.float32).astype(bf)
    cst["c_onesf"] = np.ones((128, 128), dtype=np.float32)
    pos = np.arange(T, dtype=np.float32)

    def tables(half):
        inv = np.exp(-math.log(ROPE_THETA) * np.arange(half, dtype=np.float32) / half).astype(np.float32)
        ang = (pos[None, :] * inv[:, None]).astype(np.float32)
        return np.cos(ang.astype(np.float64)).astype(np.float32), np.sin(ang.astype(np.float64)).astype(np.float32)

    cos8, sin8 = tables(8)
    cosd = np.ones((128, T), np.float32); sind = np.zeros((128, T), np.float32)
    rdm = np.zeros((128, 128), np.float32)
    for gb in (0, 64):
        for i in range(8):
            cosd[gb + i] = cos8[i]; cosd[gb + 8 + i] = cos8[i]
            sind[gb + i] = -sin8[i]; sind[gb + 8 + i] = sin8[i]
            rdm[gb + 8 + i, gb + i] = 1.0
            rdm[gb + i, gb + 8 + i] = 1.0
    cst["c_cosd"] = cosd; cst["c_sind"] = sind; cst["c_rd"] = rdm.astype(bf)
    cos16, sin16 = tables(16)
    cosm = np.ones((128, T), np.float32); sinm = np.zeros((128, T), np.float32)
    rmm = np.zeros((128, 128), np.float32)
    for i in range(16):
        cosm[64 + i] = cos16[i]; cosm[80 + i] = cos16[i]
        sinm[64 + i] = -sin16[i]; sinm[80 + i] = sin16[i]
        rmm[80 + i, 64 + i] = 1.0
        rmm[64 + i, 80 + i] = 1.0
    cst["c_cosm"] = cosm; cst["c_sinm"] = sinm; cst["c_rm"] = rmm.astype(bf)
    cst["c_cosk"] = np.ascontiguousarray(cosm[64:96]); cst["c_sink"] = np.ascontiguousarray(sinm[64:96])
    cst["c_rk"] = np.ascontiguousarray(rmm[64:96, 64:96]).astype(bf)
    qc = np.arange(64)
    ws = np.clip(qc - 8, 0, 48)
    kc_ = np.arange(64)
    valid = (kc_[:, None] >= ws[None, :]) & (kc_[:, None] < ws[None, :] + 16)
    m = np.where(valid, 0.0, NEG).astype(np.float32)
    m2 = np.concatenate([m, m], axis=0)
    cst["namask"] = np.ascontiguousarray(np.broadcast_to(m2[:, None, :], (128, 14, 64))).reshape(128, 14 * 64)
    return cst


_CACHE = {}


def _prep_shared(inp):
    f = lambda a: np.ascontiguousarray(np.asarray(a, dtype=np.float32))
    sh = {}
    sh["ln_in_g"] = f(inp["ln_in_g"]).reshape(1, D)
    sh["ln_in_b"] = f(inp["ln_in_b"]).reshape(1, D)
    sh["w_in"] = f(inp["w_in"])
    sh["bgate"] = np.ascontiguousarray(f(inp["b_gate"]).reshape(NL, 24, 128).transpose(0, 2, 1))
    rpb = f(inp["na_rpb"])
    kcol = np.arange(64)[:, None]; qcol = np.arange(64)[None, :]
    idx = np.clip(kcol - qcol, -15, 15) + 15
    sh["nabias"] = np.ascontiguousarray(rpb[:, :, :, idx]).reshape(NL, 8, 960, 64)
    sh["dlam"] = f(inp["diff_lambda"]).reshape(NL, 1, 256)
    sh["subg"] = f(inp["diff_subln_g"]).reshape(NL, 128, 1)
    sh["gq"] = np.ascontiguousarray(f(inp["mla_q_norm_g"]).reshape(NL, 3, 128).transpose(0, 2, 1))
    sh["gkv"] = np.ascontiguousarray(f(inp["mla_kv_norm_g"]).reshape(NL, 2, 128).transpose(0, 2, 1))
    sh["w_qb"] = f(inp["w_mla_qb"])
    sh["w_kvb"] = f(inp["w_mla_kvb"])
    sh["w_br"] = f(inp["w_branch"])
    sh["w_out"] = f(inp["w_out"])
    sh["ln1_g"] = f(inp["ln1_g"]); sh["ln1_b"] = f(inp["ln1_b"])
    sh["w_fi"] = f(inp["w_ffn_in"]); sh["w_fo"] = f(inp["w_ffn_out"])
    sh["ln2_g"] = f(inp["ln2_g"]); sh["ln2_b"] = f(inp["ln2_b"])
    sh.update(_consts())
    return sh


def kernel(**inputs):
    x = np.ascontiguousarray(np.asarray(inputs["x"], dtype=np.float32))
    nb = x.shape[0]
    sh = _prep_shared(inputs)
    if "nc" not in _CACHE:
        _CACHE["nc"] = build()
    nc = _CACHE["nc"]
    in_maps = []
    for b in range(nb):
        m = dict(sh)
        m["x"] = x[b]
        in_maps.append(m)
    res = run_bass_kernel_spmd(nc, in_maps, core_ids=list(range(nb)))
    return np.stack([np.asarray(r["out"], dtype=np.float32) for r in res.results], axis=0)
```

```python
import math
import os
import numpy as np
import ml_dtypes
import concourse.bass as bass
import concourse.mybir as mybir
from concourse.bass_utils import run_bass_kernel_spmd

F32 = mybir.dt.float32
BF16 = mybir.dt.bfloat16
AF = mybir.ActivationFunctionType
ALU = mybir.AluOpType
AX = mybir.AxisListType

T = 4096
D = 1024
KC = 8
NL = 4
DIN = 6816
DFF = 2816
FC = 22
ALPHA = (2 * NL) ** 0.25
LN_EPS = 1e-5
RMS_EPS = 1e-6
ROPE_THETA = 500000.0
NEG = -30000.0
DBGB = os.environ.get('DBG_B', 'qkvr')


class Buf:
    __slots__ = ("name", "w", "r", "lsem", "ssem", "last_s")

    def __init__(self, name):
        self.name = name
        self.w = None
        self.r = {}
        self.lsem = None
        self.ssem = None
        self.last_s = None


class Prog:
    CE = ("pe", "act", "dve", "pool")

    def __init__(self, nc, sems):
        self.nc = nc
        self.sems = sems
        self.q = {e: [] for e in ("pe", "act", "dve", "pool", "sp")}
        self.esi = {e: i for i, e in enumerate(self.CE)}
        self.bar_si = 4
        self.free = list(range(5, len(sems)))
        self.ecnt = {e: 0 for e in self.CE}
        self.barcnt = 0
        self.semval = {i: 0 for i in range(len(sems))}
        self.waited = {e: {} for e in self.q}
        self.phase_dma = {}
        self.sem_bufs = []
        self.ninst = 0

    def wait(self, eng, ev):
        if ev is None:
            return
        si, val = ev
        if self.waited[eng].get(si, 0) >= val:
            return
        self.waited[eng][si] = val
        sem = self.sems[si]
        self.q[eng].append(("wait_ge", (sem, val), {}))

    def _deps(self, eng, reads, writes):
        deps = {}

        def add(ev):
            if ev is None:
                return
            si, val = ev
            if deps.get(si, 0) < val:
                deps[si] = val

        for b in reads:
            add(b.w)
        for b in writes:
            add(b.w)
            for si, val in b.r.items():
                add((si, val))
        for si, val in deps.items():
            if eng == "pe" and si == self.esi["pe"]:
                continue
            self.wait(eng, (si, val))

    def op(self, eng, fns, reads=(), writes=()):
        self._deps(eng, reads, writes)
        si = self.esi[eng]
        self.ecnt[eng] += 1
        val = self.ecnt[eng]
        ev = (si, val)
        if isinstance(fns, tuple):
            fns = [fns]
        for f in fns[:-1]:
            self.q[eng].append(f)
        last = fns[-1]
        sem = self.sems[si]
        self.q[eng].append((last[0], last[1], last[2], sem, 1))
        self.ninst += len(fns)
        for b in reads:
            if b.r.get(si, 0) < val:
                b.r[si] = val
        for b in writes:
            b.w = ev
            b.r = {}
        return ev

    def _bufsem(self, b, kind):
        cur = b.lsem if kind == "l" else b.ssem
        if cur is None:
            cur = self.free.pop()
            if kind == "l":
                b.lsem = cur
            else:
                b.ssem = cur
            self.sem_bufs.append(b)
        return cur

    def dma(self, pairs, reads=(), writes=(), owner=None, kind="l", eng="sp"):
        self._deps(eng, reads, writes)
        if kind == "s" and owner.last_s is not None:
            self.wait(eng, owner.last_s)
        si = self._bufsem(owner, kind)
        self.semval[si] += 16 * len(pairs)
        val = self.semval[si]
        assert val < 60000, "semaphore value too large"
        ev = (si, val)
        sem = self.sems[si]
        for (o, i) in pairs:
            self.q[eng].append(("dma_start", (), dict(out=o, in_=i), sem, 16))
        self.ninst += len(pairs)
        for b in reads:
            if b.r.get(si, 0) < val:
                b.r[si] = val
        for b in writes:
            b.w = ev
            b.r = {}
        if kind == "s":
            owner.last_s = ev
        self.phase_dma[si] = val
        return ev

    def barrier(self):
        for e in self.CE:
            if self.ecnt[e] > 0:
                self.wait("sp", (self.esi[e], self.ecnt[e]))
        for si, val in self.phase_dma.items():
            self.wait("sp", (si, val))
        self.barcnt += 1
        n = self.barcnt
        sem = self.sems[self.bar_si]
        self.q["sp"].append(("sem_inc", (sem, 1), {}))
        for e in self.CE:
            self.wait(e, (self.bar_si, n))
        for e in self.q:
            for x in self.CE:
                self.waited[e][self.esi[x]] = self.ecnt[x]
            for si, val in self.semval.items():
                if si > self.bar_si:
                    self.waited[e][si] = val
        self.phase_dma = {}
        for b in self.sem_bufs:
            if b.lsem is not None:
                self.free.append(b.lsem)
                b.lsem = None
            if b.ssem is not None:
                self.free.append(b.ssem)
                b.ssem = None
        self.sem_bufs = []

    def emit(self, block):
        q = self.q

        def run(e, lst):
            for it in lst:
                ins = getattr(e, it[0])(*it[1], **it[2])
                if len(it) == 5:
                    ins.then_inc(it[3], it[4])

        @block.tensor
        def _(e):
            run(e, q["pe"])

        @block.scalar
        def _(e):
            run(e, q["act"])

        @block.vector
        def _(e):
            run(e, q["dve"])

        @block.gpsimd
        def _(e):
            run(e, q["pool"])

        @block.sync
        def _(e):
            run(e, q["sp"])


def I(name, *a, **k):
    return (name, a, k)


class Carver:
    def __init__(self, t, n):
        self.t = t
        self.n = n
        self.off = 0

    def reset(self):
        self.off = 0

    def take(self, n):
        assert self.off + n <= self.n, ("sbuf pool overflow", self.off, n, self.n)
        a = self.t[:, self.off:self.off + n]
        self.off += n
        return a


def build(n_layers=NL, debug=(), stop_after=None):
    nc = bass.Bass("TRN2", target_bir_lowering=False)

    def din(name, shape, dt=F32):
        return nc.dram_tensor(name, list(shape), dt, kind="ExternalInput").ap()

    def scr(name, shape, dt):
        kind = "ExternalOutput" if name in debug else "Internal"
        return nc.dram_tensor(name, list(shape), dt, kind=kind).ap()

    x_in = din("x", [T, D])
    ln_in_g = din("ln_in_g", [1, D])
    ln_in_b = din("ln_in_b", [1, D])
    w_in = din("w_in", [NL, D, DIN])
    bgate = din("bgate", [NL, 128, 24])
    nabias = din("nabias", [NL, 8, 960, 64])
    namask = din("namask", [128, 14 * 64])
    dlam = din("dlam", [NL, 1, 256])
    subg = din("subg", [NL, 128, 1])
    gq_in = din("gq", [NL, 128, 3])
    gkv_in = din("gkv", [NL, 128, 2])
    w_qb = din("w_qb", [NL, 384, 768])
    w_kvb = din("w_kvb", [NL, 256, 1024])
    w_br = din("w_br", [NL, 3, 512, D])
    w_out = din("w_out", [NL, D, D])
    ln1_g = din("ln1_g", [NL, D])
    ln1_b = din("ln1_b", [NL, D])
    w_fi = din("w_fi", [NL, D, 2 * DFF])
    w_fo = din("w_fo", [NL, DFF, D])
    ln2_g = din("ln2_g", [NL, D])
    ln2_b = din("ln2_b", [NL, D])
    c_ident = din("c_ident", [128, 128], BF16)
    c_onesb = din("c_onesb", [128, 128], BF16)
    c_onesf = din("c_onesf", [128, 128])
    c_rd = din("c_rd", [128, 128], BF16)
    c_rm = din("c_rm", [128, 128], BF16)
    c_rk = din("c_rk", [32, 32], BF16)
    c_cosd = din("c_cosd", [128, T])
    c_sind = din("c_sind", [128, T])
    c_cosm = din("c_cosm", [128, T])
    c_sinm = din("c_sinm", [128, T])
    c_cosk = din("c_cosk", [32, T])
    c_sink = din("c_sink", [32, T])
    out = nc.dram_tensor("out", [T, D], F32, kind="ExternalOutput").ap()

    XTOK = scr("XTOK", [T, D], F32)
    NQ = scr("NQ", [4, 128, T], BF16)
    NK = scr("NK", [4, 128, T], BF16)
    NV = scr("NV", [T, 512], BF16)
    DQ = scr("DQ", [4, 128, T], BF16)
    DK = scr("DK", [4, 128, T], BF16)
    DV = scr("DV", [T, 512], BF16)
    MC = scr("MC", [6, 128, T], F32)
    GT = scr("GT", [24, 128, T], BF16)
    MQ = scr("MQ", [8, 128, T], BF16)
    MK = scr("MK", [8, 128, T], BF16)
    MV = scr("MV", [T, 512], BF16)
    AO = scr("AO", [3, 4, 128, T], BF16)
    HT = scr("HT", [FC, 128, T], BF16)
    RS = scr("RS", [2, 512], F32)

    NPB = 36864
    NPF = 10240
    xT = nc.alloc_sbuf_tensor("xT", [128, KC, T], BF16)
    PBt = nc.alloc_sbuf_tensor("PB", [128, NPB], BF16)
    PFt = nc.alloc_sbuf_tensor("PF", [128, NPF], F32)
    ident = nc.alloc_sbuf_tensor("ident", [128, 128], BF16)
    onesb = nc.alloc_sbuf_tensor("onesb", [128, 128], BF16)
    onesf = nc.alloc_sbuf_tensor("onesf", [128, 128], F32)
    rd = nc.alloc_sbuf_tensor("rd", [128, 128], BF16)
    rm = nc.alloc_sbuf_tensor("rm", [128, 128], BF16)
    rk = nc.alloc_sbuf_tensor("rk", [32, 32], BF16)
    bg_t = nc.alloc_sbuf_tensor("bg_t", [128, NL * 24], F32)
    gq_t = nc.alloc_sbuf_tensor("gq_t", [128, NL * 3], F32)
    gkv_t = nc.alloc_sbuf_tensor("gkv_t", [128, NL * 2], F32)
    gsc_t = nc.alloc_sbuf_tensor("gsc_t", [128, NL], F32)
    nlam_t = nc.alloc_sbuf_tensor("nlam_t", [128, NL], F32)
    lamw = nc.alloc_sbuf_tensor("lamw", [128, NL * 256], F32)
    lamp = nc.alloc_sbuf_tensor("lamp", [128, NL * 128], F32)
    lams = nc.alloc_sbuf_tensor("lams", [128, NL * 2], F32)
    lame = nc.alloc_sbuf_tensor("lame", [128, NL * 2], F32)
    lnst = nc.alloc_sbuf_tensor("lnst", [128, 4 * 16], F32)
    epsln = nc.alloc_sbuf_tensor("epsln", [128, 1], F32)
    epsrms = nc.alloc_sbuf_tensor("epsrms", [128, 1], F32)
    PSt = [nc.alloc_psum_tensor("ps%d" % i, [128, 1024], F32) for i in range(4)]

    def bank(i):
        return PSt[i // 2][:, (i % 2) * 512:(i % 2) * 512 + 512]

    PB = Carver(PBt, NPB)
    PF = Carver(PFt, NPF)

    import contextlib
    with contextlib.ExitStack() as es:
        sems = [es.enter_context(nc.semaphore("s%d" % i)) for i in range(100)]
        block = es.enter_context(nc.Block())
        P = Prog(nc, sems)

        def wload(src, dst_ap, dst_buf, stage):
            a, n = src.shape[1], src.shape[2]
            per = max(1, 2048 // n)
            i = 0
            k = wload.k
            while i < a:
                j = min(a, i + per)
                sap, sbuf = stage[k % len(stage)]
                k += 1
                sv = sap[:, 0:(j - i) * n].rearrange("p (a n) -> p a n", n=n)
                P.dma([(sv, src[:, i:j, :])], writes=[sbuf], owner=sbuf)
                P.op("pool", I("tensor_copy", dst_ap[:, i:j, :], sv), reads=[sbuf], writes=[dst_buf])
                i = j
            wload.k = k

        wload.k = 0

        def rope_epi(ps_ap, ps_buf, rows, cos_ap, sin_ap, cs_buf, rmat, ps2_ap, ps2_buf, xb, t1, t2, stg_ap, stg_buf):
            xb_ap, xb_buf = xb
            t1_ap, t1_buf = t1
            t2_ap, t2_buf = t2
            nst = int(os.environ.get('DBG_RSTEPS', '5'))
            P.op("act", I("copy", xb_ap[0:rows, :], ps_ap[0:rows, :]), reads=[ps_buf], writes=[xb_buf])
            if nst < 2: return
            P.op("pe", I("matmul", ps2_ap[0:rows, :], lhsT=rmat[0:rows, 0:rows], rhs=xb_ap[0:rows, :], start=True, stop=True),
                 reads=[xb_buf], writes=[ps2_buf])
            if nst < 3: return
            v = os.environ.get('DBG_T1', '')
            if v == 'xb':
                P.op("dve", I("tensor_tensor", t1_ap[0:rows, :], xb_ap[0:rows, :], cos_ap[0:rows, :], ALU.mult),
                     reads=[xb_buf, cs_buf], writes=[t1_buf])
            elif v == 'sin':
                P.op("dve", I("tensor_tensor", t1_ap[0:rows, :], ps_ap[0:rows, :], sin_ap[0:rows, :], ALU.mult),
                     reads=[ps_buf, cs_buf], writes=[t1_buf])
            elif v == 't2':
                P.op("dve", I("tensor_tensor", t2_ap[0:rows, :], ps_ap[0:rows, :], cos_ap[0:rows, :], ALU.mult),
                     reads=[ps_buf, cs_buf], writes=[t2_buf])
            else:
                P.op("dve", I("tensor_tensor", t1_ap[0:rows, :], ps_ap[0:rows, :], cos_ap[0:rows, :], ALU.mult),
                     reads=[ps_buf, cs_buf, xb_buf], writes=[t1_buf])
            if nst < 4: return
            P.op("dve", I("tensor_tensor", t2_ap[0:rows, :], ps2_ap[0:rows, :], sin_ap[0:rows, :], ALU.mult),
                 reads=[ps2_buf, cs_buf], writes=[t2_buf])
            if nst < 5: return
            P.op("pool", I("tensor_tensor", stg_ap[0:rows, :], t1_ap[0:rows, :], t2_ap[0:rows, :], ALU.add),
                 reads=[t1_buf, t2_buf], writes=[stg_buf])

        class LNCtx:
            pass

        def ln_setup(g_src, b_src, need_y=True):
            c = LNCtx()
            c.g = PF.take(1024)
            c.b = PF.take(1024)
            c.gb = Buf("lngb")
            P.dma([(c.g, g_src.partition_broadcast(128)[:, 0, :]), (c.b, b_src.partition_broadcast(128)[:, 0, :])],
                  writes=[c.gb], owner=c.gb)
            c.y = [(PF.take(1024), Buf("lny%d" % i)) for i in range(2)] if need_y else None
            c.xo = [(PF.take(1024), Buf("lnxo%d" % i)) for i in range(2)]
            c.xb = [(PB.take(1024), Buf("lnxb%d" % i)) for i in range(2)]
            c.st = [(lnst[:, i * 16:(i + 1) * 16], Buf("lnst%d" % i)) for i in range(2)]
            c.k = 0
            return c

        def ln_tile(c, y_ap, y_buf, t, psT_ap, psT_buf, final=False):
            k = c.k
            c.k += 1
            st_ap, st_buf = c.st[k % 2]
            xo_ap, xo_buf = c.xo[k % 2]
            xb_ap, xb_buf = c.xb[k % 2]
            P.op("dve", [I("bn_stats", st_ap[:, 0:6], y_ap[:, 0:512]),
                         I("bn_stats", st_ap[:, 6:12], y_ap[:, 512:1024])], reads=[y_buf], writes=[st_buf])
            P.op("dve", I("bn_aggr", st_ap[:, 12:14], st_ap[:, 0:12]), reads=[st_buf], writes=[st_buf])
            P.op("act", I("activation", st_ap[:, 14:15], st_ap[:, 13:14], AF.Sqrt, bias=epsln[:, 0:1], scale=1.0), reads=[st_buf, cbuf], writes=[st_buf])
            P.op("dve", I("reciprocal", st_ap[:, 14:15], st_ap[:, 14:15]), reads=[st_buf], writes=[st_buf])
            P.op("dve", I("tensor_scalar", y_ap, y_ap, st_ap[:, 12:13], st_ap[:, 14:15], op0=ALU.subtract, op1=ALU.mult),
                 reads=[st_buf, y_buf], writes=[y_buf])
            P.op("pool", I("tensor_tensor", xo_ap, y_ap, c.g, ALU.mult), reads=[y_buf, c.gb], writes=[xo_buf])
            P.op("pool", I("tensor_tensor", xo_ap, xo_ap, c.b, ALU.add), reads=[xo_buf, c.gb], writes=[xo_buf])
            pairs = [(XTOK[t * 128:(t + 1) * 128, :], xo_ap)]
            if final:
                pairs = [(out[t * 128:(t + 1) * 128, :], xo_ap)]
            P.dma(pairs, reads=[xo_buf], owner=xo_buf, kind="s")
            if final:
                return
            P.op("act", I("copy", xb_ap, xo_ap), reads=[xo_buf], writes=[xb_buf])
            psb = psT_ap.bitcast(BF16)
            P.op("pe", [I("transpose", psb[:, kc * 128:(kc + 1) * 128], xb_ap[:, kc * 128:(kc + 1) * 128], ident[:, :])
                        for kc in range(KC)], reads=[xb_buf, cbuf], writes=[psT_buf])
            P.op("act", I("copy", xT[:, :, t * 128:(t + 1) * 128], psb.rearrange("p (a n) -> p a n", n=128)),
                 reads=[psT_buf], writes=[])

        cbuf = Buf("consts")
        P.dma([(ident[:, :], c_ident), (onesb[:, :], c_onesb), (onesf[:, :], c_onesf), (rd[:, :], c_rd), (rm[:, :], c_rm),
               (rk[:, :], c_rk)], writes=[cbuf], owner=cbuf)
        P.op("pool", [I("memset", epsln[:, :], LN_EPS), I("memset", epsrms[:, :], RMS_EPS)], writes=[cbuf])
        sbuf_ = Buf("smalls")
        pairs = []
        for l in range(NL):
            pairs += [(bg_t[:, l * 24:(l + 1) * 24], bgate[l]), (gq_t[:, l * 3:(l + 1) * 3], gq_in[l]),
                      (gkv_t[:, l * 2:(l + 1) * 2], gkv_in[l]), (gsc_t[:, l:l + 1], subg[l]),
                      (lamw[:, l * 256:(l + 1) * 256], dlam[l].partition_broadcast(128)[:, 0, :])]
        P.dma(pairs, writes=[sbuf_], owner=sbuf_)
        lb = Buf("lam")
        for l in range(NL):
            lam_init = 0.8 - 0.6 * math.exp(-0.3 * l)
            lw = lamw[:, l * 256:(l + 1) * 256].rearrange("p (a b) -> p a b", b=64)
            lp = lamp[:, l * 128:(l + 1) * 128].rearrange("p (a b) -> p a b", b=64)
            P.op("dve", I("tensor_tensor", lp, lw[:, 0:4:2, :], lw[:, 1:4:2, :], ALU.mult), reads=[sbuf_], writes=[lb])
            P.op("dve", I("reduce_sum", lams[:, l * 2:(l + 1) * 2], lp, AX.X), reads=[lb], writes=[lb])
            P.op("act", I("activation", lame[:, l * 2:(l + 1) * 2], lams[:, l * 2:(l + 1) * 2], AF.Exp), reads=[lb], writes=[lb])
            P.op("dve", I("tensor_tensor", nlam_t[:, l:l + 1], lame[:, l * 2 + 1:l * 2 + 2], lame[:, l * 2:l * 2 + 1], ALU.subtract),
                 reads=[lb], writes=[lb])
            P.op("dve", I("tensor_scalar", nlam_t[:, l:l + 1], nlam_t[:, l:l + 1], -lam_init, None, op0=ALU.add),
                 reads=[lb], writes=[lb])
            P.op("dve", I("tensor_scalar", gsc_t[:, l:l + 1], gsc_t[:, l:l + 1], 1.0 - lam_init, None, op0=ALU.mult),
                 reads=[lb, sbuf_], writes=[lb])

        PB.reset(); PF.reset()
        c = ln_setup(ln_in_g[0:1, :], ln_in_b[0:1, :])
        psT = [(bank(i), Buf("psT%d" % i)) for i in range(2)]
        for t in range(T // 128):
            y_ap, y_buf = c.y[t % 2]
            P.dma([(y_ap, x_in[t * 128:(t + 1) * 128, :])], writes=[y_buf], owner=y_buf)
            ln_tile(c, y_ap, y_buf, t, psT[t % 2][0], psT[t % 2][1])
        P.barrier()

        for l in range(n_layers):
            if stop_after == "0":
                break
            last_layer = (l == NL - 1)
            PB.reset(); PF.reset()
            wstage = [(PF.take(2048), Buf("wstg%d" % i)) for i in range(2)]
            wst = [(PB.take(2048).rearrange("p (a n) -> p a n", n=256), Buf("wst%d" % i)) for i in range(3)]
            stg = [(PB.take(512), Buf("stg%d" % i)) for i in range(4)]
            stgf = [(PF.take(512), Buf("stgf%d" % i)) for i in range(2)]
            xbr = [(PB.take(512), Buf("xbr%d" % i)) for i in range(2)]
            t1s = [(PF.take(512), Buf("t1s%d" % i)) for i in range(2)]
            t2s = [(PF.take(512), Buf("t2s%d" % i)) for i in range(2)]
            css = [(PF.take(1024), Buf("css%d" % i)) for i in range(2)]
            psA = [(bank(i), Buf("psA%d" % i)) for i in range(6)]
            ps2 = [(bank(6 + i), Buf("ps2%d" % i)) for i in range(2)]
            cnt = dict(item=0, stg=0, stgf=0, rope=0)

            strips = []
            for e0 in range(0, 1024, 256):
                strips.append(("fm", e0))
            strips.append(("v", 1024)); strips.append(("v", 1280))
            for e0 in range(1536, 2560, 256):
                strips.append(("fm", e0))
            strips.append(("v", 2560)); strips.append(("v", 2816))
            for e0 in range(3072, 3584, 256):
                strips.append(("fm", e0))
            strips.append(("fm", 3584))
            for e0 in range(3744, DIN, 256):
                strips.append(("gate", e0))

            def chunk_dst(e0, wd):
                if e0 < 512:
                    return "plain", NQ[e0 // 128]
                if e0 < 1024:
                    return "plain", NK[(e0 - 512) // 128]
                if 1536 <= e0 < 2048:
                    return "rope", DQ[(e0 - 1536) // 128]
                if 2048 <= e0 < 2560:
                    return "rope", DK[(e0 - 2048) // 128]
                if 3072 <= e0 < 3712:
                    return "f32", MC[(e0 - 3072) // 128]
                if e0 == 3712:
                    return "f32", MC[5]
                if e0 >= 3744:
                    return "gate", GT[(e0 - 3744) // 128]
                raise AssertionError(e0)

            def load_strip(si_):
                kind, e0 = strips[si_]
                ncol = 256
                if kind == "fm" and e0 == 3584:
                    ncol = 160
                w_ap, w_buf = wst[si_ % 3]
                src = w_in[l][:, e0:e0 + ncol].rearrange("(kc p) e -> p kc e", p=128)
                wload(src, w_ap[:, :, 0:ncol], w_buf, wstage)

            load_strip(0)
            load_strip(1)
            for si_, (kind, e0) in enumerate(strips):
                if si_ + 2 < len(strips):
                    load_strip(si_ + 2)
                w_ap, w_buf = wst[si_ % 3]
                if kind == "v":
                    vdst = NV if e0 < 1536 else DV
                    c0 = (e0 - 1024) if e0 < 1536 else (e0 - 2560)
                    for t in range(T // 128):
                        ps_ap, ps_buf = psA[cnt["item"] % 6]
                        cnt["item"] += 1
                        P.op("pe", [I("matmul",
                            ps_ap[:, 0:256], lhsT=xT[:, kc, t * 128:(t + 1) * 128], rhs=w_ap[:, kc, :], start=(kc == 0), stop=(kc == KC - 1))
                            for kc in range(KC)], reads=[w_buf], writes=[ps_buf])
                        s_ap, s_buf = stg[cnt["stg"] % 4]
                        cnt["stg"] += 1
                        P.op("act", I("copy", s_ap[:, 0:256], ps_ap[:, 0:256]), reads=[ps_buf], writes=[s_buf])
                        P.dma([(vdst[t * 128:(t + 1) * 128, c0:c0 + 256], s_ap[:, 0:256])], reads=[s_buf], owner=s_buf, kind="s")
                    continue
                if kind == "fm" and e0 == 3584:
                    chunks = [(3584, 128, 0), (3712, 32, 128)]
                else:
                    chunks = [(e0, 128, 0), (e0 + 128, 128, 128)]
                for (ce0, wd, coff) in chunks:
                    ckind, dst = chunk_dst(ce0, wd)
                    for tb in range(8):
                        ps_ap, ps_buf = psA[cnt["item"] % 6]
                        cnt["item"] += 1
                        P.op("pe", [I("matmul",
                            ps_ap[0:wd, :], lhsT=w_ap[:, kc, coff:coff + wd], rhs=xT[:, kc, tb * 512:(tb + 1) * 512],
                            start=(kc == 0), stop=(kc == KC - 1)) for kc in range(KC)], reads=[w_buf], writes=[ps_buf])
                        dsl = dst[0:wd, tb * 512:(tb + 1) * 512]
                        if ckind == "plain":
                            s_ap, s_buf = stg[cnt["stg"] % 4]; cnt["stg"] += 1
                            P.op("act", I("copy", s_ap, ps_ap), reads=[ps_buf], writes=[s_buf])
                            P.dma([(dsl, s_ap)], reads=[s_buf], owner=s_buf, kind="s")
                        elif ckind == "gate":
                            gi = (ce0 - 3744) // 128
                            s_ap, s_buf = stg[cnt["stg"] % 4]; cnt["stg"] += 1
                            P.op("act", I("activation",
                                s_ap, ps_ap, AF.Sigmoid, bias=bg_t[:, l * 24 + gi:l * 24 + gi + 1]), reads=[ps_buf], writes=[s_buf])
                            P.dma([(dsl, s_ap)], reads=[s_buf], owner=s_buf, kind="s")
                        elif ckind == "f32":
                            s_ap, s_buf = stgf[cnt["stgf"] % 2]; cnt["stgf"] += 1
                            P.op("act", I("copy", s_ap[0:wd, :], ps_ap[0:wd, :]), reads=[ps_buf], writes=[s_buf])
                            P.dma([(dsl, s_ap[0:wd, :])], reads=[s_buf], owner=s_buf, kind="s")
                        else:
                            k = cnt["rope"]; cnt["rope"] += 1
                            cs_ap, cs_buf = css[k % 2]
                            P.dma([(cs_ap[:, 0:512], c_cosd[:, tb * 512:(tb + 1) * 512]), (cs_ap[:, 512:1024], c_sind[:, tb * 512:(tb + 1) * 512])],
                                  writes=[cs_buf], owner=cs_buf)
                            s_ap, s_buf = stg[cnt["stg"] % 4]; cnt["stg"] += 1
                            rope_epi(ps_ap, ps_buf, 128, cs_ap[:, 0:512], cs_ap[:, 512:1024], cs_buf, rd, ps2[k % 2][0], ps2[k % 2][1],
                                     xbr[k % 2], t1s[k % 2], t2s[k % 2], s_ap, s_buf)
                            P.dma([(dsl, s_ap)], reads=[s_buf], owner=s_buf, kind="s")
            P.barrier()
            if stop_after == "A":
                break

            PB.reset(); PF.reset()
            wstage = [(PF.take(2048), Buf("wstg%d" % i)) for i in range(1)]
            wqb = PB.take(3 * 800).rearrange("p (a n) -> p a n", n=800); wqb_buf = Buf("wqb")
            zt_ap = PB.take(512); zt_buf = Buf("zt")
            P.op("pool", [I("memset", wqb[:, :, 768:800], 0.0), I("memset", zt_ap, 0.0)], writes=[wqb_buf, zt_buf])
            wkvb = PB.take(2 * 1024).rearrange("p (a n) -> p a n", n=1024); wkvb_buf = Buf("wkvb")
            wload(w_qb[l].rearrange("(kc p) e -> p kc e", p=128), wqb[:, :, 0:768], wqb_buf, wstage)
            wload(w_kvb[l].rearrange("(kc p) e -> p kc e", p=128), wkvb, wkvb_buf, wstage)
            wkvb_v = wkvb.rearrange("p a (h two d) -> p a h two d", two=2, d=64)
            cq = [(PF.take(5 * 512).rearrange("p (a n) -> p a n", n=512), Buf("cq%d" % i)) for i in range(1)]
            krs = [(PF.take(512), Buf("krs%d" % i)) for i in range(1)]
            sq = (PF.take(512), Buf("sq"))
            rstd = [(PF.take(512), Buf("rstd%d" % i)) for i in range(2)]
            cn = (PB.take(5 * 512).rearrange("p (a n) -> p a n", n=512), Buf("cn"))
            hlb = (PB.take(1024).rearrange("p (a n) -> p a n", n=512), Buf("hlb"))
            css = [(PF.take(1024), Buf("cssm%d" % i)) for i in range(1)]
            csk = [(PF.take(1024), Buf("csk%d" % i)) for i in range(1)]
            xbr = [(PB.take(512), Buf("xbr%d" % i)) for i in range(2)]
            t1s = [(PF.take(512), Buf("t1s%d" % i)) for i in range(1)]
            t2s = [(PF.take(512), Buf("t2s%d" % i)) for i in range(1)]
            stg = [(PB.take(512), Buf("stg%d" % i)) for i in range(4)]
            psA = [(bank(i), Buf("psA%d" % i)) for i in range(4)]
            ps2 = [(bank(4 + i), Buf("ps2%d" % i)) for i in range(2)]
            psS = [(bank(6 + i), Buf("psS%d" % i)) for i in range(2)]
            cnt = dict(item=0, stg=0, rope=0)
            for tb in range(int(os.environ.get('DBG_BTB', '8'))):
                tsl = slice(tb * 512, (tb + 1) * 512)
                cq_ap, cq_buf = cq[0]
                kr_ap, kr_buf = krs[0]
                P.dma([(cq_ap, MC[0:5, :, tsl].rearrange("c p t -> p c t"))], writes=[cq_buf], owner=cq_buf)
                P.dma([(kr_ap[0:32, :], MC[5, 0:32, tsl])], writes=[kr_buf], owner=kr_buf)
                cs_ap, cs_buf = css[0]
                P.dma([(cs_ap[:, 0:512], c_cosm[:, tsl]), (cs_ap[:, 512:1024], c_sinm[:, tsl])], writes=[cs_buf], owner=cs_buf)
                ck_ap, ck_buf = csk[0]
                P.dma([(ck_ap[0:32, 0:512], c_cosk[:, tsl]), (ck_ap[0:32, 512:1024], c_sink[:, tsl])], writes=[ck_buf], owner=ck_buf)
                cn_ap, cn_buf = cn
                hl_ap, hl_buf = hlb
                for (c0, ncn, gt, goff, nfeat) in ((0, 3, gq_t, l * 3, 384.0), (3, 2, gkv_t, l * 2, 256.0)):
                    pS_ap, pS_buf = psS[0 if c0 == 0 else 1]
                    sq_ap, sq_buf = sq
                    for ci in range(ncn):
                        P.op("act", I("activation", sq_ap, cq_ap[:, c0 + ci, :], AF.Square), reads=[cq_buf], writes=[sq_buf])
                        P.op("pool", I("tensor_copy", hl_ap[:, 0, :], sq_ap), reads=[sq_buf], writes=[hl_buf])
                        P.op("pool", I("tensor_tensor", hl_ap[:, 1, :], sq_ap, hl_ap[:, 0, :], ALU.subtract), reads=[sq_buf, hl_buf], writes=[hl_buf])
                        P.op("pe", [I("matmul", pS_ap, lhsT=onesb[:, :], rhs=hl_ap[:, 0, :], start=(ci == 0), stop=False),
                                    I("matmul", pS_ap, lhsT=onesb[:, :], rhs=hl_ap[:, 1, :], start=False, stop=(ci == ncn - 1))],
                             reads=[hl_buf], writes=[pS_buf])
                    r_ap, r_buf = rstd[0 if c0 == 0 else 1]
                    P.op("act", I("activation", r_ap, pS_ap, AF.Sqrt, bias=epsrms[:, 0:1], scale=1.0 / nfeat), reads=[pS_buf], writes=[r_buf])
                    P.op("dve", I("reciprocal", r_ap, r_ap), reads=[r_buf], writes=[r_buf])
                    for ci in range(ncn):
                        P.op("dve", I("scalar_tensor_tensor",
                            cn_ap[:, c0 + ci, :], cq_ap[:, c0 + ci, :], gt[:, goff + ci:goff + ci + 1], r_ap, op0=ALU.mult, op1=ALU.mult),
                            reads=[cq_buf, r_buf], writes=[cn_buf])
                for h in range(int(os.environ.get('DBG_QH', '8')) if 'q' in DBGB else 0):
                    ps_ap, ps_buf = psA[cnt["item"] % 4]; cnt["item"] += 1
                    P.op("pe", [I("matmul", ps_ap, lhsT=wqb[:, ci, h * 96:h * 96 + 128], rhs=cn_ap[:, ci, :],
                                                                        start=(ci == 0), stop=(ci == 2)) for ci in range(3)],
                         reads=[cn_buf, wqb_buf], writes=[ps_buf])
                    k = cnt["rope"]; cnt["rope"] += 1
                    s_ap, s_buf = stg[cnt["stg"] % 4]; cnt["stg"] += 1
                    if os.environ.get('DBG_NOROPE'):
                        P.op("act", I("copy", s_ap, ps_ap), reads=[ps_buf], writes=[s_buf])
                    else:
                        rope_epi(ps_ap, ps_buf, int(os.environ.get('DBG_ROWS', '128')), cs_ap[:, 0:512], cs_ap[:, 512:1024], cs_buf, (rd if os.environ.get('DBG_RD') else rm), ps2[k % 2][0], ps2[k % 2][1],
                                 xbr[k % 2], t1s[0], t2s[0], s_ap, s_buf)
                    P.dma([(MQ[h, :, tsl], s_ap)], reads=[s_buf], owner=s_buf, kind="s")
                for h in range(8 if 'k' in DBGB else 0):
                    ps_ap, ps_buf = psA[cnt["item"] % 4]; cnt["item"] += 1
                    P.op("pe", [I("matmul", ps_ap[0:64, :], lhsT=wkvb[:, ci, h * 128:h * 128 + 64], rhs=cn_ap[:, 3 + ci, :],
                                                                        start=(ci == 0), stop=(ci == 1)) for ci in range(2)],
                         reads=[cn_buf, wkvb_buf], writes=[ps_buf])
                    s_ap, s_buf = stg[cnt["stg"] % 4]; cnt["stg"] += 1
                    P.op("act", I("copy", s_ap[0:64, :], ps_ap[0:64, :]), reads=[ps_buf], writes=[s_buf])
                    P.dma([(MK[h, 0:64, tsl], s_ap[0:64, :])], reads=[s_buf], owner=s_buf, kind="s")
                for s in range(4 if 'v' in DBGB else 0):
                    ps_ap, ps_buf = psA[cnt["item"] % 4]; cnt["item"] += 1
                    P.op("pe", [I("matmul", ps_ap.rearrange("p (h d) -> p h d", d=64), lhsT=cn_ap[:, 3 + ci, s * 128:(s + 1) * 128],
                                                                        rhs=wkvb_v[:, ci, :, 1, :], start=(ci == 0), stop=(ci == 1)) for ci in range(2)],
                         reads=[cn_buf, wkvb_buf], writes=[ps_buf])
                    s_ap, s_buf = stg[cnt["stg"] % 4]; cnt["stg"] += 1
                    P.op("act", I("copy", s_ap, ps_ap), reads=[ps_buf], writes=[s_buf])
                    tt = tb * 4 + s
                    P.dma([(MV[tt * 128:(tt + 1) * 128, :], s_ap)], reads=[s_buf], owner=s_buf, kind="s")
                if 'r' not in DBGB:
                    continue
                k = cnt["rope"]; cnt["rope"] += 1
                s_ap, s_buf = stg[cnt["stg"] % 4]; cnt["stg"] += 1
                rope_epi(kr_ap, kr_buf, 32, ck_ap[:, 0:512], ck_ap[:, 512:1024], ck_buf, rk, ps2[k % 2][0], ps2[k % 2][1],
                         xbr[k % 2], t1s[0], t2s[0], s_ap, s_buf)
                P.dma([(MK[h, 64:96, tsl], s_ap[0:32, :]) for h in range(8)], reads=[s_buf], owner=s_buf, kind="s")
                P.dma([(MK[h, 96:128, tsl], zt_ap[0:32, :]) for h in range(8)], reads=[zt_buf], owner=zt_buf, kind="s")
            P.barrier()
            if stop_after == "B":
                break

            PB.reset(); PF.reset()
            hq = [(PB.take(T), Buf("hq%d" % i)) for i in range(2)]
            hk = [(PB.take(T), Buf("hk%d" % i)) for i in range(2)]
            hv = [(PB.take(32 * 128).rearrange("p (a n) -> p a n", n=128), Buf("hv%d" % i)) for i in range(2)]
            pts = [(PB.take(1024).rearrange("p (a n) -> p a n", n=512), Buf("pt%d" % i)) for i in range(4)]
            stg = [(PB.take(512), Buf("stg%d" % i)) for i in range(2)]
            pss = [(PSt[i][:, :].rearrange("p (a n) -> p a n", n=512), Buf("pss%d" % i)) for i in range(3)]
            bk6 = Buf("bk6"); bk7 = Buf("bk7")
            accO = [(bank(6), bk6)]
            accS = [(bank(7), bk7)]
            pend = []
            rr = [(PF.take(512), Buf("rr%d" % i)) for i in range(2)]
            a1 = (PF.take(512), Buf("a1"))
            a2 = (PF.take(512), Buf("a2"))
            oo = (PF.take(512), Buf("oo"))
            sqq = (PF.take(512), Buf("sqq"))
            hld = (PB.take(1024).rearrange("p (a n) -> p a n", n=512), Buf("hld"))
            cnt = dict(g=0, acc=0, pt=0, stg=0, rr=0)

            padd = [(PB.take(512), Buf("padd%d" % i)) for i in range(3)]
            accM = [(bank(6), bk6), (bank(7), bk7)]
            rrow = [(PF.take(512), Buf("rrow%d" % i)) for i in range(2)]
            rsb = [Buf("rsb%d" % i) for i in range(2)]

            def attn_pass(q_ap, k_ap, q_buf, k_buf, v_ap, v_buf, kd, dv, scale, qb, mla=False):
                if mla:
                    ai = cnt["acc"] % 2; cnt["acc"] += 1
                    O_ap, O_buf = accM[ai]
                    S_ap, S_buf = None, None
                else:
                    ai = 0
                    O_ap, O_buf = accO[ai]
                    S_ap, S_buf = accS[ai]
                NG = 16
                gl = []

                def emit_S(g):
                    ps_ap, ps_buf = pss[(cnt["g"] + g) % 3]
                    P.op("pe", [I("matmul", ps_ap[:, j, :], lhsT=k_ap[0:kd, (2 * g + j) * 128:(2 * g + j + 1) * 128],
                                                                       rhs=q_ap[0:kd, qb * 512:(qb + 1) * 512], start=True, stop=True) for j in range(2)],
                         reads=[q_buf, k_buf], writes=[ps_buf])

                emit_S(0)
                emit_S(1)
                emit_S(2)
                for f in pend:
                    f()
                del pend[:]
                for g in range(NG):
                    ps_ap, ps_buf = pss[(cnt["g"] + g) % 3]
                    pt_ap, pt_buf = pts[cnt["pt"] % 4]; cnt["pt"] += 1
                    P.op("act", I("activation", pt_ap, ps_ap, AF.Exp, scale=scale), reads=[ps_buf], writes=[pt_buf])
                    fns = []
                    for j in range(2):
                        first = (g == 0 and j == 0)
                        lastf = (g == NG - 1 and j == 1)
                        fns.append(I("matmul", O_ap[0:dv, :], lhsT=v_ap[:, 2 * g + j, 0:dv], rhs=pt_ap[:, j, :], start=first, stop=lastf))
                    if mla:
                        P.op("pe", fns, reads=[pt_buf, v_buf], writes=[O_buf])
                        if g + 3 < NG:
                            emit_S(g + 3)
                    else:
                        pa_ap, pa_buf = padd[cnt["pt"] % 3]
                        P.op("dve", I("tensor_tensor", pa_ap, pt_ap[:, 0, :], pt_ap[:, 1, :], ALU.add), reads=[pt_buf], writes=[pa_buf])
                        P.op("pe", fns, reads=[pt_buf, v_buf], writes=[O_buf])
                        if g + 3 < NG:
                            emit_S(g + 3)
                        P.op("pe", I("matmul", S_ap, lhsT=onesb[:, :], rhs=pa_ap, start=(g == 0), stop=(g == NG - 1)), reads=[pa_buf], writes=[S_buf])
                cnt["g"] += NG
                return ai

            heads = [("d", h) for h in range(int(os.environ.get('DBG_ND', '4')))] + [("m", h) for h in range(int(os.environ.get('DBG_NM', '8')))]

            def load_head(i):
                kind, h = heads[i]
                q_ap, q_buf = hq[i % 2]; k_ap, k_buf = hk[i % 2]; v_ap, v_buf = hv[i % 2]
                if kind == "d":
                    P.dma([(q_ap, DQ[h])], writes=[q_buf], owner=q_buf)
                    P.dma([(k_ap, DK[h])], writes=[k_buf], owner=k_buf)
                    P.dma([(v_ap, DV[:, h * 128:(h + 1) * 128].rearrange("(t p) d -> p t d", p=128))], writes=[v_buf], owner=v_buf)
                else:
                    P.dma([(q_ap, MQ[h])], writes=[q_buf], owner=q_buf)
                    P.dma([(k_ap, MK[h])], writes=[k_buf], owner=k_buf)
                    P.dma([(v_ap[:, :, 0:64], MV[:, h * 64:(h + 1) * 64].rearrange("(t p) d -> p t d", p=128))], writes=[v_buf], owner=v_buf)
                    P.op("pool", I("memset", v_ap[:, :, 64:65], 1.0), writes=[v_buf])

            load_head(0)
            for i, (kind, h) in enumerate(heads):
                if i + 1 < len(heads):
                    load_head(i + 1)
                q_ap, q_buf = hq[i % 2]; k_ap, k_buf = hk[i % 2]; v_ap, v_buf = hv[i % 2]
                for qb in range(int(os.environ.get('DBG_NQB', '8'))):
                    tsl = slice(qb * 512, (qb + 1) * 512)
                    if kind == "d":
                        res = []
                        for comp in range(2):
                            ai = attn_pass(q_ap[comp * 64:(comp + 1) * 64, :], k_ap[comp * 64:(comp + 1) * 64, :], q_buf, k_buf,
                                           v_ap, v_buf, 64, 128, 0.125, qb)
                            O_ap, O_buf = accO[ai]; S_ap, S_buf = accS[ai]
                            r_ap, r_buf = rr[cnt["rr"] % 2]; cnt["rr"] += 1
                            a_ap, a_buf = (a1, a2)[comp]
                            P.op("act", I("activation", r_ap, S_ap, AF.Ln), reads=[S_buf], writes=[r_buf])
                            P.op("act", I("activation", r_ap, r_ap, AF.Exp, scale=-1.0), reads=[r_buf], writes=[r_buf])
                            P.op("dve", I("tensor_tensor", a_ap, O_ap, r_ap, ALU.mult),
                                 reads=[O_buf, r_buf], writes=[a_buf])
                            res.append(ai)
                        ai = res[1]
                        S_ap, S_buf = accS[ai]
                        o_ap, o_buf = oo
                        s_ap2, s_buf2 = sqq
                        P.op("dve", I("scalar_tensor_tensor", o_ap, a2[0], nlam_t[:, l:l + 1], a1[0], op0=ALU.mult, op1=ALU.add),
                             reads=[a1[1], a2[1]], writes=[o_buf])
                        P.op("dve", I("tensor_tensor", hld[0][:, 0, :], o_ap, o_ap, ALU.mult), reads=[o_buf], writes=[hld[1]])

                        def tail(S_ap=S_ap, S_buf=S_buf, o_ap=o_ap, o_buf=o_buf, h=h, tsl=tsl, l=l):
                            P.op("pe", I("matmul", S_ap, lhsT=onesb[:, :], rhs=hld[0][:, 0, :], start=True, stop=True), reads=[hld[1]], writes=[S_buf])
                            r_ap, r_buf = rr[cnt["rr"] % 2]; cnt["rr"] += 1
                            P.op("act", I("activation", r_ap, S_ap, AF.Ln, bias=epsrms[:, 0:1], scale=1.0 / 128.0), reads=[S_buf], writes=[r_buf])
                            P.op("act", I("activation", r_ap, r_ap, AF.Exp, scale=-0.5), reads=[r_buf], writes=[r_buf])
                            st_ap, st_buf = stg[cnt["stg"] % 2]; cnt["stg"] += 1
                            P.op("dve", I("scalar_tensor_tensor", st_ap, o_ap, gsc_t[:, l:l + 1], r_ap, op0=ALU.mult, op1=ALU.mult),
                                 reads=[o_buf, r_buf], writes=[st_buf])
                            P.dma([(AO[1, h, :, tsl], st_ap)], reads=[st_buf], owner=st_buf, kind="s")
                        pend.append(tail)
                    else:
                        ai = attn_pass(q_ap, k_ap, q_buf, k_buf, v_ap, v_buf, 128, 65, 96.0 ** -0.5, qb, mla=True)
                        O_ap, O_buf = accM[ai]
                        r_ap, r_buf = rr[cnt["rr"] % 2]; cnt["rr"] += 1
                        rw_ap, rw_buf = rrow[ai]
                        rs_buf = rsb[ai]
                        P.op("act", I("activation", rw_ap[64:65, :], O_ap[64:65, :], AF.Ln), reads=[O_buf], writes=[rw_buf])
                        P.op("act", I("activation", rw_ap[64:65, :], rw_ap[64:65, :], AF.Exp, scale=-1.0), reads=[rw_buf], writes=[rw_buf])
                        P.dma([(RS[ai:ai + 1, :], rw_ap[64:65, :])], reads=[rw_buf], writes=[rs_buf], owner=rw_buf, kind="s")
                        P.dma([(r_ap[0:64, :], RS[ai:ai + 1, :].partition_broadcast(64)[:, 0, :])], reads=[rs_buf], writes=[r_buf], owner=r_buf)
                        st_ap, st_buf = stg[cnt["stg"] % 2]; cnt["stg"] += 1
                        P.op("dve", I("tensor_tensor", st_ap[0:64, :], O_ap[0:64, :], r_ap[0:64, :], ALU.mult),
                             reads=[O_buf, r_buf], writes=[st_buf])
                        P.dma([(AO[2, h // 2, (h % 2) * 64:(h % 2) * 64 + 64, tsl], st_ap[0:64, :])], reads=[st_buf], owner=st_buf, kind="s")
            for f in pend:
                f()
            del pend[:]
            P.barrier()
            if stop_after == "CD":
                break

            PB.reset(); PF.reset()
            nq = [(PB.take(T), Buf("nq%d" % i)) for i in range(2)]
            nk = [(PB.take(T), Buf("nk%d" % i)) for i in range(2)]
            nve = [(PB.take(32 * 64).rearrange("p (a n) -> p a n", n=64), Buf("nve%d" % i)) for i in range(2)]
            nvo = [(PB.take(32 * 64).rearrange("p (a n) -> p a n", n=64), Buf("nvo%d" % i)) for i in range(2)]
            npt = [(PB.take(1024).rearrange("p (r j n) -> p r j n", r=4, j=4), Buf("npt%d" % i)) for i in range(3)]
            nout = [(PB.take(T), Buf("nout%d" % i)) for i in range(2)]
            tab = [(PF.take(14 * 64).rearrange("p (s n) -> p s n", n=64), Buf("tab%d" % i)) for i in range(2)]
            msk = (PF.take(14 * 64).rearrange("p (s n) -> p s n", n=64), Buf("msk"))
            tmp = [(PF.take(1024).rearrange("p (r j n) -> p r j n", r=4, j=4), Buf("tmp%d" % i)) for i in range(3)]
            nrr = [(PF.take(256), Buf("nrr%d" % i)) for i in range(2)]
            tabi = [PF.take(256) for i in range(2)]
            pssn = [(PSt[i][:, :].rearrange("p (r j n) -> p r j n", r=4, j=4), Buf("pssn%d" % i)) for i in range(3)]
            naOS = [(bank(6 + i), Buf("naOS%d" % i)) for i in range(2)]
            P.dma([(msk[0].rearrange("p s n -> p (s n)"), namask)], writes=[msk[1]], owner=msk[1])
            it = 0
            for cch in range(4):
                q_ap, q_buf = nq[cch % 2]; k_ap, k_buf = nk[cch % 2]
                P.dma([(q_ap, NQ[cch])], writes=[q_buf], owner=q_buf)
                P.dma([(k_ap, NK[cch])], writes=[k_buf], owner=k_buf)
                for hh in range(2):
                    h = cch * 2 + hh
                    hi = h % 2
                    ve_ap, ve_buf = nve[hi]; vo_ap, vo_buf = nvo[hi]
                    P.dma([(ve_ap, NV[:, h * 64:(h + 1) * 64].rearrange("(t p) d -> p t d", p=128))], writes=[ve_buf], owner=ve_buf)
                    P.dma([(vo_ap[:, 0:31, :], NV[64:T - 64, h * 64:(h + 1) * 64].rearrange("(t p) d -> p t d", p=128))], writes=[vo_buf], owner=vo_buf)
                    tb_ap, tb_buf = tab[hi]
                    src = bass.AP(tensor=nabias.tensor, offset=nabias[l, h].offset, ap=[[64, 128], [64 * 64, 14], [1, 64]])
                    P.dma([(tb_ap, src)], writes=[tb_buf], owner=tb_buf)
                    P.op("pool", I("tensor_tensor", tb_ap, tb_ap, msk[0], ALU.add), reads=[tb_buf, msk[1]], writes=[tb_buf])
                    ti_ap = tabi[hi]
                    P.op("pool", I("tensor_copy", ti_ap.rearrange("p (j n) -> p j n", n=64), tb_ap[:, 3:10:2, :]), reads=[tb_buf], writes=[tb_buf])
                    o_ap, o_buf = nout[hi]
                    qh = q_ap[hh * 64:(hh + 1) * 64, :]
                    kh = k_ap[hh * 64:(hh + 1) * 64, :]
                    def rows_of(rg):
                        return [(rg * 4 + ri, min(max(rg * 4 + ri - 4, 0), 56)) for ri in range(4)]

                    def na_S(rg, k):
                        ps_ap, ps_buf = pssn[k % 3]
                        fns = []
                        for ri, (r, rs) in enumerate(rows_of(rg)):
                            for j in range(4):
                                k0 = (rs + 2 * j) * 64
                                fns.append(I("matmul", ps_ap[:, ri, j, :], lhsT=kh[:, k0:k0 + 128], rhs=qh[:, r * 64:(r + 1) * 64], start=True, stop=True))
                        P.op("pe", fns, reads=[q_buf, k_buf], writes=[ps_buf])

                    def na_bias(rg, k):
                        ps_ap, ps_buf = pssn[k % 3]
                        tm_ap, tm_buf = tmp[k % 3]
                        offs = [rs - r + 7 for (r, rs) in rows_of(rg)]
                        if all(o == 3 for o in offs):
                            bc = bass.AP(tensor=ti_ap.tensor, offset=ti_ap.offset, ap=[list(ti_ap.ap[0]), [0, 4], [1, 256]])
                            P.op("dve", I("scalar_tensor_tensor", tm_ap.rearrange("p r j n -> p r (j n)"), ps_ap.rearrange("p r j n -> p r (j n)"), 0.125, bc,
                                          op0=ALU.mult, op1=ALU.add), reads=[ps_buf, tb_buf], writes=[tm_buf])
                        else:
                            for ri, off in enumerate(offs):
                                P.op("dve", I("scalar_tensor_tensor", tm_ap[:, ri, :, :], ps_ap[:, ri, :, :], 0.125, tb_ap[:, off:off + 7:2, :],
                                              op0=ALU.mult, op1=ALU.add), reads=[ps_buf, tb_buf], writes=[tm_buf])

                    def na_fin(rg, k):
                        OS_ap, OS_buf = naOS[k % 2]
                        O_ap = OS_ap[:, 0:256]; S_ap = OS_ap[:, 256:512]
                        r_ap, r_buf = nrr[k % 2]
                        P.op("act", I("activation", r_ap[0:64, :], S_ap[0:64, :], AF.Ln), reads=[OS_buf], writes=[r_buf])
                        P.op("act", I("activation", r_ap[0:64, :], r_ap[0:64, :], AF.Exp, scale=-1.0), reads=[r_buf], writes=[r_buf])
                        P.op("dve", I("tensor_tensor", o_ap[0:64, rg * 256:(rg + 1) * 256], O_ap[0:64, :], r_ap[0:64, :], ALU.mult),
                             reads=[OS_buf, r_buf], writes=[o_buf])

                    for g0 in range(3):
                        na_S(g0, it + g0)
                    na_bias(0, it)
                    for rg in range(16):
                        k = it + rg
                        tm_ap, tm_buf = tmp[k % 3]
                        pt_ap, pt_buf = npt[k % 3]
                        OS_ap, OS_buf = naOS[k % 2]
                        O_ap = OS_ap[:, 0:256]; S_ap = OS_ap[:, 256:512]
                        rows = rows_of(rg)
                        P.op("act", I("activation", pt_ap, tm_ap, AF.Exp), reads=[tm_buf], writes=[pt_buf])
                        if rg >= 1:
                            na_fin(rg - 1, k - 1)
                        if rg + 1 < 16:
                            na_bias(rg + 1, k + 1)
                        fns = []
                        for ri, (r, rs) in enumerate(rows):
                            for j in range(4):
                                a = rs + 2 * j
                                if a % 2 == 0:
                                    vt = ve_ap[:, a // 2, :]
                                else:
                                    vt = vo_ap[:, (a - 1) // 2, :]
                                fns.append(I("matmul", O_ap[0:64, ri * 64:(ri + 1) * 64], lhsT=vt, rhs=pt_ap[:, ri, j, :], start=(j == 0), stop=(j == 3)))
                            for j in range(4):
                                fns.append(I("matmul", S_ap[0:64, ri * 64:(ri + 1) * 64], lhsT=onesb[:, 0:64], rhs=pt_ap[:, ri, j, :], start=(j == 0), stop=(j == 3)))
                        P.op("pe", fns, reads=[pt_buf, ve_buf, vo_buf], writes=[OS_buf])
                        if rg + 3 < 16:
                            na_S(rg + 3, it + rg + 3)
                    na_fin(15, it + 15)
                    it += 16
                    P.dma([(AO[0, cch, hh * 64:(hh + 1) * 64, :], o_ap[0:64, :])], reads=[o_buf], owner=o_buf, kind="s")
            P.barrier()
            if stop_after == "E":
                break

            PB.reset(); PF.reset()
            wstage = [(PF.take(2048), Buf("wstg%d" % i)) for i in range(1)]
            wbr = PB.take(12 * 1024).rearrange("p (a n) -> p a n", n=1024); wbr_buf = Buf("wbr")
            wo = PB.take(8 * 1024).rearrange("p (a n) -> p a n", n=1024); wo_buf = Buf("wo")
            for i in range(3):
                wload(w_br[l, i].rearrange("(c p) d -> p c d", p=128), wbr[:, 4 * i:4 * i + 4, :], wbr_buf, wstage)
            wload(w_out[l].rearrange("(c p) d -> p c d", p=128), wo, wo_buf, wstage)
            c = ln_setup(ln1_g[l:l + 1, :], ln1_b[l:l + 1, :], need_y=False)
            aob = (PB.take(12 * 512).rearrange("p (a n) -> p a n", n=512), Buf("aob"))
            gts = [(PB.take(3 * 512).rearrange("p (a n) -> p a n", n=512), Buf("gts%d" % i)) for i in range(2)]
            mg = (PB.take(8 * 512).rearrange("p (a n) -> p a n", n=512), [Buf("mg%d" % i) for i in range(8)])
            mm = [(PF.take(512), Buf("mm%d" % i)) for i in range(3)]
            xold = [(PF.take(1024), Buf("xold%d" % i)) for i in range(2)]
            psb = [(bank(i), Buf("psb%d" % i)) for i in range(5)]
            psy = (PSt[3][:, :], Buf("psy"))
            psT = [(bank(5), Buf("psT"))]
            GTv = GT.rearrange("(i c) p t -> c p i t", i=3)
            cnt = dict(b=0, g=0, x=0)
            for tb in range(8):
                tsl = slice(tb * 512, (tb + 1) * 512)
                ao_ap, ao_buf = aob
                P.dma([(ao_ap[:, 4 * i:4 * i + 4, :], AO[i, :, :, tsl].rearrange("c p t -> p c t")) for i in range(3)], writes=[ao_buf], owner=ao_buf)
                for dm in range(8):
                    g_ap, g_buf = gts[cnt["g"] % 2]; cnt["g"] += 1
                    P.dma([(g_ap, GTv[dm, :, :, tsl])], writes=[g_buf], owner=g_buf)
                    pbs = []
                    for i in range(3):
                        pb_ap, pb_buf = psb[cnt["b"] % 5]; cnt["b"] += 1
                        pbs.append((pb_ap, pb_buf))
                        P.op("pe", [I("matmul", pb_ap, lhsT=wbr[:, 4 * i + cc, dm * 128:(dm + 1) * 128],
                                                                                  rhs=ao_ap[:, 4 * i + cc, :], start=(cc == 0), stop=(cc == 3)) for cc in range(4)],
                             reads=[ao_buf, wbr_buf], writes=[pb_buf])
                    for i in range(3):
                        P.op("dve", I("tensor_tensor", mm[i][0], pbs[i][0], g_ap[:, i, :], ALU.mult),
                             reads=[pbs[i][1], g_buf], writes=[mm[i][1]])
                    P.op("pool", I("tensor_tensor", mm[0][0], mm[0][0], mm[1][0], ALU.add), reads=[mm[0][1], mm[1][1]], writes=[mm[0][1]])
                    P.op("pool", I("tensor_tensor", mg[0][:, dm, :], mm[0][0], mm[2][0], ALU.add), reads=[mm[0][1], mm[2][1]], writes=[mg[1][dm]])
                for s in range(4):
                    t = tb * 4 + s
                    xo_ap, xo_buf = xold[cnt["x"] % 2]
                    y_ap, y_buf = xo_ap, xo_buf
                    cnt["x"] += 1
                    P.dma([(xo_ap, XTOK[t * 128:(t + 1) * 128, :])], writes=[xo_buf], owner=xo_buf)
                    py_ap, py_buf = psy
                    P.op("pe", [I("matmul", py_ap[:, half * 512:(half + 1) * 512], lhsT=mg[0][:, dm, s * 128:(s + 1) * 128],
                                                                      rhs=wo[:, dm, half * 512:(half + 1) * 512], start=(dm == 0), stop=(dm == 7))
                                for half in range(2) for dm in range(8)], reads=mg[1] + [wo_buf], writes=[py_buf])
                    P.op("dve", I("scalar_tensor_tensor", y_ap, xo_ap, ALPHA, py_ap, op0=ALU.mult, op1=ALU.add),
                         reads=[xo_buf, py_buf], writes=[y_buf])
                    ln_tile(c, y_ap, y_buf, t, psT[0][0], psT[0][1])
            P.barrier()
            if stop_after == "F":
                break

            PB.reset(); PF.reset()
            wstage = [(PF.take(2048), Buf("wstg%d" % i)) for i in range(2)]
            wg = [(PB.take(1024).rearrange("p (a n) -> p a n", n=128), Buf("wg%d" % i)) for i in range(3)]
            wu = [(PB.take(1024).rearrange("p (a n) -> p a n", n=128), Buf("wu%d" % i)) for i in range(3)]
            stg = [(PB.take(512), Buf("stg%d" % i)) for i in range(3)]
            sg = [(PF.take(512), Buf("sg%d" % i)) for i in range(2)]
            psG = [(bank(2 * i), Buf("psG%d" % i)) for i in range(4)]
            psU = [(bank(2 * i + 1), Buf("psU%d" % i)) for i in range(4)]

            def load_f(fc):
                wload(w_fi[l][:, fc * 128:(fc + 1) * 128].rearrange("(kc p) e -> p kc e", p=128), wg[fc % 3][0], wg[fc % 3][1], wstage)
                wload(w_fi[l][:, DFF + fc * 128:DFF + (fc + 1) * 128].rearrange("(kc p) e -> p kc e", p=128), wu[fc % 3][0], wu[fc % 3][1], wstage)

            load_f(0); load_f(1)
            it = 0
            for fc in range(FC):
                if fc + 2 < FC:
                    load_f(fc + 2)
                wg_ap, wg_buf = wg[fc % 3]; wu_ap, wu_buf = wu[fc % 3]
                for tb in range(8):
                    pg_ap, pg_buf = psG[it % 4]; pu_ap, pu_buf = psU[it % 4]
                    P.op("pe", [I("matmul", pg_ap, lhsT=wg_ap[:, kc, :], rhs=xT[:, kc, tb * 512:(tb + 1) * 512],
                                                                                    start=(kc == 0), stop=(kc == KC - 1)) for kc in range(KC)],
                         reads=[wg_buf], writes=[pg_buf])
                    P.op("pe", [I("matmul", pu_ap, lhsT=wu_ap[:, kc, :], rhs=xT[:, kc, tb * 512:(tb + 1) * 512],
                                                                                    start=(kc == 0), stop=(kc == KC - 1)) for kc in range(KC)],
                         reads=[wu_buf], writes=[pu_buf])
                    sg_ap, sg_buf = sg[it % 2]
                    st_ap, st_buf = stg[it % 3]
                    it += 1
                    P.op("act", I("activation", sg_ap, pg_ap, AF.Silu), reads=[pg_buf], writes=[sg_buf])
                    P.op("dve", I("tensor_tensor", st_ap, pu_ap, sg_ap, ALU.mult),
                         reads=[pu_buf, sg_buf], writes=[st_buf])
                    P.dma([(HT[fc, :, tb * 512:(tb + 1) * 512], st_ap)], reads=[st_buf], owner=st_buf, kind="s")
            P.barrier()
            if stop_after == "G":
                break

            PB.reset(); PF.reset()
            wstage = [(PF.take(2048), Buf("wstg%d" % i)) for i in range(1)]
            wfo = PB.take(FC * 1024).rearrange("p (a n) -> p a n", n=1024); wfo_buf = Buf("wfo")
            wload(w_fo[l].rearrange("(c p) d -> p c d", p=128), wfo, wfo_buf, wstage)
            c = ln_setup(ln2_g[l:l + 1, :], ln2_b[l:l + 1, :], need_y=False)
            hb = [(PB.take(FC * 256).rearrange("p (a n) -> p a n", n=256), Buf("hb%d" % i)) for i in range(2)]
            xold = [(PF.take(1024), Buf("xold%d" % i)) for i in range(2)]
            psy = [(PSt[i][:, :], Buf("psy%d" % i)) for i in range(2)]
            psT = [(bank(4 + i), Buf("psT%d" % i)) for i in range(2)]
            k = 0
            for hbi in range(16):
                h_ap, h_buf = hb[hbi % 2]
                P.dma([(h_ap, HT[:, :, hbi * 256:(hbi + 1) * 256].rearrange("c p t -> p c t"))], writes=[h_buf], owner=h_buf)
                for s in range(2):
                    t = hbi * 2 + s
                    xo_ap, xo_buf = xold[k % 2]
                    y_ap, y_buf = xo_ap, xo_buf
                    py_ap, py_buf = psy[k % 2]
                    pT_ap, pT_buf = psT[k % 2]
                    k += 1
                    P.dma([(xo_ap, XTOK[t * 128:(t + 1) * 128, :])], writes=[xo_buf], owner=xo_buf)
                    P.op("pe", [I("matmul",
                        py_ap[:, half * 512:(half + 1) * 512], lhsT=h_ap[:, fc, s * 128:(s + 1) * 128], rhs=wfo[:, fc, half * 512:(half + 1) * 512],
                        start=(fc == 0), stop=(fc == FC - 1)) for half in range(2) for fc in range(FC)], reads=[h_buf, wfo_buf], writes=[py_buf])
                    P.op("dve", I("scalar_tensor_tensor", y_ap, xo_ap, ALPHA, py_ap, op0=ALU.mult, op1=ALU.add),
                         reads=[xo_buf, py_buf], writes=[y_buf])
                    ln_tile(c, y_ap, y_buf, t, pT_ap, pT_buf, final=(l == n_layers - 1))
            P.barrier()

        P.emit(block)
        build.stats = dict(ninst=P.ninst, ecnt=dict(P.ecnt), maxsem=max(P.semval.values()))
    return nc


def _consts():
    bf = ml_dtypes.bfloat16
    cst = {}
    cst["c_ident"] = np.eye(128, dtype=np.float32).astype(bf)
    cst["c_onesb"] = np.ones((128, 128), dtype=np.float32).astype(bf)
    cst["c_onesf"] = np.ones((128, 128), dtype=np.float32)
    pos = np.arange(T, dtype=np.float32)

    def tables(half):
        inv = np.exp(-math.log(ROPE_THETA) * np.arange(half, dtype=np.float32) / half).astype(np.float32)
        ang = (pos[None, :] * inv[:, None]).astype(np.float32)
        return np.cos(ang.astype(np.float64)).astype(np.float32), np.sin(ang.astype(np.float64)).astype(np.float32)

    cos8, sin8 = tables(8)
    cosd = np.ones((128, T), np.float32); sind = np.zeros((128, T), np.float32)
    rdm = np.zeros((128, 128), np.float32)
    for gb in (0, 64):
        for i in range(8):
            cosd[gb + i] = cos8[i]; cosd[gb + 8 + i] = cos8[i]
            sind[gb + i] = -sin8[i]; sind[gb + 8 + i] = sin8[i]
            rdm[gb + 8 + i, gb + i] = 1.0
            rdm[gb + i, gb + 8 + i] = 1.0
    cst["c_cosd"] = cosd; cst["c_sind"] = sind; cst["c_rd"] = rdm.astype(bf)
    cos16, sin16 = tables(16)
    cosm = np.ones((128, T), np.float32); sinm = np.zeros((128, T), np.float32)
    rmm = np.zeros((128, 128), np.float32)
    for i in range(16):
        cosm[64 + i] = cos16[i]; cosm[80 + i] = cos16[i]
        sinm[64 + i] = -sin16[i]; sinm[80 + i] = sin16[i]
        rmm[80 + i, 64 + i] = 1.0
        rmm[64 + i, 80 + i] = 1.0
    cst["c_cosm"] = cosm; cst["c_sinm"] = sinm; cst["c_rm"] = rmm.astype(bf)
    cst["c_cosk"] = np.ascontiguousarray(cosm[64:96]); cst["c_sink"] = np.ascontiguousarray(sinm[64:96])
    cst["c_rk"] = np.ascontiguousarray(rmm[64:96, 64:96]).astype(bf)
    qc = np.arange(64)
    ws = np.clip(qc - 8, 0, 48)
    kc_ = np.arange(64)
    valid = (kc_[:, None] >= ws[None, :]) & (kc_[:, None] < ws[None, :] + 16)
    m = np.where(valid, 0.0, NEG).astype(np.float32)
    m2 = np.concatenate([m, m], axis=0)
    cst["namask"] = np.ascontiguousarray(np.broadcast_to(m2[:, None, :], (128, 14, 64))).reshape(128, 14 * 64)
    return cst


_CACHE = {}


def _prep_shared(inp):
    f = lambda a: np.ascontiguousarray(np.asarray(a, dtype=np.float32))
    sh = {}
    sh["ln_in_g"] = f(inp["ln_in_g"]).reshape(1, D)
    sh["ln_in_b"] = f(inp["ln_in_b"]).reshape(1, D)
    sh["w_in"] = f(inp["w_in"])
    sh["bgate"] = np.ascontiguousarray(f(inp["b_gate"]).reshape(NL, 24, 128).transpose(0, 2, 1))
    rpb = f(inp["na_rpb"])
    kcol = np.arange(64)[:, None]; qcol = np.arange(64)[None, :]
    idx = np.clip(kcol - qcol, -15, 15) + 15
    sh["nabias"] = np.ascontiguousarray(rpb[:, :, :, idx]).reshape(NL, 8, 960, 64)
    sh["dlam"] = f(inp["diff_lambda"]).reshape(NL, 1, 256)
    sh["subg"] = f(inp["diff_subln_g"]).reshape(NL, 128, 1)
    sh["gq"] = np.ascontiguousarray(f(inp["mla_q_norm_g"]).reshape(NL, 3, 128).transpose(0, 2, 1))
    sh["gkv"] = np.ascontiguousarray(f(inp["mla_kv_norm_g"]).reshape(NL, 2, 128).transpose(0, 2, 1))
    sh["w_qb"] = f(inp["w_mla_qb"])
    sh["w_kvb"] = f(inp["w_mla_kvb"])
    sh["w_br"] = f(inp["w_branch"])
    sh["w_out"] = f(inp["w_out"])
    sh["ln1_g"] = f(inp["ln1_g"]); sh["ln1_b"] = f(inp["ln1_b"])
    sh["w_fi"] = f(inp["w_ffn_in"]); sh["w_fo"] = f(inp["w_ffn_out"])
    sh["ln2_g"] = f(inp["ln2_g"]); sh["ln2_b"] = f(inp["ln2_b"])
    sh.update(_consts())
    return sh


def kernel(**inputs):
    x = np.ascontiguousarray(np.asarray(inputs["x"], dtype=np.float32))
    nb = x.shape[0]
    sh = _prep_shared(inputs)
    if "nc" not in _CACHE:
        _CACHE["nc"] = build()
    nc = _CACHE["nc"]
    in_maps = []
    for b in range(nb):
        m = dict(sh)
        m["x"] = x[b]
        in_maps.append(m)
    res = run_bass_kernel_spmd(nc, in_maps, core_ids=list(range(nb)))
    return np.stack([np.asarray(r["out"], dtype=np.float32) for r in res.results], axis=0)
```

```python
import math
import os
import numpy as np
import ml_dtypes
import concourse.bass as bass
import concourse.mybir as mybir
from concourse.bass_utils import run_bass_kernel_spmd

F32 = mybir.dt.float32
BF16 = mybir.dt.bfloat16
AF = mybir.ActivationFunctionType
ALU = mybir.AluOpType
AX = mybir.AxisListType

T = 4096
D = 1024
KC = 8
NL = 4
DIN = 6816
DFF = 2816
FC = 22
ALPHA = (2 * NL) ** 0.25
LN_EPS = 1e-5
RMS_EPS = 1e-6
ROPE_THETA = 500000.0
NEG = -30000.0
DBGB = os.environ.get('DBG_B', 'qkvr')


class Buf:
    __slots__ = ("name", "w", "r", "lsem", "ssem", "last_s")

    def __init__(self, name):
        self.name = name
        self.w = None
        self.r = {}
        self.lsem = None
        self.ssem = None
        self.last_s = None


class Prog:
    CE = ("pe", "act", "dve", "pool")

    def __init__(self, nc, sems):
        self.nc = nc
        self.sems = sems
        self.q = {e: [] for e in ("pe", "act", "dve", "pool", "sp")}
        self.esi = {e: i for i, e in enumerate(self.CE)}
        self.bar_si = 4
        self.free = list(range(5, len(sems)))
        self.ecnt = {e: 0 for e in self.CE}
        self.barcnt = 0
        self.semval = {i: 0 for i in range(len(sems))}
        self.waited = {e: {} for e in self.q}
        self.phase_dma = {}
        self.sem_bufs = []
        self.ninst = 0

    def wait(self, eng, ev):
        if ev is None:
            return
        si, val = ev
        if self.waited[eng].get(si, 0) >= val:
            return
        self.waited[eng][si] = val
        sem = self.sems[si]
        self.q[eng].append(("wait_ge", (sem, val), {}))

    def _deps(self, eng, reads, writes):
        deps = {}

        def add(ev):
            if ev is None:
                return
            si, val = ev
            if deps.get(si, 0) < val:
                deps[si] = val

        for b in reads:
            add(b.w)
        for b in writes:
            add(b.w)
            for si, val in b.r.items():
                add((si, val))
        for si, val in deps.items():
            if eng == "pe" and si == self.esi["pe"]:
                continue
            self.wait(eng, (si, val))

    def op(self, eng, fns, reads=(), writes=()):
        self._deps(eng, reads, writes)
        si = self.esi[eng]
        self.ecnt[eng] += 1
        val = self.ecnt[eng]
        ev = (si, val)
        if isinstance(fns, tuple):
            fns = [fns]
        for f in fns[:-1]:
            self.q[eng].append(f)
        last = fns[-1]
        sem = self.sems[si]
        self.q[eng].append((last[0], last[1], last[2], sem, 1))
        self.ninst += len(fns)
        for b in reads:
            if b.r.get(si, 0) < val:
                b.r[si] = val
        for b in writes:
            b.w = ev
            b.r = {}
        return ev

    def _bufsem(self, b, kind):
        cur = b.lsem if kind == "l" else b.ssem
        if cur is None:
            cur = self.free.pop()
            if kind == "l":
                b.lsem = cur
            else:
                b.ssem = cur
            self.sem_bufs.append(b)
        return cur

    def dma(self, pairs, reads=(), writes=(), owner=None, kind="l", eng="sp"):
        self._deps(eng, reads, writes)
        if kind == "s" and owner.last_s is not None:
            self.wait(eng, owner.last_s)
        si = self._bufsem(owner, kind)
        self.semval[si] += 16 * len(pairs)
        val = self.semval[si]
        assert val < 60000, "semaphore value too large"
        ev = (si, val)
        sem = self.sems[si]
        for (o, i) in pairs:
            self.q[eng].append(("dma_start", (), dict(out=o, in_=i), sem, 16))
        self.ninst += len(pairs)
        for b in reads:
            if b.r.get(si, 0) < val:
                b.r[si] = val
        for b in writes:
            b.w = ev
            b.r = {}
        if kind == "s":
            owner.last_s = ev
        self.phase_dma[si] = val
        return ev

    def barrier(self):
        for e in self.CE:
            if self.ecnt[e] > 0:
                self.wait("sp", (self.esi[e], self.ecnt[e]))
        for si, val in self.phase_dma.items():
            self.wait("sp", (si, val))
        self.barcnt += 1
        n = self.barcnt
        sem = self.sems[self.bar_si]
        self.q["sp"].append(("sem_inc", (sem, 1), {}))
        for e in self.CE:
            self.wait(e, (self.bar_si, n))
        for e in self.q:
            for x in self.CE:
                self.waited[e][self.esi[x]] = self.ecnt[x]
            for si, val in self.semval.items():
                if si > self.bar_si:
                    self.waited[e][si] = val
        self.phase_dma = {}
        for b in self.sem_bufs:
            if b.lsem is not None:
                self.free.append(b.lsem)
                b.lsem = None
            if b.ssem is not None:
                self.free.append(b.ssem)
                b.ssem = None
        self.sem_bufs = []

    def emit(self, block):
        q = self.q

        def run(e, lst):
            for it in lst:
                ins = getattr(e, it[0])(*it[1], **it[2])
                if len(it) == 5:
                    ins.then_inc(it[3], it[4])

        @block.tensor
        def _(e):
            run(e, q["pe"])

        @block.scalar
        def _(e):
            run(e, q["act"])

        @block.vector
        def _(e):
            run(e, q["dve"])

        @block.gpsimd
        def _(e):
            run(e, q["pool"])

        @block.sync
        def _(e):
            run(e, q["sp"])


def I(name, *a, **k):
    return (name, a, k)


class Carver:
    def __init__(self, t, n):
        self.t = t
        self.n = n
        self.off = 0

    def reset(self):
        self.off = 0

    def take(self, n):
        assert self.off + n <= self.n, ("sbuf pool overflow", self.off, n, self.n)
        a = self.t[:, self.off:self.off + n]
        self.off += n
        return a


def build(n_layers=NL, debug=(), stop_after=None):
    nc = bass.Bass("TRN2", target_bir_lowering=False)

    def din(name, shape, dt=F32):
        return nc.dram_tensor(name, list(shape), dt, kind="ExternalInput").ap()

    def scr(name, shape, dt):
        kind = "ExternalOutput" if name in debug else "Internal"
        return nc.dram_tensor(name, list(shape), dt, kind=kind).ap()

    x_in = din("x", [T, D])
    ln_in_g = din("ln_in_g", [1, D])
    ln_in_b = din("ln_in_b", [1, D])
    w_in = din("w_in", [NL, D, DIN])
    bgate = din("bgate", [NL, 128, 24])
    nabias = din("nabias", [NL, 8, 960, 64])
    namask = din("namask", [128, 14 * 64])
    dlam = din("dlam", [NL, 1, 256])
    subg = din("subg", [NL, 128, 1])
    gq_in = din("gq", [NL, 128, 3])
    gkv_in = din("gkv", [NL, 128, 2])
    w_qb = din("w_qb", [NL, 384, 768])
    w_kvb = din("w_kvb", [NL, 256, 1024])
    w_br = din("w_br", [NL, 3, 512, D])
    w_out = din("w_out", [NL, D, D])
    ln1_g = din("ln1_g", [NL, D])
    ln1_b = din("ln1_b", [NL, D])
    w_fi = din("w_fi", [NL, D, 2 * DFF])
    w_fo = din("w_fo", [NL, DFF, D])
    ln2_g = din("ln2_g", [NL, D])
    ln2_b = din("ln2_b", [NL, D])
    c_ident = din("c_ident", [128, 128], BF16)
    c_onesb = din("c_onesb", [128, 128], BF16)
    c_onesf = din("c_onesf", [128, 128])
    c_rd = din("c_rd", [128, 128], BF16)
    c_rm = din("c_rm", [128, 128], BF16)
    c_rk = din("c_rk", [32, 32], BF16)
    c_cosd = din("c_cosd", [128, T])
    c_sind = din("c_sind", [128, T])
    c_cosm = din("c_cosm", [128, T])
    c_sinm = din("c_sinm", [128, T])
    c_cosk = din("c_cosk", [32, T])
    c_sink = din("c_sink", [32, T])
    out = nc.dram_tensor("out", [T, D], F32, kind="ExternalOutput").ap()

    XTOK = scr("XTOK", [T, D], F32)
    NQ = scr("NQ", [4, 128, T], BF16)
    NK = scr("NK", [4, 128, T], BF16)
    NV = scr("NV", [T, 512], BF16)
    DQ = scr("DQ", [4, 128, T], BF16)
    DK = scr("DK", [4, 128, T], BF16)
    DV = scr("DV", [T, 512], BF16)
    MC = scr("MC", [6, 128, T], F32)
    GT = scr("GT", [24, 128, T], BF16)
    MQ = scr("MQ", [8, 128, T], BF16)
    MK = scr("MK", [8, 128, T], BF16)
    MV = scr("MV", [T, 512], BF16)
    AO = scr("AO", [3, 4, 128, T], BF16)
    HT = scr("HT", [FC, 128, T], BF16)
    RS = scr("RS", [2, 512], F32)

    NPB = 36864
    NPF = 10240
    xT = nc.alloc_sbuf_tensor("xT", [128, KC, T], BF16)
    PBt = nc.alloc_sbuf_tensor("PB", [128, NPB], BF16)
    PFt = nc.alloc_sbuf_tensor("PF", [128, NPF], F32)
    ident = nc.alloc_sbuf_tensor("ident", [128, 128], BF16)
    onesb = nc.alloc_sbuf_tensor("onesb", [128, 128], BF16)
    onesf = nc.alloc_sbuf_tensor("onesf", [128, 128], F32)
    rd = nc.alloc_sbuf_tensor("rd", [128, 128], BF16)
    rm = nc.alloc_sbuf_tensor("rm", [128, 128], BF16)
    rk = nc.alloc_sbuf_tensor("rk", [32, 32], BF16)
    bg_t = nc.alloc_sbuf_tensor("bg_t", [128, NL * 24], F32)
    gq_t = nc.alloc_sbuf_tensor("gq_t", [128, NL * 3], F32)
    gkv_t = nc.alloc_sbuf_tensor("gkv_t", [128, NL * 2], F32)
    gsc_t = nc.alloc_sbuf_tensor("gsc_t", [128, NL], F32)
    nlam_t = nc.alloc_sbuf_tensor("nlam_t", [128, NL], F32)
    lamw = nc.alloc_sbuf_tensor("lamw", [128, NL * 256], F32)
    lamp = nc.alloc_sbuf_tensor("lamp", [128, NL * 128], F32)
    lams = nc.alloc_sbuf_tensor("lams", [128, NL * 2], F32)
    lame = nc.alloc_sbuf_tensor("lame", [128, NL * 2], F32)
    lnst = nc.alloc_sbuf_tensor("lnst", [128, 4 * 16], F32)
    epsln = nc.alloc_sbuf_tensor("epsln", [128, 1], F32)
    epsrms = nc.alloc_sbuf_tensor("epsrms", [128, 1], F32)
    PSt = [nc.alloc_psum_tensor("ps%d" % i, [128, 1024], F32) for i in range(4)]

    def bank(i):
        return PSt[i // 2][:, (i % 2) * 512:(i % 2) * 512 + 512]

    PB = Carver(PBt, NPB)
    PF = Carver(PFt, NPF)

    import contextlib
    with contextlib.ExitStack() as es:
        sems = [es.enter_context(nc.semaphore("s%d" % i)) for i in range(100)]
        block = es.enter_context(nc.Block())
        P = Prog(nc, sems)

        def wload(src, dst_ap, dst_buf, stage):
            a, n = src.shape[1], src.shape[2]
            per = max(1, 2048 // n)
            i = 0
            k = wload.k
            while i < a:
                j = min(a, i + per)
                sap, sbuf = stage[k % len(stage)]
                k += 1
                sv = sap[:, 0:(j - i) * n].rearrange("p (a n) -> p a n", n=n)
                P.dma([(sv, src[:, i:j, :])], writes=[sbuf], owner=sbuf)
                P.op("pool", I("tensor_copy", dst_ap[:, i:j, :], sv), reads=[sbuf], writes=[dst_buf])
                i = j
            wload.k = k

        wload.k = 0

        def rope_epi(ps_ap, ps_buf, rows, cos_ap, sin_ap, cs_buf, rmat, ps2_ap, ps2_buf, xb, t1, t2, stg_ap, stg_buf):
            xb_ap, xb_buf = xb
            t1_ap, t1_buf = t1
            t2_ap, t2_buf = t2
            nst = int(os.environ.get('DBG_RSTEPS', '5'))
            P.op("act", I("copy", xb_ap[0:rows, :], ps_ap[0:rows, :]), reads=[ps_buf], writes=[xb_buf])
            if nst < 2: return
            P.op("pe", I("matmul", ps2_ap[0:rows, :], lhsT=rmat[0:rows, 0:rows], rhs=xb_ap[0:rows, :], start=True, stop=True),
                 reads=[xb_buf], writes=[ps2_buf])
            if nst < 3: return
            v = os.environ.get('DBG_T1', '')
            if v == 'xb':
                P.op("dve", I("tensor_tensor", t1_ap[0:rows, :], xb_ap[0:rows, :], cos_ap[0:rows, :], ALU.mult),
                     reads=[xb_buf, cs_buf], writes=[t1_buf])
            elif v == 'sin':
                P.op("dve", I("tensor_tensor", t1_ap[0:rows, :], ps_ap[0:rows, :], sin_ap[0:rows, :], ALU.mult),
                     reads=[ps_buf, cs_buf], writes=[t1_buf])
            elif v == 't2':
                P.op("dve", I("tensor_tensor", t2_ap[0:rows, :], ps_ap[0:rows, :], cos_ap[0:rows, :], ALU.mult),
                     reads=[ps_buf, cs_buf], writes=[t2_buf])
            else:
                P.op("dve", I("tensor_tensor", t1_ap[0:rows, :], ps_ap[0:rows, :], cos_ap[0:rows, :], ALU.mult),
                     reads=[ps_buf, cs_buf, xb_buf], writes=[t1_buf])
            if nst < 4: return
            P.op("dve", I("tensor_tensor", t2_ap[0:rows, :], ps2_ap[0:rows, :], sin_ap[0:rows, :], ALU.mult),
                 reads=[ps2_buf, cs_buf], writes=[t2_buf])
            if nst < 5: return
            P.op("pool", I("tensor_tensor", stg_ap[0:rows, :], t1_ap[0:rows, :], t2_ap[0:rows, :], ALU.add),
                 reads=[t1_buf, t2_buf], writes=[stg_buf])

        class LNCtx:
            pass

        def ln_setup(g_src, b_src, need_y=True):
            c = LNCtx()
            c.g = PF.take(1024)
            c.b = PF.take(1024)
            c.gb = Buf("lngb")
            P.dma([(c.g, g_src.partition_broadcast(128)[:, 0, :]), (c.b, b_src.partition_broadcast(128)[:, 0, :])],
                  writes=[c.gb], owner=c.gb)
            c.y = [(PF.take(1024), Buf("lny%d" % i)) for i in range(2)] if need_y else None
            c.xo = [(PF.take(1024), Buf("lnxo%d" % i)) for i in range(2)]
            c.xb = [(PB.take(1024), Buf("lnxb%d" % i)) for i in range(2)]
            c.st = [(lnst[:, i * 16:(i + 1) * 16], Buf("lnst%d" % i)) for i in range(2)]
            c.k = 0
            return c

        def ln_tile(c, y_ap, y_buf, t, psT_ap, psT_buf, final=False):
            k = c.k
            c.k += 1
            st_ap, st_buf = c.st[k % 2]
            xo_ap, xo_buf = c.xo[k % 2]
            xb_ap, xb_buf = c.xb[k % 2]
            P.op("dve", [I("bn_stats", st_ap[:, 0:6], y_ap[:, 0:512]),
                         I("bn_stats", st_ap[:, 6:12], y_ap[:, 512:1024])], reads=[y_buf], writes=[st_buf])
            P.op("dve", I("bn_aggr", st_ap[:, 12:14], st_ap[:, 0:12]), reads=[st_buf], writes=[st_buf])
            P.op("act", I("activation", st_ap[:, 14:15], st_ap[:, 13:14], AF.Sqrt, bias=epsln[:, 0:1], scale=1.0), reads=[st_buf, cbuf], writes=[st_buf])
            P.op("dve", I("reciprocal", st_ap[:, 14:15], st_ap[:, 14:15]), reads=[st_buf], writes=[st_buf])
            P.op("dve", I("tensor_scalar", y_ap, y_ap, st_ap[:, 12:13], st_ap[:, 14:15], op0=ALU.subtract, op1=ALU.mult),
                 reads=[st_buf, y_buf], writes=[y_buf])
            P.op("pool", I("tensor_tensor", xo_ap, y_ap, c.g, ALU.mult), reads=[y_buf, c.gb], writes=[xo_buf])
            P.op("pool", I("tensor_tensor", xo_ap, xo_ap, c.b, ALU.add), reads=[xo_buf, c.gb], writes=[xo_buf])
            pairs = [(XTOK[t * 128:(t + 1) * 128, :], xo_ap)]
            if final:
                pairs = [(out[t * 128:(t + 1) * 128, :], xo_ap)]
            P.dma(pairs, reads=[xo_buf], owner=xo_buf, kind="s")
            if final:
                return
            P.op("act", I("copy", xb_ap, xo_ap), reads=[xo_buf], writes=[xb_buf])
            psb = psT_ap.bitcast(BF16)
            P.op("pe", [I("transpose", psb[:, kc * 128:(kc + 1) * 128], xb_ap[:, kc * 128:(kc + 1) * 128], ident[:, :])
                        for kc in range(KC)], reads=[xb_buf, cbuf], writes=[psT_buf])
            P.op("act", I("copy", xT[:, :, t * 128:(t + 1) * 128], psb.rearrange("p (a n) -> p a n", n=128)),
                 reads=[psT_buf], writes=[])

        cbuf = Buf("consts")
        P.dma([(ident[:, :], c_ident), (onesb[:, :], c_onesb), (onesf[:, :], c_onesf), (rd[:, :], c_rd), (rm[:, :], c_rm),
               (rk[:, :], c_rk)], writes=[cbuf], owner=cbuf)
        P.op("pool", [I("memset", epsln[:, :], LN_EPS), I("memset", epsrms[:, :], RMS_EPS)], writes=[cbuf])
        sbuf_ = Buf("smalls")
        pairs = []
        for l in range(NL):
            pairs += [(bg_t[:, l * 24:(l + 1) * 24], bgate[l]), (gq_t[:, l * 3:(l + 1) * 3], gq_in[l]),
                      (gkv_t[:, l * 2:(l + 1) * 2], gkv_in[l]), (gsc_t[:, l:l + 1], subg[l]),
                      (lamw[:, l * 256:(l + 1) * 256], dlam[l].partition_broadcast(128)[:, 0, :])]
        P.dma(pairs, writes=[sbuf_], owner=sbuf_)
        lb = Buf("lam")
        for l in range(NL):
            lam_init = 0.8 - 0.6 * math.exp(-0.3 * l)
            lw = lamw[:, l * 256:(l + 1) * 256].rearrange("p (a b) -> p a b", b=64)
            lp = lamp[:, l * 128:(l + 1) * 128].rearrange("p (a b) -> p a b", b=64)
            P.op("dve", I("tensor_tensor", lp, lw[:, 0:4:2, :], lw[:, 1:4:2, :], ALU.mult), reads=[sbuf_], writes=[lb])
            P.op("dve", I("reduce_sum", lams[:, l * 2:(l + 1) * 2], lp, AX.X), reads=[lb], writes=[lb])
            P.op("act", I("activation", lame[:, l * 2:(l + 1) * 2], lams[:, l * 2:(l + 1) * 2], AF.Exp), reads=[lb], writes=[lb])
            P.op("dve", I("tensor_tensor", nlam_t[:, l:l + 1], lame[:, l * 2 + 1:l * 2 + 2], lame[:, l * 2:l * 2 + 1], ALU.subtract),
                 reads=[lb], writes=[lb])
            P.op("dve", I("tensor_scalar", nlam_t[:, l:l + 1], nlam_t[:, l:l + 1], -lam_init, None, op0=ALU.add),
                 reads=[lb], writes=[lb])
            P.op("dve", I("tensor_scalar", gsc_t[:, l:l + 1], gsc_t[:, l:l + 1], 1.0 - lam_init, None, op0=ALU.mult),
                 reads=[lb, sbuf_], writes=[lb])

        PB.reset(); PF.reset()
        c = ln_setup(ln_in_g[0:1, :], ln_in_b[0:1, :])
        psT = [(bank(i), Buf("psT%d" % i)) for i in range(2)]
        for t in range(T // 128):
            y_ap, y_buf = c.y[t % 2]
            P.dma([(y_ap, x_in[t * 128:(t + 1) * 128, :])], writes=[y_buf], owner=y_buf)
            ln_tile(c, y_ap, y_buf, t, psT[t % 2][0], psT[t % 2][1])
        P.barrier()

        for l in range(n_layers):
            if stop_after == "0":
                break
            last_layer = (l == NL - 1)
            PB.reset(); PF.reset()
            wstage = [(PF.take(2048), Buf("wstg%d" % i)) for i in range(2)]
            wst = [(PB.take(2048).rearrange("p (a n) -> p a n", n=256), Buf("wst%d" % i)) for i in range(3)]
            stg = [(PB.take(512), Buf("stg%d" % i)) for i in range(4)]
            stgf = [(PF.take(512), Buf("stgf%d" % i)) for i in range(2)]
            xbr = [(PB.take(512), Buf("xbr%d" % i)) for i in range(2)]
            t1s = [(PF.take(512), Buf("t1s%d" % i)) for i in range(2)]
            t2s = [(PF.take(512), Buf("t2s%d" % i)) for i in range(2)]
            css = [(PF.take(1024), Buf("css%d" % i)) for i in range(2)]
            psA = [(bank(i), Buf("psA%d" % i)) for i in range(6)]
            ps2 = [(bank(6 + i), Buf("ps2%d" % i)) for i in range(2)]
            cnt = dict(item=0, stg=0, stgf=0, rope=0)

            strips = []
            for e0 in range(0, 1024, 256):
                strips.append(("fm", e0))
            strips.append(("v", 1024)); strips.append(("v", 1280))
            for e0 in range(1536, 2560, 256):
                strips.append(("fm", e0))
            strips.append(("v", 2560)); strips.append(("v", 2816))
            for e0 in range(3072, 3584, 256):
                strips.append(("fm", e0))
            strips.append(("fm", 3584))
            for e0 in range(3744, DIN, 256):
                strips.append(("gate", e0))

            def chunk_dst(e0, wd):
                if e0 < 512:
                    return "plain", NQ[e0 // 128]
                if e0 < 1024:
                    return "plain", NK[(e0 - 512) // 128]
                if 1536 <= e0 < 2048:
                    return "rope", DQ[(e0 - 1536) // 128]
                if 2048 <= e0 < 2560:
                    return "rope", DK[(e0 - 2048) // 128]
                if 3072 <= e0 < 3712:
                    return "f32", MC[(e0 - 3072) // 128]
                if e0 == 3712:
                    return "f32", MC[5]
                if e0 >= 3744:
                    return "gate", GT[(e0 - 3744) // 128]
                raise AssertionError(e0)

            def load_strip(si_):
                kind, e0 = strips[si_]
                ncol = 256
                if kind == "fm" and e0 == 3584:
                    ncol = 160
                w_ap, w_buf = wst[si_ % 3]
                src = w_in[l][:, e0:e0 + ncol].rearrange("(kc p) e -> p kc e", p=128)
                wload(src, w_ap[:, :, 0:ncol], w_buf, wstage)

            load_strip(0)
            load_strip(1)
            for si_, (kind, e0) in enumerate(strips):
                if si_ + 2 < len(strips):
                    load_strip(si_ + 2)
                w_ap, w_buf = wst[si_ % 3]
                if kind == "v":
                    vdst = NV if e0 < 1536 else DV
                    c0 = (e0 - 1024) if e0 < 1536 else (e0 - 2560)
                    for t in range(T // 128):
                        ps_ap, ps_buf = psA[cnt["item"] % 6]
                        cnt["item"] += 1
                        P.op("pe", [I("matmul",
                            ps_ap[:, 0:256], lhsT=xT[:, kc, t * 128:(t + 1) * 128], rhs=w_ap[:, kc, :], start=(kc == 0), stop=(kc == KC - 1))
                            for kc in range(KC)], reads=[w_buf], writes=[ps_buf])
                        s_ap, s_buf = stg[cnt["stg"] % 4]
                        cnt["stg"] += 1
                        P.op("act", I("copy", s_ap[:, 0:256], ps_ap[:, 0:256]), reads=[ps_buf], writes=[s_buf])
                        P.dma([(vdst[t * 128:(t + 1) * 128, c0:c0 + 256], s_ap[:, 0:256])], reads=[s_buf], owner=s_buf, kind="s")
                    continue
                if kind == "fm" and e0 == 3584:
                    chunks = [(3584, 128, 0), (3712, 32, 128)]
                else:
                    chunks = [(e0, 128, 0), (e0 + 128, 128, 128)]
                for (ce0, wd, coff) in chunks:
                    ckind, dst = chunk_dst(ce0, wd)
                    for tb in range(8):
                        ps_ap, ps_buf = psA[cnt["item"] % 6]
                        cnt["item"] += 1
                        P.op("pe", [I("matmul",
                            ps_ap[0:wd, :], lhsT=w_ap[:, kc, coff:coff + wd], rhs=xT[:, kc, tb * 512:(tb + 1) * 512],
                            start=(kc == 0), stop=(kc == KC - 1)) for kc in range(KC)], reads=[w_buf], writes=[ps_buf])
                        dsl = dst[0:wd, tb * 512:(tb + 1) * 512]
                        if ckind == "plain":
                            s_ap, s_buf = stg[cnt["stg"] % 4]; cnt["stg"] += 1
                            P.op("act", I("copy", s_ap, ps_ap), reads=[ps_buf], writes=[s_buf])
                            P.dma([(dsl, s_ap)], reads=[s_buf], owner=s_buf, kind="s")
                        elif ckind == "gate":
                            gi = (ce0 - 3744) // 128
                            s_ap, s_buf = stg[cnt["stg"] % 4]; cnt["stg"] += 1
                            P.op("act", I("activation",
                                s_ap, ps_ap, AF.Sigmoid, bias=bg_t[:, l * 24 + gi:l * 24 + gi + 1]), reads=[ps_buf], writes=[s_buf])
                            P.dma([(dsl, s_ap)], reads=[s_buf], owner=s_buf, kind="s")
                        elif ckind == "f32":
                            s_ap, s_buf = stgf[cnt["stgf"] % 2]; cnt["stgf"] += 1
                            P.op("act", I("copy", s_ap[0:wd, :], ps_ap[0:wd, :]), reads=[ps_buf], writes=[s_buf])
                            P.dma([(dsl, s_ap[0:wd, :])], reads=[s_buf], owner=s_buf, kind="s")
                        else:
                            k = cnt["rope"]; cnt["rope"] += 1
                            cs_ap, cs_buf = css[k % 2]
                            P.dma([(cs_ap[:, 0:512], c_cosd[:, tb * 512:(tb + 1) * 512]), (cs_ap[:, 512:1024], c_sind[:, tb * 512:(tb + 1) * 512])],
                                  writes=[cs_buf], owner=cs_buf)
                            s_ap, s_buf = stg[cnt["stg"] % 4]; cnt["stg"] += 1
                            rope_epi(ps_ap, ps_buf, 128, cs_ap[:, 0:512], cs_ap[:, 512:1024], cs_buf, rd, ps2[k % 2][0], ps2[k % 2][1],
                                     xbr[k % 2], t1s[k % 2], t2s[k % 2], s_ap, s_buf)
                            P.dma([(dsl, s_ap)], reads=[s_buf], owner=s_buf, kind="s")
            P.barrier()
            if stop_after == "A":
                break

            PB.reset(); PF.reset()
            wstage = [(PF.take(2048), Buf("wstg%d" % i)) for i in range(1)]
            wqb = PB.take(3 * 800).rearrange("p (a n) -> p a n", n=800); wqb_buf = Buf("wqb")
            zt_ap = PB.take(512); zt_buf = Buf("zt")
            P.op("pool", [I("memset", wqb[:, :, 768:800], 0.0), I("memset", zt_ap, 0.0)], writes=[wqb_buf, zt_buf])
            wkvb = PB.take(2 * 1024).rearrange("p (a n) -> p a n", n=1024); wkvb_buf = Buf("wkvb")
            wload(w_qb[l].rearrange("(kc p) e -> p kc e", p=128), wqb[:, :, 0:768], wqb_buf, wstage)
            wload(w_kvb[l].rearrange("(kc p) e -> p kc e", p=128), wkvb, wkvb_buf, wstage)
            wkvb_v = wkvb.rearrange("p a (h two d) -> p a h two d", two=2, d=64)
            cq = [(PF.take(5 * 512).rearrange("p (a n) -> p a n", n=512), Buf("cq%d" % i)) for i in range(1)]
            krs = [(PF.take(512), Buf("krs%d" % i)) for i in range(1)]
            sq = (PF.take(512), Buf("sq"))
            rstd = [(PF.take(512), Buf("rstd%d" % i)) for i in range(2)]
            cn = (PB.take(5 * 512).rearrange("p (a n) -> p a n", n=512), Buf("cn"))
            hlb = (PB.take(1024).rearrange("p (a n) -> p a n", n=512), Buf("hlb"))
            hlb2 = [Buf("hlb2_%d" % i) for i in range(2)]
            css = [(PF.take(1024), Buf("cssm%d" % i)) for i in range(1)]
            csk = [(PF.take(1024), Buf("csk%d" % i)) for i in range(1)]
            xbr = [(PB.take(512), Buf("xbr%d" % i)) for i in range(2)]
            t1s = [(PF.take(512), Buf("t1s%d" % i)) for i in range(1)]
            t2s = [(PF.take(512), Buf("t2s%d" % i)) for i in range(1)]
            stg = [(PB.take(512), Buf("stg%d" % i)) for i in range(4)]
            psA = [(bank(i), Buf("psA%d" % i)) for i in range(4)]
            ps2 = [(bank(4 + i), Buf("ps2%d" % i)) for i in range(2)]
            psS = [(bank(6 + i), Buf("psS%d" % i)) for i in range(2)]
            cnt = dict(item=0, stg=0, rope=0)
            for tb in range(int(os.environ.get('DBG_BTB', '8'))):
                tsl = slice(tb * 512, (tb + 1) * 512)
                cq_ap, cq_buf = cq[0]
                kr_ap, kr_buf = krs[0]
                P.dma([(cq_ap, MC[0:5, :, tsl].rearrange("c p t -> p c t"))], writes=[cq_buf], owner=cq_buf)
                P.dma([(kr_ap[0:32, :], MC[5, 0:32, tsl])], writes=[kr_buf], owner=kr_buf)
                cs_ap, cs_buf = css[0]
                P.dma([(cs_ap[:, 0:512], c_cosm[:, tsl]), (cs_ap[:, 512:1024], c_sinm[:, tsl])], writes=[cs_buf], owner=cs_buf)
                ck_ap, ck_buf = csk[0]
                P.dma([(ck_ap[0:32, 0:512], c_cosk[:, tsl]), (ck_ap[0:32, 512:1024], c_sink[:, tsl])], writes=[ck_buf], owner=ck_buf)
                cn_ap, cn_buf = cn
                hl_ap, hl_buf = hlb
                hlbufs = hlb2
                for (c0, ncn, gt, goff, nfeat) in ((0, 3, gq_t, l * 3, 384.0), (3, 2, gkv_t, l * 2, 256.0)):
                    pS_ap, pS_buf = psS[0 if c0 == 0 else 1]
                    sq_ap, sq_buf = sq
                    for ci in range(ncn):
                        sb_ap = hl_ap[:, ci % 2, :]; sb_buf = hlbufs[ci % 2]
                        P.op("act", I("activation", sb_ap, cq_ap[:, c0 + ci, :], AF.Square), reads=[cq_buf], writes=[sb_buf])
                        P.op("pe", I("matmul", pS_ap, lhsT=onesb[:, :], rhs=sb_ap, start=(ci == 0), stop=(ci == ncn - 1)),
                             reads=[sb_buf], writes=[pS_buf])
                    r_ap, r_buf = rstd[0 if c0 == 0 else 1]
                    P.op("act", I("activation", r_ap, pS_ap, AF.Ln, bias=epsrms[:, 0:1], scale=1.0 / nfeat), reads=[pS_buf], writes=[r_buf])
                    P.op("act", I("activation", r_ap, r_ap, AF.Exp, scale=-0.5), reads=[r_buf], writes=[r_buf])
                    for ci in range(ncn):
                        P.op("dve", I("scalar_tensor_tensor",
                            cn_ap[:, c0 + ci, :], cq_ap[:, c0 + ci, :], gt[:, goff + ci:goff + ci + 1], r_ap, op0=ALU.mult, op1=ALU.mult),
                            reads=[cq_buf, r_buf], writes=[cn_buf])
                for h in range(int(os.environ.get('DBG_QH', '8')) if 'q' in DBGB else 0):
                    ps_ap, ps_buf = psA[cnt["item"] % 4]; cnt["item"] += 1
                    P.op("pe", [I("matmul", ps_ap, lhsT=wqb[:, ci, h * 96:h * 96 + 128], rhs=cn_ap[:, ci, :],
                                                                        start=(ci == 0), stop=(ci == 2)) for ci in range(3)],
                         reads=[cn_buf, wqb_buf], writes=[ps_buf])
                    k = cnt["rope"]; cnt["rope"] += 1
                    s_ap, s_buf = stg[cnt["stg"] % 4]; cnt["stg"] += 1
                    if os.environ.get('DBG_NOROPE'):
                        P.op("act", I("copy", s_ap, ps_ap), reads=[ps_buf], writes=[s_buf])
                    else:
                        rope_epi(ps_ap, ps_buf, int(os.environ.get('DBG_ROWS', '128')), cs_ap[:, 0:512], cs_ap[:, 512:1024], cs_buf, (rd if os.environ.get('DBG_RD') else rm), ps2[k % 2][0], ps2[k % 2][1],
                                 xbr[k % 2], t1s[0], t2s[0], s_ap, s_buf)
                    P.dma([(MQ[h, :, tsl], s_ap)], reads=[s_buf], owner=s_buf, kind="s")
                for h in range(8 if 'k' in DBGB else 0):
                    ps_ap, ps_buf = psA[cnt["item"] % 4]; cnt["item"] += 1
                    P.op("pe", [I("matmul", ps_ap[0:64, :], lhsT=wkvb[:, ci, h * 128:h * 128 + 64], rhs=cn_ap[:, 3 + ci, :],
                                                                        start=(ci == 0), stop=(ci == 1)) for ci in range(2)],
                         reads=[cn_buf, wkvb_buf], writes=[ps_buf])
                    s_ap, s_buf = stg[cnt["stg"] % 4]; cnt["stg"] += 1
                    P.op("act", I("copy", s_ap[0:64, :], ps_ap[0:64, :]), reads=[ps_buf], writes=[s_buf])
                    P.dma([(MK[h, 0:64, tsl], s_ap[0:64, :])], reads=[s_buf], owner=s_buf, kind="s")
                for s in range(4 if 'v' in DBGB else 0):
                    ps_ap, ps_buf = psA[cnt["item"] % 4]; cnt["item"] += 1
                    P.op("pe", [I("matmul", ps_ap.rearrange("p (h d) -> p h d", d=64), lhsT=cn_ap[:, 3 + ci, s * 128:(s + 1) * 128],
                                                                        rhs=wkvb_v[:, ci, :, 1, :], start=(ci == 0), stop=(ci == 1)) for ci in range(2)],
                         reads=[cn_buf, wkvb_buf], writes=[ps_buf])
                    s_ap, s_buf = stg[cnt["stg"] % 4]; cnt["stg"] += 1
                    P.op("act", I("copy", s_ap, ps_ap), reads=[ps_buf], writes=[s_buf])
                    tt = tb * 4 + s
                    P.dma([(MV[tt * 128:(tt + 1) * 128, :], s_ap)], reads=[s_buf], owner=s_buf, kind="s")
                if 'r' not in DBGB:
                    continue
                k = cnt["rope"]; cnt["rope"] += 1
                s_ap, s_buf = stg[cnt["stg"] % 4]; cnt["stg"] += 1
                rope_epi(kr_ap, kr_buf, 32, ck_ap[:, 0:512], ck_ap[:, 512:1024], ck_buf, rk, ps2[k % 2][0], ps2[k % 2][1],
                         xbr[k % 2], t1s[0], t2s[0], s_ap, s_buf)
                P.dma([(MK[h, 64:96, tsl], s_ap[0:32, :]) for h in range(8)], reads=[s_buf], owner=s_buf, kind="s")
                P.dma([(MK[h, 96:128, tsl], zt_ap[0:32, :]) for h in range(8)], reads=[zt_buf], owner=zt_buf, kind="s")
            P.barrier()
            if stop_after == "B":
                break

            PB.reset(); PF.reset()
            hq = [(PB.take(T), Buf("hq%d" % i)) for i in range(2)]
            hk = [(PB.take(T), Buf("hk%d" % i)) for i in range(2)]
            hv = [(PB.take(32 * 128).rearrange("p (a n) -> p a n", n=128), Buf("hv%d" % i)) for i in range(2)]
            pts = [(PB.take(1024).rearrange("p (a n) -> p a n", n=512), Buf("pt%d" % i)) for i in range(4)]
            stg = [(PB.take(512), Buf("stg%d" % i)) for i in range(2)]
            pss = [(PSt[i][:, :].rearrange("p (a n) -> p a n", n=512), Buf("pss%d" % i)) for i in range(3)]
            bk6 = Buf("bk6"); bk7 = Buf("bk7")
            accO = [(bank(6), bk6)]
            accS = [(bank(7), bk7)]
            pend = []
            rr = [(PF.take(512), Buf("rr%d" % i)) for i in range(2)]
            a1 = (PF.take(512), Buf("a1"))
            a2 = (PF.take(512), Buf("a2"))
            oo = (PF.take(512), Buf("oo"))
            sqq = (PF.take(512), Buf("sqq"))
            hld = (PB.take(1024).rearrange("p (a n) -> p a n", n=512), Buf("hld"))
            cnt = dict(g=0, acc=0, pt=0, stg=0, rr=0)

            padd = [(PB.take(512), Buf("padd%d" % i)) for i in range(3)]
            accM = [(bank(6), bk6), (bank(7), bk7)]
            rrow = [(PF.take(512), Buf("rrow%d" % i)) for i in range(2)]
            rsb = [Buf("rsb%d" % i) for i in range(2)]

            def attn_pass(q_ap, k_ap, q_buf, k_buf, v_ap, v_buf, kd, dv, scale, qb, mla=False):
                if mla:
                    ai = cnt["acc"] % 2; cnt["acc"] += 1
                    O_ap, O_buf = accM[ai]
                    S_ap, S_buf = None, None
                else:
                    ai = 0
                    O_ap, O_buf = accO[ai]
                    S_ap, S_buf = accS[ai]
                NG = 16
                gl = []

                def emit_S(g):
                    ps_ap, ps_buf = pss[(cnt["g"] + g) % 3]
                    P.op("pe", [I("matmul", ps_ap[:, j, :], lhsT=k_ap[0:kd, (2 * g + j) * 128:(2 * g + j + 1) * 128],
                                                                       rhs=q_ap[0:kd, qb * 512:(qb + 1) * 512], start=True, stop=True) for j in range(2)],
                         reads=[q_buf, k_buf], writes=[ps_buf])

                emit_S(0)
                emit_S(1)
                emit_S(2)
                for f in pend:
                    f()
                del pend[:]
                for g in range(NG):
                    ps_ap, ps_buf = pss[(cnt["g"] + g) % 3]
                    pt_ap, pt_buf = pts[cnt["pt"] % 4]; cnt["pt"] += 1
                    P.op("act", I("activation", pt_ap, ps_ap, AF.Exp, scale=scale), reads=[ps_buf], writes=[pt_buf])
                    fns = []
                    for j in range(2):
                        first = (g == 0 and j == 0)
                        lastf = (g == NG - 1 and j == 1)
                        fns.append(I("matmul", O_ap[0:dv, :], lhsT=v_ap[:, 2 * g + j, 0:dv], rhs=pt_ap[:, j, :], start=first, stop=lastf))
                    if mla:
                        P.op("pe", fns, reads=[pt_buf, v_buf], writes=[O_buf])
                        if g + 3 < NG:
                            emit_S(g + 3)
                    else:
                        pa_ap, pa_buf = padd[cnt["pt"] % 3]
                        P.op("dve", I("tensor_tensor", pa_ap, pt_ap[:, 0, :], pt_ap[:, 1, :], ALU.add), reads=[pt_buf], writes=[pa_buf])
                        P.op("pe", fns, reads=[pt_buf, v_buf], writes=[O_buf])
                        if g + 3 < NG:
                            emit_S(g + 3)
                        P.op("pe", I("matmul", S_ap, lhsT=onesb[:, :], rhs=pa_ap, start=(g == 0), stop=(g == NG - 1)), reads=[pa_buf], writes=[S_buf])
                cnt["g"] += NG
                return ai

            heads = [("d", h) for h in range(int(os.environ.get('DBG_ND', '4')))] + [("m", h) for h in range(int(os.environ.get('DBG_NM', '8')))]

            def load_head(i):
                kind, h = heads[i]
                q_ap, q_buf = hq[i % 2]; k_ap, k_buf = hk[i % 2]; v_ap, v_buf = hv[i % 2]
                if kind == "d":
                    P.dma([(q_ap, DQ[h])], writes=[q_buf], owner=q_buf)
                    P.dma([(k_ap, DK[h])], writes=[k_buf], owner=k_buf)
                    P.dma([(v_ap, DV[:, h * 128:(h + 1) * 128].rearrange("(t p) d -> p t d", p=128))], writes=[v_buf], owner=v_buf)
                else:
                    P.dma([(q_ap, MQ[h])], writes=[q_buf], owner=q_buf)
                    P.dma([(k_ap, MK[h])], writes=[k_buf], owner=k_buf)
                    P.dma([(v_ap[:, :, 0:64], MV[:, h * 64:(h + 1) * 64].rearrange("(t p) d -> p t d", p=128))], writes=[v_buf], owner=v_buf)
                    P.op("pool", I("memset", v_ap[:, :, 64:65], 1.0), writes=[v_buf])

            load_head(0)
            for i, (kind, h) in enumerate(heads):
                if i + 1 < len(heads):
                    load_head(i + 1)
                q_ap, q_buf = hq[i % 2]; k_ap, k_buf = hk[i % 2]; v_ap, v_buf = hv[i % 2]
                for qb in range(int(os.environ.get('DBG_NQB', '8'))):
                    tsl = slice(qb * 512, (qb + 1) * 512)
                    if kind == "d":
                        res = []
                        for comp in range(2):
                            ai = attn_pass(q_ap[comp * 64:(comp + 1) * 64, :], k_ap[comp * 64:(comp + 1) * 64, :], q_buf, k_buf,
                                           v_ap, v_buf, 64, 128, 0.125, qb)
                            O_ap, O_buf = accO[ai]; S_ap, S_buf = accS[ai]
                            r_ap, r_buf = rr[cnt["rr"] % 2]; cnt["rr"] += 1
                            a_ap, a_buf = (a1, a2)[comp]
                            P.op("act", I("activation", r_ap, S_ap, AF.Ln), reads=[S_buf], writes=[r_buf])
                            P.op("act", I("activation", r_ap, r_ap, AF.Exp, scale=-1.0), reads=[r_buf], writes=[r_buf])
                            P.op("dve", I("tensor_tensor", a_ap, O_ap, r_ap, ALU.mult),
                                 reads=[O_buf, r_buf], writes=[a_buf])
                            res.append(ai)
                        ai = res[1]
                        S_ap, S_buf = accS[ai]
                        o_ap, o_buf = oo
                        s_ap2, s_buf2 = sqq
                        P.op("dve", I("scalar_tensor_tensor", o_ap, a2[0], nlam_t[:, l:l + 1], a1[0], op0=ALU.mult, op1=ALU.add),
                             reads=[a1[1], a2[1]], writes=[o_buf])
                        P.op("dve", I("tensor_tensor", hld[0][:, 0, :], o_ap, o_ap, ALU.mult), reads=[o_buf], writes=[hld[1]])

                        def tail(S_ap=S_ap, S_buf=S_buf, o_ap=o_ap, o_buf=o_buf, h=h, tsl=tsl, l=l):
                            P.op("pe", I("matmul", S_ap, lhsT=onesb[:, :], rhs=hld[0][:, 0, :], start=True, stop=True), reads=[hld[1]], writes=[S_buf])
                            r_ap, r_buf = rr[cnt["rr"] % 2]; cnt["rr"] += 1
                            P.op("act", I("activation", r_ap, S_ap, AF.Ln, bias=epsrms[:, 0:1], scale=1.0 / 128.0), reads=[S_buf], writes=[r_buf])
                            P.op("act", I("activation", r_ap, r_ap, AF.Exp, scale=-0.5), reads=[r_buf], writes=[r_buf])
                            st_ap, st_buf = stg[cnt["stg"] % 2]; cnt["stg"] += 1
                            P.op("dve", I("scalar_tensor_tensor", st_ap, o_ap, gsc_t[:, l:l + 1], r_ap, op0=ALU.mult, op1=ALU.mult),
                                 reads=[o_buf, r_buf], writes=[st_buf])
                            P.dma([(AO[1, h, :, tsl], st_ap)], reads=[st_buf], owner=st_buf, kind="s")
                        pend.append(tail)
                    else:
                        ai = attn_pass(q_ap, k_ap, q_buf, k_buf, v_ap, v_buf, 128, 65, 96.0 ** -0.5, qb, mla=True)
                        O_ap, O_buf = accM[ai]
                        r_ap, r_buf = rr[cnt["rr"] % 2]; cnt["rr"] += 1
                        rw_ap, rw_buf = rrow[ai]
                        rs_buf = rsb[ai]
                        P.op("act", I("activation", rw_ap[64:65, :], O_ap[64:65, :], AF.Ln), reads=[O_buf], writes=[rw_buf])
                        P.op("act", I("activation", rw_ap[64:65, :], rw_ap[64:65, :], AF.Exp, scale=-1.0), reads=[rw_buf], writes=[rw_buf])
                        P.dma([(RS[ai:ai + 1, :], rw_ap[64:65, :])], reads=[rw_buf], writes=[rs_buf], owner=rw_buf, kind="s")
                        P.dma([(r_ap[0:64, :], RS[ai:ai + 1, :].partition_broadcast(64)[:, 0, :])], reads=[rs_buf], writes=[r_buf], owner=r_buf)
                        st_ap, st_buf = stg[cnt["stg"] % 2]; cnt["stg"] += 1
                        P.op("dve", I("tensor_tensor", st_ap[0:64, :], O_ap[0:64, :], r_ap[0:64, :], ALU.mult),
                             reads=[O_buf, r_buf], writes=[st_buf])
                        P.dma([(AO[2, h // 2, (h % 2) * 64:(h % 2) * 64 + 64, tsl], st_ap[0:64, :])], reads=[st_buf], owner=st_buf, kind="s")
            for f in pend:
                f()
            del pend[:]
            P.barrier()
            if stop_after == "CD":
                break

            PB.reset(); PF.reset()
            nq = [(PB.take(T), Buf("nq%d" % i)) for i in range(2)]
            nk = [(PB.take(T), Buf("nk%d" % i)) for i in range(2)]
            nve = [(PB.take(32 * 64).rearrange("p (a n) -> p a n", n=64), Buf("nve%d" % i)) for i in range(2)]
            nvo = [(PB.take(32 * 64).rearrange("p (a n) -> p a n", n=64), Buf("nvo%d" % i)) for i in range(2)]
            npt = [(PB.take(1024).rearrange("p (r j n) -> p r j n", r=4, j=4), Buf("npt%d" % i)) for i in range(3)]
            nout = [(PB.take(T), Buf("nout%d" % i)) for i in range(2)]
            tab = [(PF.take(14 * 64).rearrange("p (s n) -> p s n", n=64), Buf("tab%d" % i)) for i in range(2)]
            msk = (PF.take(14 * 64).rearrange("p (s n) -> p s n", n=64), Buf("msk"))
            tmp = [(PF.take(1024).rearrange("p (r j n) -> p r j n", r=4, j=4), Buf("tmp%d" % i)) for i in range(3)]
            nrr = [(PF.take(256), Buf("nrr%d" % i)) for i in range(2)]
            tabi = [PF.take(256) for i in range(2)]
            pssn = [(PSt[i][:, :].rearrange("p (r j n) -> p r j n", r=4, j=4), Buf("pssn%d" % i)) for i in range(3)]
            naOS = [(bank(6 + i), Buf("naOS%d" % i)) for i in range(2)]
            P.dma([(msk[0].rearrange("p s n -> p (s n)"), namask)], writes=[msk[1]], owner=msk[1])
            it = 0
            for cch in range(4):
                q_ap, q_buf = nq[cch % 2]; k_ap, k_buf = nk[cch % 2]
                P.dma([(q_ap, NQ[cch])], writes=[q_buf], owner=q_buf)
                P.dma([(k_ap, NK[cch])], writes=[k_buf], owner=k_buf)
                for hh in range(2):
                    h = cch * 2 + hh
                    hi = h % 2
                    ve_ap, ve_buf = nve[hi]; vo_ap, vo_buf = nvo[hi]
                    P.dma([(ve_ap, NV[:, h * 64:(h + 1) * 64].rearrange("(t p) d -> p t d", p=128))], writes=[ve_buf], owner=ve_buf)
                    P.dma([(vo_ap[:, 0:31, :], NV[64:T - 64, h * 64:(h + 1) * 64].rearrange("(t p) d -> p t d", p=128))], writes=[vo_buf], owner=vo_buf)
                    tb_ap, tb_buf = tab[hi]
                    src = bass.AP(tensor=nabias.tensor, offset=nabias[l, h].offset, ap=[[64, 128], [64 * 64, 14], [1, 64]])
                    P.dma([(tb_ap, src)], writes=[tb_buf], owner=tb_buf)
                    P.op("pool", I("tensor_tensor", tb_ap, tb_ap, msk[0], ALU.add), reads=[tb_buf, msk[1]], writes=[tb_buf])
                    ti_ap = tabi[hi]
                    P.op("pool", I("tensor_copy", ti_ap.rearrange("p (j n) -> p j n", n=64), tb_ap[:, 3:10:2, :]), reads=[tb_buf], writes=[tb_buf])
                    o_ap, o_buf = nout[hi]
                    qh = q_ap[hh * 64:(hh + 1) * 64, :]
                    kh = k_ap[hh * 64:(hh + 1) * 64, :]
                    def rows_of(rg):
                        return [(rg * 4 + ri, min(max(rg * 4 + ri - 4, 0), 56)) for ri in range(4)]

                    def na_S(rg, k):
                        ps_ap, ps_buf = pssn[k % 3]
                        fns = []
                        for ri, (r, rs) in enumerate(rows_of(rg)):
                            for j in range(4):
                                k0 = (rs + 2 * j) * 64
                                fns.append(I("matmul", ps_ap[:, ri, j, :], lhsT=kh[:, k0:k0 + 128], rhs=qh[:, r * 64:(r + 1) * 64], start=True, stop=True))
                        P.op("pe", fns, reads=[q_buf, k_buf], writes=[ps_buf])

                    def na_bias(rg, k):
                        ps_ap, ps_buf = pssn[k % 3]
                        tm_ap, tm_buf = tmp[k % 3]
                        offs = [rs - r + 7 for (r, rs) in rows_of(rg)]
                        if all(o == 3 for o in offs):
                            bc = bass.AP(tensor=ti_ap.tensor, offset=ti_ap.offset, ap=[list(ti_ap.ap[0]), [0, 4], [1, 256]])
                            P.op("dve", I("scalar_tensor_tensor", tm_ap.rearrange("p r j n -> p r (j n)"), ps_ap.rearrange("p r j n -> p r (j n)"), 0.125, bc,
                                          op0=ALU.mult, op1=ALU.add), reads=[ps_buf, tb_buf], writes=[tm_buf])
                        else:
                            for ri, off in enumerate(offs):
                                P.op("dve", I("scalar_tensor_tensor", tm_ap[:, ri, :, :], ps_ap[:, ri, :, :], 0.125, tb_ap[:, off:off + 7:2, :],
                                              op0=ALU.mult, op1=ALU.add), reads=[ps_buf, tb_buf], writes=[tm_buf])

                    def na_fin(rg, k):
                        OS_ap, OS_buf = naOS[k % 2]
                        O_ap = OS_ap[:, 0:256]; S_ap = OS_ap[:, 256:512]
                        r_ap, r_buf = nrr[k % 2]
                        P.op("act", I("activation", r_ap[0:64, :], S_ap[0:64, :], AF.Ln), reads=[OS_buf], writes=[r_buf])
                        P.op("act", I("activation", r_ap[0:64, :], r_ap[0:64, :], AF.Exp, scale=-1.0), reads=[r_buf], writes=[r_buf])
                        P.op("dve", I("tensor_tensor", o_ap[0:64, rg * 256:(rg + 1) * 256], O_ap[0:64, :], r_ap[0:64, :], ALU.mult),
                             reads=[OS_buf, r_buf], writes=[o_buf])

                    for g0 in range(3):
                        na_S(g0, it + g0)
                    na_bias(0, it)
                    for rg in range(16):
                        k = it + rg
                        tm_ap, tm_buf = tmp[k % 3]
                        pt_ap, pt_buf = npt[k % 3]
                        OS_ap, OS_buf = naOS[k % 2]
                        O_ap = OS_ap[:, 0:256]; S_ap = OS_ap[:, 256:512]
                        rows = rows_of(rg)
                        P.op("act", I("activation", pt_ap, tm_ap, AF.Exp), reads=[tm_buf], writes=[pt_buf])
                        if rg >= 1:
                            na_fin(rg - 1, k - 1)
                        if rg + 1 < 16:
                            na_bias(rg + 1, k + 1)
                        fns = []
                        for ri, (r, rs) in enumerate(rows):
                            for j in range(4):
                                a = rs + 2 * j
                                if a % 2 == 0:
                                    vt = ve_ap[:, a // 2, :]
                                else:
                                    vt = vo_ap[:, (a - 1) // 2, :]
                                fns.append(I("matmul", O_ap[0:64, ri * 64:(ri + 1) * 64], lhsT=vt, rhs=pt_ap[:, ri, j, :], start=(j == 0), stop=(j == 3)))
                            for j in range(4):
                                fns.append(I("matmul", S_ap[0:64, ri * 64:(ri + 1) * 64], lhsT=onesb[:, 0:64], rhs=pt_ap[:, ri, j, :], start=(j == 0), stop=(j == 3)))
                        P.op("pe", fns, reads=[pt_buf, ve_buf, vo_buf], writes=[OS_buf])
                        if rg + 3 < 16:
                            na_S(rg + 3, it + rg + 3)
                    na_fin(15, it + 15)
                    it += 16
                    P.dma([(AO[0, cch, hh * 64:(hh + 1) * 64, :], o_ap[0:64, :])], reads=[o_buf], owner=o_buf, kind="s")
            P.barrier()
            if stop_after == "E":
                break

            PB.reset(); PF.reset()
            wstage = [(PF.take(2048), Buf("wstg%d" % i)) for i in range(1)]
            wbr = PB.take(12 * 1024).rearrange("p (a n) -> p a n", n=1024); wbr_buf = Buf("wbr")
            wo = PB.take(8 * 1024).rearrange("p (a n) -> p a n", n=1024); wo_buf = Buf("wo")
            for i in range(3):
                wload(w_br[l, i].rearrange("(c p) d -> p c d", p=128), wbr[:, 4 * i:4 * i + 4, :], wbr_buf, wstage)
            wload(w_out[l].rearrange("(c p) d -> p c d", p=128), wo, wo_buf, wstage)
            c = ln_setup(ln1_g[l:l + 1, :], ln1_b[l:l + 1, :], need_y=False)
            aob = (PB.take(12 * 512).rearrange("p (a n) -> p a n", n=512), Buf("aob"))
            gts = [(PB.take(3 * 512).rearrange("p (a n) -> p a n", n=512), Buf("gts%d" % i)) for i in range(2)]
            mg = (PB.take(8 * 512).rearrange("p (a n) -> p a n", n=512), [Buf("mg%d" % i) for i in range(8)])
            mm = [(PF.take(512), Buf("mm%d" % i)) for i in range(3)]
            xold = [(PF.take(1024), Buf("xold%d" % i)) for i in range(2)]
            psb = [(bank(i), Buf("psb%d" % i)) for i in range(3)]
            psyF = [(PSt[2 + i][:, :], Buf("psy%d" % i)) for i in range(2)]
            psT = [(bank(3), Buf("psT"))]
            GTv = GT.rearrange("(i c) p t -> c p i t", i=3)
            cnt = dict(b=0, g=0, x=0)
            for tb in range(8):
                tsl = slice(tb * 512, (tb + 1) * 512)
                ao_ap, ao_buf = aob
                P.dma([(ao_ap[:, 4 * i:4 * i + 4, :], AO[i, :, :, tsl].rearrange("c p t -> p c t")) for i in range(3)], writes=[ao_buf], owner=ao_buf)
                for dm in range(8):
                    g_ap, g_buf = gts[cnt["g"] % 2]; cnt["g"] += 1
                    P.dma([(g_ap, GTv[dm, :, :, tsl])], writes=[g_buf], owner=g_buf)
                    pbs = []
                    for i in range(3):
                        pb_ap, pb_buf = psb[cnt["b"] % 3]; cnt["b"] += 1
                        pbs.append((pb_ap, pb_buf))
                        P.op("pe", [I("matmul", pb_ap, lhsT=wbr[:, 4 * i + cc, dm * 128:(dm + 1) * 128],
                                                                                  rhs=ao_ap[:, 4 * i + cc, :], start=(cc == 0), stop=(cc == 3)) for cc in range(4)],
                             reads=[ao_buf, wbr_buf], writes=[pb_buf])
                    for i in range(3):
                        P.op("dve", I("tensor_tensor", mm[i][0], pbs[i][0], g_ap[:, i, :], ALU.mult),
                             reads=[pbs[i][1], g_buf], writes=[mm[i][1]])
                    P.op("pool", I("tensor_tensor", mm[0][0], mm[0][0], mm[1][0], ALU.add), reads=[mm[0][1], mm[1][1]], writes=[mm[0][1]])
                    P.op("pool", I("tensor_tensor", mg[0][:, dm, :], mm[0][0], mm[2][0], ALU.add), reads=[mm[0][1], mm[2][1]], writes=[mg[1][dm]])
                def f_mm(s_, kx):
                    t = tb * 4 + s_
                    xo_ap, xo_buf = xold[kx % 2]
                    py_ap, py_buf = psyF[kx % 2]
                    P.dma([(xo_ap, XTOK[t * 128:(t + 1) * 128, :])], writes=[xo_buf], owner=xo_buf)
                    P.op("pe", [I("matmul", py_ap[:, half * 512:(half + 1) * 512], lhsT=mg[0][:, dm, s_ * 128:(s_ + 1) * 128],
                                  rhs=wo[:, dm, half * 512:(half + 1) * 512], start=(dm == 0), stop=(dm == 7))
                                for half in range(2) for dm in range(8)], reads=mg[1] + [wo_buf], writes=[py_buf])

                f_mm(0, cnt["x"])
                for s_ in range(4):
                    kx = cnt["x"]; cnt["x"] += 1
                    if s_ + 1 < 4:
                        f_mm(s_ + 1, kx + 1)
                    t = tb * 4 + s_
                    xo_ap, xo_buf = xold[kx % 2]
                    py_ap, py_buf = psyF[kx % 2]
                    P.op("dve", I("scalar_tensor_tensor", xo_ap, xo_ap, ALPHA, py_ap, op0=ALU.mult, op1=ALU.add),
                         reads=[xo_buf, py_buf], writes=[xo_buf])
                    ln_tile(c, xo_ap, xo_buf, t, psT[0][0], psT[0][1])
            P.barrier()
            if stop_after == "F":
                break

            PB.reset(); PF.reset()
            wstage = [(PF.take(2048), Buf("wstg%d" % i)) for i in range(2)]
            wg = [(PB.take(1024).rearrange("p (a n) -> p a n", n=128), Buf("wg%d" % i)) for i in range(3)]
            wu = [(PB.take(1024).rearrange("p (a n) -> p a n", n=128), Buf("wu%d" % i)) for i in range(3)]
            stg = [(PB.take(512), Buf("stg%d" % i)) for i in range(3)]
            sg = [(PF.take(512), Buf("sg%d" % i)) for i in range(2)]
            psG = [(bank(2 * i), Buf("psG%d" % i)) for i in range(4)]
            psU = [(bank(2 * i + 1), Buf("psU%d" % i)) for i in range(4)]

            def load_f(fc):
                wload(w_fi[l][:, fc * 128:(fc + 1) * 128].rearrange("(kc p) e -> p kc e", p=128), wg[fc % 3][0], wg[fc % 3][1], wstage)
                wload(w_fi[l][:, DFF + fc * 128:DFF + (fc + 1) * 128].rearrange("(kc p) e -> p kc e", p=128), wu[fc % 3][0], wu[fc % 3][1], wstage)

            load_f(0); load_f(1)
            it = 0
            for fc in range(FC):
                if fc + 2 < FC:
                    load_f(fc + 2)
                wg_ap, wg_buf = wg[fc % 3]; wu_ap, wu_buf = wu[fc % 3]
                for tb in range(8):
                    pg_ap, pg_buf = psG[it % 4]; pu_ap, pu_buf = psU[it % 4]
                    P.op("pe", [I("matmul", pg_ap, lhsT=wg_ap[:, kc, :], rhs=xT[:, kc, tb * 512:(tb + 1) * 512],
                                                                                    start=(kc == 0), stop=(kc == KC - 1)) for kc in range(KC)],
                         reads=[wg_buf], writes=[pg_buf])
                    P.op("pe", [I("matmul", pu_ap, lhsT=wu_ap[:, kc, :], rhs=xT[:, kc, tb * 512:(tb + 1) * 512],
                                                                                    start=(kc == 0), stop=(kc == KC - 1)) for kc in range(KC)],
                         reads=[wu_buf], writes=[pu_buf])
                    sg_ap, sg_buf = sg[it % 2]
                    st_ap, st_buf = stg[it % 3]
                    it += 1
                    P.op("act", I("activation", sg_ap, pg_ap, AF.Silu), reads=[pg_buf], writes=[sg_buf])
                    P.op("dve", I("tensor_tensor", st_ap, pu_ap, sg_ap, ALU.mult),
                         reads=[pu_buf, sg_buf], writes=[st_buf])
                    P.dma([(HT[fc, :, tb * 512:(tb + 1) * 512], st_ap)], reads=[st_buf], owner=st_buf, kind="s")
            P.barrier()
            if stop_after == "G":
                break

            PB.reset(); PF.reset()
            wstage = [(PF.take(2048), Buf("wstg%d" % i)) for i in range(1)]
            wfo = PB.take(FC * 1024).rearrange("p (a n) -> p a n", n=1024); wfo_buf = Buf("wfo")
            wload(w_fo[l].rearrange("(c p) d -> p c d", p=128), wfo, wfo_buf, wstage)
            c = ln_setup(ln2_g[l:l + 1, :], ln2_b[l:l + 1, :], need_y=False)
            hb = [(PB.take(FC * 256).rearrange("p (a n) -> p a n", n=256), Buf("hb%d" % i)) for i in range(2)]
            xold = [(PF.take(1024), Buf("xold%d" % i)) for i in range(3)]
            psy = [(PSt[i][:, :], Buf("psy%d" % i)) for i in range(3)]
            psT = [(bank(6 + i), Buf("psT%d" % i)) for i in range(2)]

            def h_mm(idx):
                hbi, s_ = idx // 2, idx % 2
                h_ap, h_buf = hb[hbi % 2]
                if s_ == 0:
                    P.dma([(h_ap, HT[:, :, hbi * 256:(hbi + 1) * 256].rearrange("c p t -> p c t"))], writes=[h_buf], owner=h_buf)
                xo_ap, xo_buf = xold[idx % 3]
                py_ap, py_buf = psy[idx % 3]
                P.dma([(xo_ap, XTOK[idx * 128:(idx + 1) * 128, :])], writes=[xo_buf], owner=xo_buf)
                P.op("pe", [I("matmul", py_ap[:, half * 512:(half + 1) * 512], lhsT=h_ap[:, fc, s_ * 128:(s_ + 1) * 128],
                              rhs=wfo[:, fc, half * 512:(half + 1) * 512], start=(fc == 0), stop=(fc == FC - 1))
                            for half in range(2) for fc in range(FC)], reads=[h_buf, wfo_buf], writes=[py_buf])

            h_mm(0)
            h_mm(1)
            for idx in range(32):
                if idx + 2 < 32:
                    h_mm(idx + 2)
                xo_ap, xo_buf = xold[idx % 3]
                py_ap, py_buf = psy[idx % 3]
                pT_ap, pT_buf = psT[idx % 2]
                P.op("dve", I("scalar_tensor_tensor", xo_ap, xo_ap, ALPHA, py_ap, op0=ALU.mult, op1=ALU.add),
                     reads=[xo_buf, py_buf], writes=[xo_buf])
                ln_tile(c, xo_ap, xo_buf, idx, pT_ap, pT_buf, final=(l == n_layers - 1))
            P.barrier()

        P.emit(block)
        build.stats = dict(ninst=P.ninst, ecnt=dict(P.ecnt), maxsem=max(P.semval.values()))
    return nc


def _consts():
    bf = ml_dtypes.bfloat16
    cst = {}
    cst["c_ident"] = np.eye(128, dtype=np.float32).astype(bf)
    cst["c_onesb"] = np.ones((128, 128), dtype=np.float32).astype(bf)
    cst["c_onesf"] = np.ones((128, 128), dtype=np.float32)
    pos = np.arange(T, dtype=np.float32)

    def tables(half):
        inv = np.exp(-math.log(ROPE_THETA) * np.arange(half, dtype=np.float32) / half).astype(np.float32)
        ang = (pos[None, :] * inv[:, None]).astype(np.float32)
        return np.cos(ang.astype(np.float64)).astype(np.float32), np.sin(ang.astype(np.float64)).astype(np.float32)

    cos8, sin8 = tables(8)
    cosd = np.ones((128, T), np.float32); sind = np.zeros((128, T), np.float32)
    rdm = np.zeros((128, 128), np.float32)
    for gb in (0, 64):
        for i in range(8):
            cosd[gb + i] = cos8[i]; cosd[gb + 8 + i] = cos8[i]
            sind[gb + i] = -sin8[i]; sind[gb + 8 + i] = sin8[i]
            rdm[gb + 8 + i, gb + i] = 1.0
            rdm[gb + i, gb + 8 + i] = 1.0
    cst["c_cosd"] = cosd; cst["c_sind"] = sind; cst["c_rd"] = rdm.astype(bf)
    cos16, sin16 = tables(16)
    cosm = np.ones((128, T), np.float32); sinm = np.zeros((128, T), np.float32)
    rmm = np.zeros((128, 128), np.float32)
    for i in range(16):
        cosm[64 + i] = cos16[i]; cosm[80 + i] = cos16[i]
        sinm[64 + i] = -sin16[i]; sinm[80 + i] = sin16[i]
        rmm[80 + i, 64 + i] = 1.0
        rmm[64 + i, 80 + i] = 1.0
    cst["c_cosm"] = cosm; cst["c_sinm"] = sinm; cst["c_rm"] = rmm.astype(bf)
    cst["c_cosk"] = np.ascontiguousarray(cosm[64:96]); cst["c_sink"] = np.ascontiguousarray(sinm[64:96])
    cst["c_rk"] = np.ascontiguousarray(rmm[64:96, 64:96]).astype(bf)
    qc = np.arange(64)
    ws = np.clip(qc - 8, 0, 48)
    kc_ = np.arange(64)
    valid = (kc_[:, None] >= ws[None, :]) & (kc_[:, None] < ws[None, :] + 16)
    m = np.where(valid, 0.0, NEG).astype(np.float32)
    m2 = np.concatenate([m, m], axis=0)
    cst["namask"] = np.ascontiguousarray(np.broadcast_to(m2[:, None, :], (128, 14, 64))).reshape(128, 14 * 64)
    return cst


_CACHE = {}


def _prep_shared(inp):
    f = lambda a: np.ascontiguousarray(np.asarray(a, dtype=np.float32))
    sh = {}
    sh["ln_in_g"] = f(inp["ln_in_g"]).reshape(1, D)
    sh["ln_in_b"] = f(inp["ln_in_b"]).reshape(1, D)
    sh["w_in"] = f(inp["w_in"])
    sh["bgate"] = np.ascontiguousarray(f(inp["b_gate"]).reshape(NL, 24, 128).transpose(0, 2, 1))
    rpb = f(inp["na_rpb"])
    kcol = np.arange(64)[:, None]; qcol = np.arange(64)[None, :]
    idx = np.clip(kcol - qcol, -15, 15) + 15
    sh["nabias"] = np.ascontiguousarray(rpb[:, :, :, idx]).reshape(NL, 8, 960, 64)
    sh["dlam"] = f(inp["diff_lambda"]).reshape(NL, 1, 256)
    sh["subg"] = f(inp["diff_subln_g"]).reshape(NL, 128, 1)
    sh["gq"] = np.ascontiguousarray(f(inp["mla_q_norm_g"]).reshape(NL, 3, 128).transpose(0, 2, 1))
    sh["gkv"] = np.ascontiguousarray(f(inp["mla_kv_norm_g"]).reshape(NL, 2, 128).transpose(0, 2, 1))
    sh["w_qb"] = f(inp["w_mla_qb"])
    sh["w_kvb"] = f(inp["w_mla_kvb"])
    sh["w_br"] = f(inp["w_branch"])
    sh["w_out"] = f(inp["w_out"])
    sh["ln1_g"] = f(inp["ln1_g"]); sh["ln1_b"] = f(inp["ln1_b"])
    sh["w_fi"] = f(inp["w_ffn_in"]); sh["w_fo"] = f(inp["w_ffn_out"])
    sh["ln2_g"] = f(inp["ln2_g"]); sh["ln2_b"] = f(inp["ln2_b"])
    sh.update(_consts())
    return sh


def kernel(**inputs):
    x = np.ascontiguousarray(np.asarray(inputs["x"], dtype=np.float32))
    nb = x.shape[0]
    sh = _prep_shared(inputs)
    if "nc" not in _CACHE:
        _CACHE["nc"] = build()
    nc = _CACHE["nc"]
    in_maps = []
    for b in range(nb):
        m = dict(sh)
        m["x"] = x[b]
        in_maps.append(m)
    res = run_bass_kernel_spmd(nc, in_maps, core_ids=list(range(nb)))
    return np.stack([np.asarray(r["out"], dtype=np.float32) for r in res.results], axis=0)
```
